# Optimizing a Trainium2 kernel written in Bass

```python
import math
import jax, jax.numpy as jnp
from jax import lax
import numpy as np

D_MODEL = 1024
BATCH = 32
SEQ = 2048
DEPTH = 2

N_META = 16
Q_BLOCK = 128
GROUP_HEAD_DIM = 64
D_MIX = D_MODEL
ATTN_HEADS = 8
ATTN_HEAD_DIM = 64
D_ATTN = ATTN_HEADS * ATTN_HEAD_DIM
D_CONV = D_MIX // 4
D_SC = D_MIX - D_ATTN - D_CONV
CONV_KERNEL = 31
SC_KERNEL = 3
D_FF = 2816
FFN_KERNEL = 3
RMS_EPS = 1e-6
LN_EPS = 1e-5
IN_SPLITS = (D_ATTN, D_ATTN, D_ATTN, ATTN_HEADS,
             D_CONV, D_CONV,
             D_SC, D_SC, D_SC)
D_IN = sum(IN_SPLITS)
NEG_INF = -1e30

kernel_name = "hymba_style_fox_conformer_shortconv_hybrid"


def rms_norm(x, g):
    xf = x.astype(jnp.float32)
    y = xf * lax.rsqrt(jnp.mean(xf * xf, axis=-1, keepdims=True) + RMS_EPS)
    return (y * g.astype(jnp.float32)).astype(x.dtype)


def head_rms_norm(y, g):
    b, l, d = y.shape
    yf = y.astype(jnp.float32).reshape(b, l, d // GROUP_HEAD_DIM, GROUP_HEAD_DIM)
    yf = yf * lax.rsqrt(jnp.mean(yf * yf, axis=-1, keepdims=True) + RMS_EPS)
    return (yf.reshape(b, l, d) * g.astype(jnp.float32)).astype(y.dtype)


def layer_norm(x, g, b):
    xf = x.astype(jnp.float32)
    mu = jnp.mean(xf, axis=-1, keepdims=True)
    var = jnp.mean(jnp.square(xf - mu), axis=-1, keepdims=True)
    y = (xf - mu) * lax.rsqrt(var + LN_EPS)
    return (y * g.astype(jnp.float32) + b.astype(jnp.float32)).astype(x.dtype)


def causal_depthwise_conv(x, w):
    k, c = w.shape
    return lax.conv_general_dilated(
        x, w[:, None, :].astype(x.dtype), window_strides=(1,),
        padding=[(k - 1, 0)], dimension_numbers=("NWC", "WIO", "NWC"),
        feature_group_count=c)


def forgetting_attention(q, k, v, log_f):
    L = q.shape[1]
    scale = ATTN_HEAD_DIM ** -0.5
    F = jnp.cumsum(log_f, axis=1).transpose(0, 2, 1)

    def attend(start, end):
        qb = q[:, start:end]
        s = jnp.einsum("bqhd,bkhd->bhqk", qb, k[:, :end],
                       preferred_element_type=jnp.float32) * scale
        s = s + F[:, :, start:end, None] - F[:, :, None, :end]
        mask = jnp.arange(start, end)[:, None] >= jnp.arange(end)[None, :]
        s = jnp.where(mask, s, NEG_INF)
        p = jax.nn.softmax(s, axis=-1)
        return jnp.einsum("bhqk,bkhd->bqhd", p.astype(v.dtype), v[:, :end])

    outs = [attend(0, N_META)]
    for i in range((L - N_META) // Q_BLOCK):
        start = N_META + i * Q_BLOCK
        outs.append(attend(start, start + Q_BLOCK))
    return jnp.concatenate(outs, axis=1)


def mixer_sublayer(h, pre_g, post_g, w_in, b_forget, a_dw_w, a_dw_b, a_ln_g,
                   a_ln_b, a_pw_w, sc_conv_w, head_g, w_out):
    bsz, L, _ = h.shape
    u = rms_norm(h, pre_g)
    z = jnp.einsum("bld,de->ble", u, w_in)
    idx = np.cumsum(IN_SPLITS)[:-1].tolist()
    q, k, v, f_logit, a_val, a_gate, sc_b, sc_c, sc_x = jnp.split(z, idx, axis=-1)

    log_f = jax.nn.log_sigmoid((f_logit + b_forget).astype(jnp.float32))
    hs = (bsz, L, ATTN_HEADS, ATTN_HEAD_DIM)
    y_attn = forgetting_attention(q.reshape(hs), k.reshape(hs), v.reshape(hs), log_f)
    y_attn = y_attn.reshape(bsz, L, D_ATTN)

    a = a_val * jax.nn.sigmoid(a_gate)
    a = causal_depthwise_conv(a, a_dw_w) + a_dw_b
    a = jax.nn.silu(layer_norm(a, a_ln_g, a_ln_b))
    y_conv = jnp.einsum("blc,ce->ble", a, a_pw_w)

    y_sc = sc_b * causal_depthwise_conv(sc_c * sc_x, sc_conv_w)

    y = jnp.concatenate([y_attn, y_conv, y_sc], axis=-1)
    y = jnp.einsum("ble,ed->bld", head_rms_norm(y, head_g), w_out)
    return h + rms_norm(y, post_g)


def ffn_sublayer(h, pre_g, post_g, w_up, conv_w, w_down):
    u = rms_norm(h, pre_g)
    z = jnp.einsum("bld,df->blf", u, w_up)
    z = causal_depthwise_conv(z, conv_w)
    g, up = jnp.split(z, 2, axis=-1)
    y = jnp.einsum("blf,fd->bld", jax.nn.silu(g) * up, w_down)
    return h + rms_norm(y, post_g)


def setup_inputs(seed: int = 0) -> dict:
    key = jax.random.key(seed)
    ks = jax.random.split(key, 20)
    f32 = jnp.float32

    def nrm(k, shape, scale):
        return jax.random.normal(k, shape, f32) * scale

    def gain(k, shape):
        return 1.0 + 0.05 * jax.random.normal(k, shape, f32)

    x = jax.random.normal(ks[0], (BATCH, SEQ, D_MODEL), f32)
    meta_tokens = nrm(ks[1], (N_META, D_MODEL), 1.0)
    col_scale = np.ones((D_IN,), np.float32)
    f0 = 3 * D_ATTN
    col_scale[f0:f0 + ATTN_HEADS] = 0.1
    w_in = nrm(ks[2], (DEPTH, D_MODEL, D_IN), D_MODEL ** -0.5) * jnp.asarray(col_scale)
    b_forget = jax.random.uniform(ks[3], (DEPTH, ATTN_HEADS), f32, 3.0, 6.0)
    return {
        "x": x,
        "meta_tokens": meta_tokens,
        "mix_pre_g": gain(ks[4], (DEPTH, D_MODEL)),
        "mix_post_g": gain(ks[5], (DEPTH, D_MODEL)),
        "w_in": w_in,
        "b_forget": b_forget,
        "a_dw_w": nrm(ks[6], (DEPTH, CONV_KERNEL, D_CONV), CONV_KERNEL ** -0.5),
        "a_dw_b": nrm(ks[7], (DEPTH, D_CONV), 0.02),
        "a_ln_g": gain(ks[8], (DEPTH, D_CONV)),
        "a_ln_b": nrm(ks[9], (DEPTH, D_CONV), 0.02),
        "a_pw_w": nrm(ks[10], (DEPTH, D_CONV, D_CONV), D_CONV ** -0.5),
        "sc_conv_w": nrm(ks[11], (DEPTH, SC_KERNEL, D_SC), SC_KERNEL ** -0.5),
        "head_g": gain(ks[12], (DEPTH, D_MIX)),
        "w_out": nrm(ks[13], (DEPTH, D_MIX, D_MODEL), D_MIX ** -0.5),
        "ffn_pre_g": gain(ks[14], (DEPTH, D_MODEL)),
        "ffn_post_g": gain(ks[15], (DEPTH, D_MODEL)),
        "ffn_w_up": nrm(ks[16], (DEPTH, D_MODEL, 2 * D_FF), D_MODEL ** -0.5),
        "ffn_conv_w": nrm(ks[17], (DEPTH, FFN_KERNEL, 2 * D_FF), FFN_KERNEL ** -0.5),
        "ffn_w_down": nrm(ks[18], (DEPTH, D_FF, D_MODEL), D_FF ** -0.5),
    }


def reference(x, meta_tokens, mix_pre_g, mix_post_g, w_in, b_forget, a_dw_w,
              a_dw_b, a_ln_g, a_ln_b, a_pw_w, sc_conv_w, head_g, w_out,
              ffn_pre_g, ffn_post_g, ffn_w_up, ffn_conv_w, ffn_w_down):
    bsz = x.shape[0]
    meta = jnp.broadcast_to(meta_tokens[None].astype(x.dtype), (bsz, N_META, x.shape[-1]))
    h = jnp.concatenate([meta, x], axis=1)
    for l in range(DEPTH):
        h = mixer_sublayer(h, mix_pre_g[l], mix_post_g[l], w_in[l], b_forget[l],
                           a_dw_w[l], a_dw_b[l], a_ln_g[l], a_ln_b[l], a_pw_w[l],
                           sc_conv_w[l], head_g[l], w_out[l])
        h = ffn_sublayer(h, ffn_pre_g[l], ffn_post_g[l], ffn_w_up[l],
                         ffn_conv_w[l], ffn_w_down[l])
    return h[:, N_META:]
```

```python
import numpy as np
import concourse.bass as bass
import concourse.mybir as mybir
from concourse.bass_utils import run_bass_kernel_spmd

F32 = mybir.dt.float32
BF16 = mybir.dt.bfloat16
AF = mybir.ActivationFunctionType
ALU = mybir.AluOpType
AX = mybir.AxisListType

NCORES = 8
SEQ = 2048
NMETA = 16
L = SEQ + NMETA
DM = 1024
DFF = 2816
DIN = 2824
TT = [(0, 416), (416, 412), (828, 412), (1240, 412), (1652, 412)]
BLK = [(128 * b, 128) for b in range(16)] + [(2048, 16)]
FFN_PASSES = [[0, 1], [2, 3], [4]]
RMS_EPS = 1e-6
LN_EPS = 1e-5

ZC = dict(q=0, k=512, v=1024, f=1536, aval=1544, agate=1800, scb=2056, scc=2312, scx=2568)
WIN_FM = [("aval", 0), ("agate", 0), ("aval", 1), ("agate", 1),
          ("scc", 0), ("scx", 0), ("scb", 0), ("scc", 1), ("scx", 1), ("scb", 1),
          ("q", 0), ("q", 1), ("q", 2), ("q", 3), ("k", 0), ("k", 1), ("k", 2), ("k", 3)]
PIECES = []
for i in range(9):
    PIECES.append(("win%d" % i, 2048))
PIECES.append(("pw", 512))
PIECES.append(("vfa", 2048))
PIECES.append(("vfb", 2112))
for i in range(4):
    PIECES.append(("wout%d" % i, 2048))
for i in range(22):
    PIECES.append(("wup%d" % i, 2048))
for i in range(8):
    PIECES.append(("wdn%d" % i, 2816))
POFF = {}
_o = 0
for _n, _s in PIECES:
    POFF[_n] = (_o, _s)
    _o += _s
WL = _o
assert WL == 98880
SLOT = 2816
NSLOT = 4
PRE_BLK = 4120

VEC = {}
_v = 0
for _n, _s in [("mix_pre_g", 8), ("mix_post_g", 8), ("ffn_pre_g", 8), ("ffn_post_g", 8), ("head_g", 8),
               ("dw_w", 62), ("dw_b", 2), ("ln_g", 2), ("ln_b", 2), ("sc_w", 6), ("ffn_cw", 132)]:
    VEC[_n] = _v
    _v += _s
NVEC = _v

C_ID, C_ONES, C_TRI, C_NEG, C_BO = 0, 128, 256, 384, 512
NCONST = 640


def host_consts():
    c = np.zeros((128, NCONST), np.float32)
    i = np.arange(128)
    c[:, C_ID:C_ID + 128] = np.eye(128, dtype=np.float32)
    c[:, C_ONES:C_ONES + 128] = 1.0
    c[:, C_TRI:C_TRI + 128] = (i[:, None] <= i[None, :]).astype(np.float32)
    c[:, C_NEG:C_NEG + 128] = np.where(i[:, None] > i[None, :], -30000.0, 0.0).astype(np.float32)
    c[:, C_BO:C_BO + 128] = ((i[:, None] // 64) == (i[None, :] // 64)).astype(np.float32)
    return c


def host_weights(w_in, a_pw_w, w_out, ffn_w_up, ffn_w_down):
    out = np.empty((128, 2 * WL), np.float32)
    for l in range(2):
        base = l * WL
        wi = w_in[l].reshape(8, 128, DIN)
        for i in range(9):
            o, s = POFF["win%d" % i]
            blk = np.empty((128, 2, 8, 128), np.float32)
            for j in range(2):
                nm, cc = WIN_FM[2 * i + j]
                c0 = ZC[nm] + 128 * cc
                blk[:, j] = wi[:, :, c0:c0 + 128].transpose(1, 0, 2)
            out[:, base + o:base + o + s] = blk.reshape(128, -1)
        o, s = POFF["pw"]
        out[:, base + o:base + o + s] = a_pw_w[l].reshape(2, 128, 256).transpose(1, 0, 2).reshape(128, -1)
        o, s = POFF["vfa"]
        out[:, base + o:base + o + s] = wi[:, :, 1024:1280].transpose(1, 0, 2).reshape(128, -1)
        o, s = POFF["vfb"]
        out[:, base + o:base + o + s] = wi[:, :, 1280:1544].transpose(1, 0, 2).reshape(128, -1)
        wo = w_out[l].reshape(8, 128, DM)
        for i in range(4):
            o, s = POFF["wout%d" % i]
            blk = np.empty((128, 2, 8, 128), np.float32)
            for j in range(2):
                c0 = 128 * (2 * i + j)
                blk[:, j] = wo[:, :, c0:c0 + 128].transpose(1, 0, 2)
            out[:, base + o:base + o + s] = blk.reshape(128, -1)
        wu = ffn_w_up[l].reshape(8, 128, 2 * DFF)
        for i in range(22):
            o, s = POFF["wup%d" % i]
            blk = np.empty((128, 2, 8, 128), np.float32)
            blk[:, 0] = wu[:, :, 128 * i:128 * i + 128].transpose(1, 0, 2)
            blk[:, 1] = wu[:, :, DFF + 128 * i:DFF + 128 * i + 128].transpose(1, 0, 2)
            out[:, base + o:base + o + s] = blk.reshape(128, -1)
        wd = ffn_w_down[l].reshape(22, 128, DM)
        for i in range(8):
            o, s = POFF["wdn%d" % i]
            out[:, base + o:base + o + s] = wd[:, :, 128 * i:128 * i + 128].transpose(1, 0, 2).reshape(128, -1)
    return out


def host_vecs(inp):
    v = np.zeros((128, 2 * NVEC), np.float32)

    def cols(a):
        return np.ascontiguousarray(a.reshape(-1, 128).T)
    for l in range(2):
        b = l * NVEC
        for nm in ["mix_pre_g", "mix_post_g", "ffn_pre_g", "ffn_post_g", "head_g"]:
            v[:, b + VEC[nm]:b + VEC[nm] + 8] = cols(inp[nm][l])
        dw = inp["a_dw_w"][l]
        v[:, b + VEC["dw_w"]:b + VEC["dw_w"] + 62] = dw.reshape(31, 2, 128).transpose(2, 0, 1).reshape(128, 62)
        v[:, b + VEC["dw_b"]:b + VEC["dw_b"] + 2] = cols(inp["a_dw_b"][l])
        v[:, b + VEC["ln_g"]:b + VEC["ln_g"] + 2] = cols(inp["a_ln_g"][l])
        v[:, b + VEC["ln_b"]:b + VEC["ln_b"] + 2] = cols(inp["a_ln_b"][l])
        sc = inp["sc_conv_w"][l]
        v[:, b + VEC["sc_w"]:b + VEC["sc_w"] + 6] = sc.reshape(3, 2, 128).transpose(2, 0, 1).reshape(128, 6)
        fc = inp["ffn_conv_w"][l]
        v[:, b + VEC["ffn_cw"]:b + VEC["ffn_cw"] + 132] = fc.reshape(3, 44, 128).transpose(2, 0, 1).reshape(128, 132)
    return v


class Rec:
    __slots__ = ("eng", "idx", "fn", "deps", "needs_inc", "waits", "clock", "sem", "semval", "cnt", "is_dma", "ph")


class Res:
    __slots__ = ("ws", "rs")

    def __init__(self):
        self.ws = {}
        self.rs = {}


class TRes:
    def __init__(self, nch, tiles=TT):
        self.tiles = tiles
        self.r = [[Res() for _ in tiles] for _ in range(nch)]

    def get(self, c, lo, hi):
        return [self.r[c][t] for t, (s, n) in enumerate(self.tiles) if s < hi and lo < s + n]

    def all(self):
        return [x for row in self.r for x in row]


class Sched:
    COMPUTE = ("pe", "act", "dve", "pool")

    def __init__(self, nc):
        self.nc = nc
        self.ins = {e: [] for e in ("pe", "act", "dve", "pool", "sp")}
        self.order = []
        self.streams = {}
        self.ph = ""

    def add(self, eng, fn, reads=(), writes=()):
        r = Rec()
        r.eng = eng
        r.fn = fn
        r.is_dma = False
        r.needs_inc = False
        r.sem = None
        r.ph = self.ph
        lst = self.ins[eng]
        r.idx = len(lst) + 1
        deps = set()
        key = eng
        for x in reads:
            deps.update(x.ws.values())
        for x in writes:
            deps.update(x.ws.values())
            deps.update(x.rs.values())
        for x in reads:
            x.rs[key] = r
        for x in writes:
            x.ws = {key: r}
            x.rs = {}
        deps.discard(r)
        r.deps = deps
        lst.append(r)
        self.order.append(r)
        return r

    def dma(self, stream, fn, reads=(), writes=(), nsem=4):
        st = self.streams.setdefault(stream, dict(recs=[], nsem=nsem, sems=None))
        r = Rec()
        r.eng = "sp"
        r.fn = fn
        r.is_dma = True
        r.needs_inc = True
        r.ph = self.ph
        lst = self.ins["sp"]
        r.idx = len(lst) + 1
        n = len(st["recs"])
        r.sem = (stream, n % st["nsem"])
        r.semval = 16 * (n // st["nsem"] + 1)
        deps = set()
        if n >= st["nsem"]:
            deps.add(st["recs"][n - st["nsem"]])
        key = ("d", id(r))
        for x in reads:
            deps.update(x.ws.values())
        for x in writes:
            deps.update(x.ws.values())
            deps.update(x.rs.values())
        for x in reads:
            x.rs[key] = r
        for x in writes:
            x.ws = {key: r}
            x.rs = {}
        r.deps = deps
        st["recs"].append(r)
        lst.append(r)
        self.order.append(r)
        return r

    def op(self, eng, method, reads=(), writes=(), **kw):
        r = self.add(eng, (lambda e: getattr(e, method)(**kw)), reads, writes)
        r.ph = r.ph + ":" + method + ":" + str(kw.get("func", kw.get("op", kw.get("op0", ""))))
        return r

    def dmaop(self, stream, out, in_, reads=(), writes=(), nsem=4):
        return self.dma(stream, (lambda e: e.dma_start(out=out, in_=in_)), reads, writes, nsem)

    def finalize(self):
        run = {e: {} for e in self.ins}
        for r in self.order:
            clock = run[r.eng]
            waits = []
            for d in sorted(r.deps, key=lambda d: (d.eng, d.idx)):
                if d.is_dma:
                    k = ("d", d.sem)
                    if clock.get(k, 0) < d.semval:
                        waits.append(d)
                        for kk, vv in d.clock.items():
                            if clock.get(kk, 0) < vv:
                                clock[kk] = vv
                        clock[k] = d.semval
                elif d.eng == r.eng:
                    if r.eng in ("pe", "sp") or r.idx - d.idx > 2:
                        continue
                    d.needs_inc = True
                    waits.append(d)
                else:
                    if clock.get(d.eng, 0) < d.idx:
                        d.needs_inc = True
                        waits.append(d)
                        for kk, vv in d.clock.items():
                            if clock.get(kk, 0) < vv:
                                clock[kk] = vv
            r.waits = waits
            if r.is_dma:
                r.clock = dict(clock)
            else:
                clock[r.eng] = r.idx
                r.clock = dict(clock)
        for e in self.COMPUTE:
            c = 0
            for r in self.ins[e]:
                if r.needs_inc:
                    c += 1
                r.cnt = c

    def emit(self, eng_name, eng, esems, dsems):
        for r in self.ins[eng_name]:
            for d in r.waits:
                if d.is_dma:
                    eng.wait_ge(dsems[d.sem], d.semval)
                else:
                    eng.wait_ge(esems[d.eng], d.cnt)
            ins = r.fn(eng)
            if r.is_dma:
                ins.then_inc(dsems[r.sem], 16)
            elif r.needs_inc:
                ins.then_inc(esems[r.eng], 1)


class Arena:
    def __init__(self, tensor, nbytes):
        self.t = tensor
        self.nbytes = nbytes
        self.off = 0

    def alloc(self, nbytes):
        nbytes = (nbytes + 63) // 64 * 64
        o = self.off
        self.off += nbytes
        assert self.off <= self.nbytes, (self.off, self.nbytes)
        return o

    def f32(self, off, n):
        return self.t[:, off // 4: off // 4 + n]

    def bf16(self, off, n):
        v = self.t[:, off // 4: off // 4 + (n + 1) // 2].bitcast(BF16)
        return v[:, 0:n]


CFG = dict(nseq=4, nlayer=2, stop=None, taps=())


class Pend:
    def __init__(self):
        self.p = {}

    def retire(self, res_list):
        for r in res_list:
            for dct in (r.ws, r.rs):
                for k, rec in dct.items():
                    o = self.p.get(k)
                    if o is None or o.idx < rec.idx:
                        self.p[k] = rec
            r.ws = {}
            r.rs = {}

    def new(self):
        r = Res()
        r.rs = dict(self.p)
        return r

    def tres(self, nch, tiles=TT):
        t = TRes.__new__(TRes)
        t.tiles = tiles
        t.r = [[self.new() for _ in tiles] for _ in range(nch)]
        return t


def build(cfg):
    NSEQ = cfg["nseq"]
    NLAYER = cfg["nlayer"]
    STOP = cfg["stop"]
    nc = bass.Bass("TRN2", target_bir_lowering=False, dynamic_dma_scratch_size=64)
    xT = nc.dram_tensor("xT", [NSEQ, 128, 8, SEQ], F32, kind="ExternalInput").ap()
    metaT = nc.dram_tensor("metaT", [128, 8, NMETA], F32, kind="ExternalInput").ap()
    wf = nc.dram_tensor("wf", [128, 2 * WL], F32, kind="ExternalInput").ap()
    vecs_d = nc.dram_tensor("vecs", [128, 2 * NVEC], F32, kind="ExternalInput").ap()
    bfg_d = nc.dram_tensor("bfg", [128, 16], F32, kind="ExternalInput").ap()
    consts_d = nc.dram_tensor("consts", [128, NCONST], F32, kind="ExternalInput").ap()
    outT = nc.dram_tensor("outT", [NSEQ, 128, 8, SEQ], F32, kind="ExternalOutput").ap()
    wb = nc.dram_tensor("wb", [128, 2 * WL], BF16, kind="Internal").ap()
    taps = {}
    for nm, shape, dt in cfg["taps"]:
        taps[nm] = nc.dram_tensor("tap_" + nm, list(shape), BF16 if dt == "bf16" else F32, kind="ExternalOutput").ap()

    S = Sched(nc)
    ARENA_BYTES = 229056
    from contextlib import ExitStack
    es = ExitStack()
    arena_t = es.enter_context(nc.sbuf_tensor("arena", [128, ARENA_BYTES // 4], F32))
    A = Arena(arena_t, ARENA_BYTES)
    psum = [es.enter_context(nc.psum_tensor("ps%d" % i, [128, 512], F32)) for i in range(8)]
    PS = [Res() for _ in range(8)]
    psrot = dict(i=0)

    def nb(banks=(0, 1, 2, 3, 4, 5)):
        b = banks[psrot["i"] % len(banks)]
        psrot["i"] += 1
        return b

    o_h = A.alloc(8 * L * 4)
    o_w = A.alloc(NSLOT * SLOT * 2)
    o_rstd = A.alloc(8320)
    o_vec = A.alloc(2 * NVEC * 4)
    o_bfg = A.alloc(16 * 4)
    o_c32 = A.alloc(NCONST * 4)
    o_c16 = A.alloc(NCONST * 2)
    NT32, NT16 = 5, 4
    o_t32 = A.alloc(NT32 * 416 * 4)
    o_t16 = A.alloc(NT16 * 416 * 2)
    o_eps = A.alloc(64)
    o_tr = A.alloc(2 * 416 * 4)
    o_P = A.off
    P_BYTES = ARENA_BYTES - o_P
    assert P_BYTES >= 100240, P_BYTES
    o_u = o_P
    o_qk = o_u + 8 * L * 2
    o_v = o_qk + 8 * L * 2
    o_yb = o_v + 17 * 520 * 2
    o_sp = o_yb + 4 * L * 2
    o_dg1 = o_sp
    o_qz = o_dg1 + 31 * 128 * 2
    assert o_qz + 2 * 1024 <= ARENA_BYTES, (o_qz, ARENA_BYTES)

    hT = A.f32(o_h, 8 * L).rearrange("p (c t) -> p c t", c=8)
    rstd = A.f32(o_rstd, L)
    vec = A.f32(o_vec, 2 * NVEC)
    bfg = A.f32(o_bfg, 16)
    c32 = A.f32(o_c32, NCONST)
    c16 = A.bf16(o_c16, NCONST)
    wslot = [A.bf16(o_w + i * SLOT * 2, SLOT) for i in range(NSLOT)]
    tmp32 = [A.f32(o_t32 + i * 416 * 4, 416) for i in range(NT32)]
    tmp16 = [A.bf16(o_t16 + i * 416 * 2, 416) for i in range(NT16)]
    epsv = A.f32(o_eps, 8)
    tmpr = [A.f32(o_tr + i * 416 * 4, 416) for i in range(2)]
    R_tmpr = [Res(), Res()]
    epsc, lnepsc, onec = epsv[:, 0:1], epsv[:, 1:2], epsv[:, 2:3]

    PD = Pend()
    R_h = TRes(8)
    R_const = Res()
    R_wb = Res()
    R_wslot = [Res() for _ in range(NSLOT)]
    R_tmp32 = [Res() for _ in range(NT32)]
    R_tmp16 = [Res() for _ in range(NT16)]
    rot = dict(t32=0, t16=0, w=0, tr=0)

    def vcol(l, name, j=0):
        c = l * NVEC + VEC[name] + j
        return vec[:, c:c + 1]

    def MM(out, lhsT, rhs, start, stop, reads, writes, **kw):
        S.op("pe", "matmul", reads, writes, out=out, lhsT=lhsT, rhs=rhs, start=start, stop=stop, **kw)

    def ACT(out, in_, func, reads, writes, bias=None, scale=None):
        kw = dict(out=out, in_=in_, func=func)
        if bias is not None:
            kw["bias"] = bias
        if scale is not None:
            kw["scale"] = scale
        S.op("act", "activation", reads, writes, **kw)

    def STT(eng, out, in0, scalar, in1, op0, op1, reads, writes):
        S.op(eng, "scalar_tensor_tensor", reads, writes, out=out, in0=in0, scalar=scalar, in1=in1, op0=op0, op1=op1)

    def TTOP(eng, out, in0, in1, op, reads, writes):
        S.op(eng, "tensor_tensor", reads, writes, out=out, in0=in0, in1=in1, op=op)

    def TS(eng, out, in0, s1, op0, reads, writes, s2=None, op1=None):
        kw = dict(out=out, in0=in0, scalar1=s1, scalar2=s2, op0=op0)
        if op1 is not None:
            kw["op1"] = op1
        S.op(eng, "tensor_scalar", reads, writes, **kw)

    def CP(eng, out, in_, reads, writes):
        if eng == "act":
            S.op("act", "activation", reads, writes, out=out, in_=in_, func=AF.Copy)
        else:
            S.op(eng, "tensor_copy", reads, writes, out=out, in_=in_)

    def RSQRT(out, in_, scale, eps_ap, reads, writes):
        np_ = in_.shape[0]
        tm, rtm = tmpr[rot["tr"] % 2], R_tmpr[rot["tr"] % 2]
        rot["tr"] += 1
        n_ = in_.shape[-1]
        tv = tm[0:np_, 0:n_]
        if eps_ap is None:
            ACT(tv, in_, AF.Ln, list(reads), [rtm], scale=scale)
        else:
            ACT(tv, in_, AF.Ln, list(reads) + [R_const], [rtm], bias=eps_ap, scale=scale)
        ACT(out, tv, AF.Exp, [rtm], writes, scale=-0.5)

    def RECIP(out, in_, reads, writes):
        S.op("dve", "reciprocal", reads, writes, out=out, in_=in_)

    def MEMSET(eng, ap, val, writes):
        S.op(eng, "memset", (), writes, ap=ap, constant=val)

    S.dmaop("misc", c32, consts_d, writes=[R_const])
    S.dmaop("misc", vec, vecs_d, writes=[R_const])
    S.dmaop("misc", bfg, bfg_d, writes=[R_const])
    CP("dve", c16, c32, [R_const], [R_const])
    MEMSET("dve", epsv[:, 0:1], RMS_EPS, [R_const])
    MEMSET("dve", epsv[:, 1:2], LN_EPS, [R_const])
    MEMSET("dve", epsv[:, 2:3], 1.0, [R_const])
    MEMSET("dve", epsv[:, 3:4], 0.0, [R_const])
    ident16 = c16[:, C_ID:C_ID + 128]
    ones16 = c16[:, C_ONES:C_ONES + 128]
    negtri16 = c16[:, C_NEG:C_NEG + 128]
    bo16 = c16[:, C_BO:C_BO + 128]
    ones32 = c32[:, C_ONES:C_ONES + 128]
    tri32 = c32[:, C_TRI:C_TRI + 128]

    dgd = nc.dram_tensor("dgd", [128, 2 * 2 * 3968], BF16, kind="Internal").ap()
    R_dgd = Res()
    dg1 = A.bf16(o_dg1, 31 * 128).rearrange("p (k m) -> p k m", k=31)
    R_dg1 = Res()
    qz = [A.bf16(o_qz + par * 1024, 512) for par in range(2)]
    R_qz = [Res(), Res()]
    for par in range(2):
        MEMSET("dve", qz[par], 0.0, [R_qz[par]])
    dgst = [A.bf16(o_P + i * 7936, 3968).rearrange("p (k m) -> p k m", k=31) for i in range(2)]
    R_dgst = [PD.new(), PD.new()]
    for l_ in range(NLAYER):
        for cc_ in range(2):
            i_ = (l_ * 2 + cc_) % 2
            for k in range(31):
                TS("dve", dgst[i_][:, k, :], ident16, vcol(l_, "dw_w", 2 * k + cc_), ALU.mult, [R_const], [R_dgst[i_]])
            S.dmaop("dgo", dgd[:, (l_ * 2 + cc_) * 3968:(l_ * 2 + cc_ + 1) * 3968], dgst[i_].rearrange("p k m -> p (k m)"), reads=[R_dgst[i_]], writes=[R_dgd], nsem=2)
    PD.retire(R_dgst)
    st32 = [A.f32(o_P + i * PRE_BLK * 4, PRE_BLK) for i in range(2)]
    st16 = [A.bf16(o_P + 2 * PRE_BLK * 4 + i * PRE_BLK * 2, PRE_BLK) for i in range(2)]
    R_st32 = [PD.new(), PD.new()]
    R_st16 = [PD.new(), PD.new()]
    nblk_used = (NLAYER * WL + PRE_BLK - 1) // PRE_BLK
    for b in range(nblk_used):
        i = b % 2
        sl = slice(b * PRE_BLK, (b + 1) * PRE_BLK)
        S.dmaop("prein", st32[i], wf[:, sl], writes=[R_st32[i]], nsem=2)
        CP(("dve", "act")[b % 2], st16[i], st32[i], [R_st32[i]], [R_st16[i]])
        S.dmaop("preout", wb[:, sl], st16[i], reads=[R_st16[i]], writes=[R_wb], nsem=2)
    PD.retire(R_st32 + R_st16)

    def wload(l, name):
        o, s = POFF[name]
        i = rot["w"] % NSLOT
        rot["w"] += 1
        S.dmaop("w", wslot[i][:, 0:s], wb[:, l * WL + o: l * WL + o + s], reads=[R_wb], writes=[R_wslot[i]], nsem=NSLOT)
        return wslot[i], R_wslot[i]

    def t32():
        i = rot["t32"] % NT32
        rot["t32"] += 1
        return tmp32[i], R_tmp32[i]

    def t16():
        i = rot["t16"] % NT16
        rot["t16"] += 1
        return tmp16[i], R_tmp16[i]

    def tap(name, src_ap, reads):
        if name in taps:
            S.dmaop("tap", taps[name], src_ap, reads=reads, nsem=1)

    def load_seq(s):
        for t, (t0, n) in enumerate(TT):
            lo, hi = max(t0, NMETA), t0 + n
            S.dmaop("xin", hT[:, :, lo:hi], xT[s, :, :, lo - NMETA:hi - NMETA], writes=[R_h.r[c][t] for c in range(8)], nsem=4)
            if t == 0:
                S.dmaop("xin", hT[:, :, 0:NMETA], metaT, writes=[R_h.r[c][0] for c in range(8)], nsem=4)

    def store_seq(s, tiles):
        for t in tiles:
            t0, n = TT[t]
            lo, hi = max(t0, NMETA), t0 + n
            S.dmaop("xout", outT[s, :, :, lo - NMETA:hi - NMETA], hT[:, :, lo:hi], reads=[R_h.r[c][t] for c in range(8)], nsem=4)

    R_rstd = dict(r=None)

    def prenorm(l, gname, tiles, u_fn, Ru):
        for t in tiles:
            t0, n = TT[t]
            pb = nb()
            for c in range(8):
                sq, rsq = t16()
                ACT(sq[:, 0:n], hT[:, c, t0:t0 + n], AF.Square, [R_h.r[c][t]], [rsq])
                MM(psum[pb][:, 0:n], ones16, sq[:, 0:n], c == 0, c == 7, [rsq, R_const], [PS[pb]])
            RSQRT(rstd[:, t0:t0 + n], psum[pb][:, 0:n], 1.0 / DM, epsc, [PS[pb]], [R_rstd["r"].r[0][t]])
            for c in range(8):
                STT("dve", u_fn(c, t0, n), hT[:, c, t0:t0 + n], vcol(l, gname, c), rstd[:, t0:t0 + n], ALU.mult, ALU.mult,
                    [R_h.r[c][t], R_rstd["r"].r[0][t], R_const], [Ru.r[c][t]])

    def hn_a(l, ychunk, src, src_reads, n, dst, dst_writes):
        sq, rsq = t16()
        ACT(sq[:, 0:n], src, AF.Square, src_reads, [rsq])
        return (l, ychunk, src, src_reads, n, dst, dst_writes, sq, rsq)

    def hn_b(st):
        l, ychunk, src, src_reads, n, dst, dst_writes, sq, rsq = st
        pb = nb()
        MM(psum[pb][:, 0:n], bo16, sq[:, 0:n], True, True, [rsq, R_const], [PS[pb]])
        sd, rsd = t32()
        RSQRT(sd[:, 0:n], psum[pb][:, 0:n], 1.0 / 64, epsc, [PS[pb]], [rsd])
        STT("dve", dst, src, vcol(l, "head_g", ychunk), sd[:, 0:n], ALU.mult, ALU.mult, src_reads + [rsd, R_const], dst_writes)

    def mixer(s, l, pre_hook=None, rstd4=None):
        uT = A.bf16(o_u, 8 * L).rearrange("p (c t) -> p c t", c=8)
        R_u = PD.tres(8)
        R_rstd["r"] = PD.tres(1)
        yB = A.bf16(o_yb, 4 * L).rearrange("p (c t) -> p c t", c=4)
        R_yb = PD.tres(4)
        S.ph = "m_prenorm"
        if rstd4 is not None:
            R_rstd["r"].r[0][4] = rstd4
        prenorm(l, "mix_pre_g", range(4), (lambda c, t0, n: uT[:, c, t0:t0 + n]), R_u)
        if pre_hook is not None:
            S.ph = "f_tail"
            pre_hook()
            S.ph = "m_prenorm"
        prenorm(l, "mix_pre_g", [4], (lambda c, t0, n: uT[:, c, t0:t0 + n]), R_u)
        S.ph = "m_conv"
        if STOP == "prenorm":
            tap("u", uT, R_u.all())
            return
        oc = o_qk
        apad = A.bf16(oc, 2 * (L + 30)).rearrange("p (c t) -> p c t", c=2)
        oc += 2 * (L + 30) * 2
        oc = (oc + 63) // 64 * 64
        cv = A.f32(oc, 2 * L).rearrange("p (c t) -> p c t", c=2)
        oc += 2 * L * 4
        a2 = A.bf16(oc, 2 * L).rearrange("p (c t) -> p c t", c=2)
        oc += 2 * L * 2
        cxpad = A.f32(oc, L + 2)
        oc += (L + 2) * 4
        oc = (oc + 63) // 64 * 64
        dg = A.bf16(o_rstd, 31 * 128).rearrange("p (k m) -> p k m", k=31)
        PD.retire(R_rstd["r"].all())
        lnt = [A.f32(oc + i * 416 * 4, 416) for i in range(3)]
        oc += 3 * 416 * 4
        assert oc <= o_yb, (oc, o_yb)
        R_apad = PD.tres(2)
        R_apad0 = PD.new()
        R_cv = PD.tres(2)
        R_a2 = PD.tres(2)
        R_cx = PD.tres(1)
        R_cx0 = PD.new()
        R_dg = PD.new()
        R_lnt = [PD.new() for _ in range(3)]
        MEMSET("dve", apad[:, :, 0:30], 0.0, [R_apad0])
        MEMSET("dve", cxpad[:, 0:2], 0.0, [R_cx0])

        def fm_mm(wv, rw, j, t, pb, halo=0):
            t0, n = TT[t]
            for k in range(8):
                MM(psum[pb][:, 0:n], wv[:, j, k, :], uT[:, k, t0:t0 + n], k == 0, k == 7, [rw, R_u.r[k][t]], [PS[pb]])

        dgs = [dg, dg1]
        R_dgs = [R_dg, R_dg1]
        for cc_ in range(2):
            S.dmaop("dg", dgs[cc_].rearrange("p k m -> p (k m)"), dgd[:, (l * 2 + cc_) * 3968:(l * 2 + cc_ + 1) * 3968], reads=[R_dgd], writes=[R_dgs[cc_]], nsem=2)
        for cc in range(2):
            wt, rw = wload(l, "win%d" % cc)
            wv = wt[:, 0:2048].rearrange("p (j k m) -> p j k m", j=2, k=8)
            for t, (t0, n) in enumerate(TT):
                pv, pg = nb(), nb()
                fm_mm(wv, rw, 0, t, pv)
                fm_mm(wv, rw, 1, t, pg)
                sg, rsg = t32()
                ACT(sg[:, 0:n], psum[pg][:, 0:n], AF.Sigmoid, [PS[pg]], [rsg])
                TTOP("dve", apad[:, cc, 30 + t0:30 + t0 + n], psum[pv][:, 0:n], sg[:, 0:n], ALU.mult, [PS[pv], rsg], [R_apad.r[cc][t]])
            if cc == 0:
                pass
            for t, (t0, n) in enumerate(TT):
                pb = nb()
                rd = R_apad.get(cc, t0 - 30, t0 + n) + [R_dgs[cc], R_apad0]
                for k in range(31):
                    MM(psum[pb][:, 0:n], dgs[cc][:, k, :], apad[:, cc, t0 + k:t0 + k + n], k == 0, k == 30, rd, [PS[pb]])
                ACT(cv[:, cc, t0:t0 + n], psum[pb][:, 0:n], AF.Identity, [PS[pb], R_const], [R_cv.r[cc][t]], bias=vcol(l, "dw_b", cc))
        tap("cv", cv, R_cv.all())
        S.ph = "m_ln_pw"
        for t, (t0, n) in enumerate(TT):
            pm, pq = nb(), nb()
            for cc in range(2):
                MM(psum[pm][:, 0:n], ones32, cv[:, cc, t0:t0 + n], cc == 0, cc == 1, [R_cv.r[cc][t], R_const], [PS[pm]])
            for cc in range(2):
                sq, rsq = t32()
                ACT(sq[:, 0:n], cv[:, cc, t0:t0 + n], AF.Square, [R_cv.r[cc][t]], [rsq])
                MM(psum[pq][:, 0:n], ones32, sq[:, 0:n], cc == 0, cc == 1, [rsq, R_const], [PS[pq]])
            mu, m2, rs_ = lnt[0], lnt[1], lnt[2]
            ACT(mu[:, 0:n], psum[pm][:, 0:n], AF.Copy, [PS[pm]], [R_lnt[0]], scale=1.0 / 256)
            TTOP("dve", m2[:, 0:n], mu[:, 0:n], mu[:, 0:n], ALU.mult, [R_lnt[0]], [R_lnt[1]])
            STT("dve", m2[:, 0:n], psum[pq][:, 0:n], 1.0 / 256, m2[:, 0:n], ALU.mult, ALU.subtract, [PS[pq], R_lnt[1]], [R_lnt[1]])
            RSQRT(rs_[:, 0:n], m2[:, 0:n], 1.0, lnepsc, [R_lnt[1]], [R_lnt[2]])
            for cc in range(2):
                xc, rxc = t32()
                TTOP("dve", xc[:, 0:n], cv[:, cc, t0:t0 + n], mu[:, 0:n], ALU.subtract, [R_cv.r[cc][t], R_lnt[0]], [rxc])
                STT("dve", xc[:, 0:n], xc[:, 0:n], vcol(l, "ln_g", cc), rs_[:, 0:n], ALU.mult, ALU.mult, [rxc, R_lnt[2], R_const], [rxc])
                ACT(a2[:, cc, t0:t0 + n], xc[:, 0:n], AF.Silu, [rxc, R_const], [R_a2.r[cc][t]], bias=vcol(l, "ln_b", cc))
        wt, rw = wload(l, "pw")
        wv = wt[:, 0:512].rearrange("p (k m) -> p k m", k=2)
        pend = None
        for oc_ in range(2):
            for t, (t0, n) in enumerate(TT):
                pb = nb()
                for k in range(2):
                    MM(psum[pb][:, 0:n], wv[:, k, oc_ * 128:(oc_ + 1) * 128], a2[:, k, t0:t0 + n], k == 0, k == 1,
                       [rw, R_a2.r[k][t]], [PS[pb]])
                st = hn_a(l, 4 + oc_, psum[pb][:, 0:n], [PS[pb]], n, yB[:, oc_, t0:t0 + n], [R_yb.r[oc_][t]])
                if pend is not None:
                    hn_b(pend)
                pend = st
        hn_b(pend)
        S.ph = "m_sc"
        w2, rw2 = wload(l, "win2")
        w3, rw3 = wload(l, "win3")
        w4, rw4 = wload(l, "win4")
        v2 = w2[:, 0:2048].rearrange("p (j k m) -> p j k m", j=2, k=8)
        v3 = w3[:, 0:2048].rearrange("p (j k m) -> p j k m", j=2, k=8)
        v4 = w4[:, 0:2048].rearrange("p (j k m) -> p j k m", j=2, k=8)
        scw = [[(v2, rw2, 0), (v2, rw2, 1), (v3, rw3, 0)], [(v3, rw3, 1), (v4, rw4, 0), (v4, rw4, 1)]]
        pend = None
        for cc in range(2):
            for t, (t0, n) in enumerate(TT):
                pc, px, pbb = nb(), nb(), nb()
                fm_mm(scw[cc][0][0], scw[cc][0][1], scw[cc][0][2], t, pc)
                fm_mm(scw[cc][1][0], scw[cc][1][1], scw[cc][1][2], t, px)
                fm_mm(scw[cc][2][0], scw[cc][2][1], scw[cc][2][2], t, pbb)
                xs, rxs = t32()
                CP("act", xs[:, 0:n], psum[px][:, 0:n], [PS[px]], [rxs])
                TTOP("dve", cxpad[:, 2 + t0:2 + t0 + n], psum[pc][:, 0:n], xs[:, 0:n], ALU.mult, [PS[pc], rxs], [R_cx.r[0][t]])
                acc, racc = t32()
                rd = R_cx.get(0, t0 - 2, t0 + n) + [R_cx0, R_const]
                TS("dve", acc[:, 0:n], cxpad[:, t0:t0 + n], vcol(l, "sc_w", 0 * 2 + cc), ALU.mult, rd, [racc])
                STT("dve", acc[:, 0:n], cxpad[:, t0 + 1:t0 + 1 + n], vcol(l, "sc_w", 1 * 2 + cc), acc[:, 0:n], ALU.mult, ALU.add, rd + [racc], [racc])
                STT("dve", acc[:, 0:n], cxpad[:, t0 + 2:t0 + 2 + n], vcol(l, "sc_w", 2 * 2 + cc), acc[:, 0:n], ALU.mult, ALU.add, rd + [racc], [racc])
                TTOP("dve", acc[:, 0:n], psum[pbb][:, 0:n], acc[:, 0:n], ALU.mult, [PS[pbb], racc], [racc])
                st = hn_a(l, 6 + cc, acc[:, 0:n], [racc], n, yB[:, 2 + cc, t0:t0 + n], [R_yb.r[2 + cc][t]])
                if pend is not None:
                    hn_b(pend)
                pend = st
        hn_b(pend)
        tap("yb", yB, R_yb.all())
        if STOP == "conv":
            return
        S.ph = "m_qkv"
        PD.retire(R_apad.all() + [R_apad0, R_cx0, R_dg] + R_cv.all() + R_a2.all() + R_cx.all() + R_lnt)
        qk = A.bf16(o_qk, 8 * L).rearrange("p (c t) -> p c t", c=8)
        R_qk = PD.tres(8)
        vaug = A.bf16(o_v, 17 * 520).rearrange("p (b h d) -> p b h d", b=17, h=8)
        R_v = [PD.new() for _ in range(17)]
        R_vones = PD.new()
        MEMSET("dve", vaug[:, :, :, 64:65], 1.0, [R_vones])
        for i in range(4):
            wt, rw = wload(l, "win%d" % (5 + i))
            wv = wt[:, 0:2048].rearrange("p (j k m) -> p j k m", j=2, k=8)
            for j in range(2):
                ch = 2 * i + j
                for t, (t0, n) in enumerate(TT):
                    pb = nb()
                    fm_mm(wv, rw, j, t, pb)
                    CP("act", qk[:, ch, t0:t0 + n], psum[pb][:, 0:n], [PS[pb]], [R_qk.r[ch][t]])
        wa, rwa = wload(l, "vfa")
        wbb, rwb = wload(l, "vfb")
        va = wa[:, 0:2048].rearrange("p (k m) -> p k m", k=8)
        vb = wbb[:, 0:2112].rearrange("p (k m) -> p k m", k=8)
        R_fl = PD.new()
        og = o_rstd
        fl = A.f32(og, 136).rearrange("p (b h) -> p b h", b=17)
        og += 136 * 4
        lf = A.f32(og, 136).rearrange("p (b h) -> p b h", b=17)
        og += 136 * 4
        Cx = A.f32(og, 18 * 8).rearrange("p (b h) -> p b h", b=18)
        og += 18 * 8 * 4
        Gt = A.f32(og, 136).rearrange("p (b h) -> p b h", b=17)
        og += 136 * 4
        biasv = A.f32(og, 5 * 136).rearrange("p (i b h) -> p i b h", i=5, b=17)
        og += 5 * 136 * 4
        assert og <= o_rstd + 8320
        MEMSET("dve", fl, 0.0, [R_fl])
        for b, (b0, nt) in enumerate(BLK):
            pa, pbk = nb(), nb()
            rdu = []
            for k in range(8):
                rdu += R_u.get(k, b0, b0 + nt)
            for k in range(8):
                MM(psum[pa][0:nt, 0:256], uT[:, k, b0:b0 + nt], va[:, k, :], k == 0, k == 7, [rwa] + rdu, [PS[pa]])
            for k in range(8):
                MM(psum[pbk][0:nt, 0:264], uT[:, k, b0:b0 + nt], vb[:, k, :], k == 0, k == 7, [rwb] + rdu, [PS[pbk]])
            CP("act", vaug[0:nt, b, 0:4, 0:64], psum[pa][0:nt, 0:256].rearrange("p (h d) -> p h d", h=4), [PS[pa]], [R_v[b]])
            CP("dve", vaug[0:nt, b, 4:8, 0:64], psum[pbk][0:nt, 0:256].rearrange("p (h d) -> p h d", h=4), [PS[pbk]], [R_v[b]])
            TTOP("dve", fl[0:nt, b, :], psum[pbk][0:nt, 256:264], bfg[0:nt, l * 8:l * 8 + 8], ALU.add, [PS[pbk], R_const], [R_fl])
        flf = fl.rearrange("p b h -> p (b h)")
        lff = lf.rearrange("p b h -> p (b h)")
        ACT(lff, flf, AF.Exp, [R_fl], [R_fl], scale=-1.0)
        ACT(lff, lff, AF.Ln, [R_fl, R_const], [R_fl], bias=onec, scale=1.0)
        pc_ = nb()
        MM(psum[pc_][:, 0:136], tri32, lff, True, True, [R_fl, R_const], [PS[pc_]])
        MM(psum[pc_][:, 136:272], ones32, lff, True, True, [R_fl, R_const], [PS[pc_]])
        MEMSET("dve", Cx[:, 0, :], 0.0, [R_fl])
        for b in range(16):
            TTOP("dve", Cx[:, b + 1, :], Cx[:, b, :], psum[pc_][:, 136 + 8 * b:136 + 8 * b + 8], ALU.add, [PS[pc_], R_fl], [R_fl])
        TTOP("dve", Gt.rearrange("p b h -> p (b h)"), psum[pc_][:, 0:136], Cx[:, 0:17, :].rearrange("p b h -> p (b h)"), ALU.add,
             [PS[pc_], R_fl], [R_fl])
        for i in range(5):
            nbk = 4 * i + 4 if i < 4 else 17
            e_i = 4 * i + 4 if i < 4 else 16
            TTOP("dve", biasv[:, i, 0:nbk, :], Gt[:, 0:nbk, :], Cx[:, e_i:e_i + 1, :].broadcast_to([128, nbk, 8]), ALU.subtract, [R_fl], [R_fl])
        tap("gt", Gt.rearrange("p b h -> p (b h)"), [R_fl])
        if STOP == "qkv":
            tap("qk", qk, R_qk.all())
            tap("v", vaug.rearrange("p b h d -> p (b h d)"), R_v)
            return
        S.ph = "m_attn"
        PD.retire(R_u.all())
        oa = o_u
        yA = A.bf16(oa, 4 * L).rearrange("p (c t) -> p c t", c=4)
        oa += 4 * L * 2
        NPT = 4
        pT = [A.bf16(oa + i * 1024, 512) for i in range(NPT)]
        oa += NPT * 1024
        ost = A.f32(oa, 4 * 8 * 65).rearrange("p (r h d) -> p r h d", r=4, h=8)
        oa += 4 * 8 * 65 * 4
        ynt = [A.bf16(oa + i * 1024, 512) for i in range(4)]
        oa += 4096
        SSb, RDb, TTb, RRb, SCb = [A.f32(og + 128 * r_, 32).rearrange("p (r h) -> p r h", r=4) for r_ in range(5)]
        og += 640
        sqt = A.f32(og, 256).rearrange("p (r d) -> p r d", r=4)
        og += 1024
        assert og <= o_rstd + 8320
        assert oa <= o_qk, (oa, o_qk)
        R_yA = PD.tres(4)
        R_pT = [PD.new() for _ in range(NPT)]
        R_ost = PD.new()
        R_sqt = PD.new()
        R_ynt = [PD.new() for _ in range(4)]
        R_sm = PD.new()
        SB, OB, TB = (0, 1, 2, 6), (3, 4), 5
        NSB = len(SB)
        steps = []
        for i in range(5):
            q0, qn = (512 * i, 512) if i < 4 else (2048, 16)
            jmax = 4 * i + 3 if i < 4 else 16
            for h in range(8):
                for j in range(jmax + 1):
                    steps.append((i, h, j, q0, qn, jmax))
        LA = 3
        nsteps = len(steps)
        ynrot = dict(i=0)
        kzrot = [0, 0]

        def stage_q(i, h):
            q0, qn = (512 * i, 512) if i < 4 else (2048, 16)
            kc, po = h // 2, (h % 2) * 64
            CP("dve", qz[h % 2][po:po + 64, 0:qn], qk[po:po + 64, kc, q0:q0 + qn], R_qk.get(kc, q0, q0 + qn), [R_qz[h % 2]])

        def emit_S(n):
            i, h, j, q0, qn, jmax = steps[n]
            kc, po = h // 2, (h % 2) * 64
            b0, kb = BLK[j]
            lo = max(q0, b0)
            N = q0 + qn - lo
            diag = (b0 >= q0)
            sb = SB[n % NSB]
            rq = R_qk.get(kc, lo, q0 + qn)
            rk = R_qk.get(4 + kc, b0, b0 + kb)
            par = h % 2
            if j == 0:
                if (i, h) == (0, 0):
                    stage_q(0, 0)
                nxt = (i, h + 1) if h < 7 else ((i + 1, 0) if i < 4 else None)
                if nxt is not None:
                    stage_q(*nxt)
            MM(psum[sb][0:kb, 0:N], qk[:, 4 + kc, b0:b0 + kb], qz[par][:, lo - q0:lo - q0 + N], True, not diag, rk + [R_qz[par]], [PS[sb]])
            if diag:
                MM(psum[sb][0:kb, 0:kb], ident16[0:kb, 0:kb], negtri16[0:kb, 0:kb], False, True, [R_const], [PS[sb]])
            ACT(pT[n % NPT][0:kb, 0:N], psum[sb][0:kb, 0:N], AF.Exp, [PS[sb], R_fl], [R_pT[n % NPT]], bias=biasv[0:kb, i, j, h:h + 1], scale=0.125)

        def emit_PV(n):
            i, h, j, q0, qn, jmax = steps[n]
            b0, kb = BLK[j]
            lo = max(q0, b0)
            ob = OB[h % 2]
            if i < 4:
                for r in range(4):
                    q = 4 * i + r
                    if q < j:
                        continue
                    off = 128 * q - lo
                    MM(psum[ob][:, r * 65:(r + 1) * 65], pT[n % NPT][0:kb, off:off + 128], vaug[0:kb, j, h, :], (j == 0 and r == 0), j == q,
                       [R_pT[n % NPT], R_v[j], R_vones], [PS[ob]], skip_group_check=True)
            else:
                MM(psum[ob][0:16, 0:65], pT[n % NPT][0:kb, 0:16], vaug[0:kb, j, h, :], j == 0, j == 16, [R_pT[n % NPT], R_v[j], R_vones], [PS[ob]])
            if j == jmax:
                if i < 4:
                    CP("dve", ost[:, :, h, :], psum[ob][:, 0:260].rearrange("p (r d) -> p r d", r=4), [PS[ob]], [R_ost])
                    nt_, R_ = 128, 4
                else:
                    CP("dve", ost[0:16, 0, h, :], psum[ob][0:16, 0:65], [PS[ob]], [R_ost])
                    nt_, R_ = 16, 1
                Oh = ost[0:nt_, 0:R_, h, 0:64]
                TTOP("dve", sqt[0:nt_, 0:R_], Oh, Oh, ALU.mult, [R_ost], [R_sqt])
                S.op("dve", "tensor_reduce", [R_sqt], [R_sm], out=SSb[0:nt_, 0:R_, h], in_=sqt[0:nt_, 0:R_], axis=AX.X, op=ALU.add)
                if h == 7:
                    finish_tile(i)

        DQ = []
        tk = [0]

        def defer(delay, fn):
            DQ.append([tk[0] + delay, fn])

        def tick():
            tk[0] += 1
            for x in [x for x in DQ if x[0] <= tk[0]]:
                DQ.remove(x)
                x[1]()

        def flushq():
            while DQ:
                DQ.pop(0)[1]()

        PS_T = [Res(), Res()]
        PS_T[0].rs = dict(PS[TB].rs)
        PS_T[0].ws = dict(PS[TB].ws)
        PS_T[1].rs = dict(PS[TB].rs)
        PS_T[1].ws = dict(PS[TB].ws)
        pst = psum[TB][:, 0:512].bitcast(BF16)

        def finish_tile(i):
            blocks = [4 * i + r for r in range(4)] if i < 4 else [16]

            nt_, R_ = (128, 4) if i < 4 else (16, 1)
            SS, RD, T_, RR, SC = [b[0:nt_, 0:R_] for b in (SSb, RDb, TTb, RRb, SCb)]

            def stA():
                RECIP(RD, ost[0:nt_, 0:R_, :, 64], [R_ost], [R_sm])
                TTOP("dve", T_, SS, RD, ALU.mult, [R_sm], [R_sm])
                TTOP("dve", T_, T_, RD, ALU.mult, [R_sm], [R_sm])
                TS("dve", T_, T_, 1.0 / 64, ALU.mult, [R_sm], [R_sm], s2=RMS_EPS, op1=ALU.add)

            def stB():
                tm, rtm = tmpr[rot["tr"] % 2], R_tmpr[rot["tr"] % 2]
                rot["tr"] += 1
                tv = tm[0:nt_, 0:R_ * 8].rearrange("p (r h) -> p r h", r=R_)
                ACT(tv, T_, AF.Ln, [R_sm], [rtm])
                ACT(RR, tv, AF.Exp, [rtm], [R_sm], scale=-0.5)

            def stC():
                TTOP("dve", SC, RR, RD, ALU.mult, [R_sm], [R_sm])
                for r, q in enumerate(blocks):
                    b0, nt = BLK[q]
                    TTOP("dve", ynt[r][0:nt].rearrange("p (h d) -> p h d", h=8), ost[0:nt, r, :, 0:64], SCb[0:nt, r, :].unsqueeze(2).broadcast_to([nt, 8, 64]),
                         ALU.mult, [R_ost, R_sm], [R_ynt[r]])

            def stD(r, q):
                b0, nt = BLK[q]
                hb = r % 2
                for c in range(4):
                    S.op("pe", "transpose", [R_ynt[r], R_const], [PS_T[hb]], out=pst[:, hb * 512 + c * 128:hb * 512 + c * 128 + nt],
                         in_=ynt[r][0:nt, c * 128:(c + 1) * 128], identity=ident16[0:nt, 0:nt])

            def stE(r, q):
                b0, nt = BLK[q]
                hb = r % 2
                for c in range(4):
                    ACT(yA[:, c, b0:b0 + nt], pst[:, hb * 512 + c * 128:hb * 512 + c * 128 + nt], AF.Copy, [PS_T[hb], R_const], R_yA.get(c, b0, b0 + nt),
                        scale=vcol(l, "head_g", c))

            stA()
            defer(3, stB)
            defer(6, stC)
            for r, q in enumerate(blocks):
                defer(13 + 2 * r, (lambda r=r, q=q: stD(r, q)))
                defer(16 + 2 * r, (lambda r=r, q=q: stE(r, q)))

        for n in range(nsteps + LA):
            if n < nsteps:
                emit_S(n)
            if n - LA >= 0:
                emit_PV(n - LA)
            tick()
        flushq()
        for hb in range(2):
            for dct in (PS_T[hb].ws, PS_T[hb].rs):
                for k_, rec_ in dct.items():
                    o_ = PS[TB].rs.get(k_)
                    if o_ is None or o_.idx < rec_.idx:
                        PS[TB].rs[k_] = rec_
        tap("ya", yA, R_yA.all())
        if STOP == "attn":
            return
        S.ph = "m_wout"
        PD.retire(R_qk.all() + R_pT + [R_ost, R_sqt, R_sm, R_fl] + R_ynt)
        R_rstd["r"] = PD.tres(1)
        ysrc = [(yA[:, k], R_yA.r[k]) for k in range(4)] + [(yB[:, k], R_yb.r[k]) for k in range(4)]
        pend = None
        wouts = [wload(l, "wout%d" % i) for i in range(4)]
        for t in range(5):
            st = proj_mm(l, None, 2, ysrc, [t], o_qk + (t % 2) * 4 * L * 2, 0, ((6, 7)[t % 2],), "mix_post_g", preloaded=wouts, bgk=2)
            bgflush()
            BG.extend(proj_tail_steps(l, "mix_post_g", st))
        bgflush()
        PD.retire(R_yA.all() + R_yb.all() + R_v + [R_vones] + R_rstd["r"].all())

    BG = []

    def bgpop(k):
        for _ in range(k):
            if BG:
                BG.pop(0)()

    def bgflush():
        while BG:
            BG.pop(0)()

    def proj_mm(l, pieces, cpp, src, tiles, o_ytmp, src_off, statbanks, gname, preloaded=None, bgk=0):
        K = len(src)
        t_lo = TT[tiles[0]][0]
        YW = 416 * len(tiles)
        ytmp = A.f32(o_ytmp, 8 * YW).rearrange("p (c t) -> p c t", c=8)
        R_yt = [[PD.new() for _ in tiles] for _ in range(8)]
        stat = {t: statbanks[i] for i, t in enumerate(tiles)}
        pend_stat = []
        for c in range(8):
            if c % cpp == 0:
                wt, rw = preloaded[c // cpp] if preloaded is not None else wload(l, pieces[c // cpp])
                if cpp == 2:
                    wv = wt[:, 0:2048].rearrange("p (j k m) -> p j k m", j=2, k=8)
                else:
                    wv = wt[:, 0:K * 128].rearrange("p (j k m) -> p j k m", j=1, k=K)
            j = c % cpp
            for ti, t in enumerate(tiles):
                t0, n = TT[t]
                pb = nb((0, 1, 2, 3, 4, 5))
                for k in range(K):
                    sap, sres = src[k]
                    MM(psum[pb][:, 0:n], wv[:, j, k, :], sap[:, t0 - src_off:t0 - src_off + n], k == 0, k == K - 1, [rw, sres[t]], [PS[pb]])
                sq, rsq = t16()
                ACT(sq[:, 0:n], psum[pb][:, 0:n], AF.Square, [PS[pb]], [rsq])
                ACT(ytmp[:, c, t0 - t_lo:t0 - t_lo + n], psum[pb][:, 0:n], AF.Copy, [PS[pb], R_const], [R_yt[c][ti]], scale=vcol(l, gname, c))
                pend_stat.append((psum[stat[t]][:, 0:n], sq[:, 0:n], c == 0, c == 7, rsq, PS[stat[t]]))
                if len(pend_stat) > 2:
                    o_, s_, a_, b_, r_, p_ = pend_stat.pop(0)
                    MM(o_, ones16, s_, a_, b_, [r_, R_const], [p_])
                bgpop(bgk)
        for o_, s_, a_, b_, r_, p_ in pend_stat:
            MM(o_, ones16, s_, a_, b_, [r_, R_const], [p_])
        return (tiles, t_lo, ytmp, R_yt, stat)

    def proj_tail_steps(l, gname, st):
        tiles, t_lo, ytmp, R_yt, stat = st
        steps = []
        for ti, t in enumerate(tiles):
            t0, n = TT[t]

            def s0(t=t, t0=t0, n=n):
                RSQRT(rstd[:, t0:t0 + n], psum[stat[t]][:, 0:n], 1.0 / DM, epsc, [PS[stat[t]]], [R_rstd["r"].r[0][t]])
            steps.append(s0)
            for c in range(8):
                def s1(c=c, ti=ti, t=t, t0=t0, n=n):
                    yv = ytmp[:, c, t0 - t_lo:t0 - t_lo + n]
                    TTOP("dve", yv, yv, rstd[:, t0:t0 + n], ALU.mult, [R_yt[c][ti], R_rstd["r"].r[0][t]], [R_yt[c][ti]])
                    TTOP("dve", hT[:, c, t0:t0 + n], hT[:, c, t0:t0 + n], yv, ALU.add, [R_yt[c][ti], R_h.r[c][t]], [R_h.r[c][t]])
                steps.append(s1)
        steps.append(lambda: PD.retire([x for row in R_yt for x in row]))
        return steps

    def proj_tail(l, gname, st):
        for f in proj_tail_steps(l, gname, st):
            f()

    def ffn(s, l):
        last_tail = []
        R_rstd["r"] = PD.tres(1)
        o = o_P
        UW = 2 + 828
        up = A.bf16(o, 8 * UW).rearrange("p (c t) -> p c t", c=8)
        o += 8 * UW * 2
        o = (o + 63) // 64 * 64
        actT = A.bf16(o, 22 * 828).rearrange("p (c t) -> p c t", c=22)
        o += 22 * 828 * 2
        o = (o + 63) // 64 * 64
        NFA = 6
        facc = [A.f32(o + i * 416 * 4, 416) for i in range(NFA)]
        o += NFA * 416 * 4
        fsg = [A.bf16(o + i * 416 * 2, 416) for i in range(2)]
        o += 2 * 416 * 2
        o = (o + 63) // 64 * 64
        o_yt = o
        assert o_yt + 8 * 832 * 4 <= ARENA_BYTES, (o_yt, ARENA_BYTES)
        R_facc = [PD.new() for _ in range(NFA)]
        R_fsg = [PD.new() for _ in range(2)]
        R_halo = PD.new()
        frot = dict(a=0, s=0)

        class _RU:
            pass

        def new_pass_res(tiles):
            R_up = [[PD.new() for _ in tiles] for _ in range(8)]
            R_act = [[PD.new() for _ in tiles] for _ in range(22)]
            ru = _RU()
            ru.r = [{t: R_up[c][ti] for ti, t in enumerate(tiles)} for c in range(8)]
            return R_up, R_act, ru

        def do_prenorm(tiles, ru):
            t_lo = TT[tiles[0]][0]
            S.ph = "f_prenorm"
            prenorm(l, "ffn_pre_g", tiles, (lambda c, t0, n: up[:, c, 2 + t0 - t_lo:2 + t0 - t_lo + n]), ru)

        R_up, R_act, ru = new_pass_res(FFN_PASSES[0])
        do_prenorm(FFN_PASSES[0], ru)
        for pi, tiles in enumerate(FFN_PASSES):
            t_lo = TT[tiles[0]][0]
            S.ph = "f_up"
            for p in range(22):
                wt, rw = wload(l, "wup%d" % p)
                wv = wt[:, 0:2048].rearrange("p (j k m) -> p j k m", j=2, k=8)
                for ti, t in enumerate(tiles):
                    t0, n = TT[t]
                    hal = 0 if t == 0 else 2
                    c0 = 2 + t0 - t_lo - hal
                    rdu = []
                    for k in range(8):
                        rdu.append(R_up[k][ti])
                        if hal and ti > 0:
                            rdu.append(R_up[k][ti - 1])
                    if hal and ti == 0:
                        rdu.append(R_halo)
                    res = []
                    for half in range(2):
                        pb = nb((0, 1, 2, 3, 4, 5))
                        for k in range(8):
                            MM(psum[pb][:, 0:n + hal], wv[:, half, k, :], up[:, k, c0:c0 + n + hal], k == 0, k == 7, [rw] + rdu, [PS[pb]])
                        ai = frot["a"] % NFA
                        frot["a"] += 1
                        acc, racc = facc[ai], R_facc[ai]
                        f = half * 22 + p
                        ACT(acc[:, 0:n], psum[pb][:, hal:hal + n], AF.Copy, [PS[pb], R_const], [racc], scale=vcol(l, "ffn_cw", 2 * 44 + f))
                        if hal:
                            STT("dve", acc[:, 0:n], psum[pb][:, 1:1 + n], vcol(l, "ffn_cw", 1 * 44 + f), acc[:, 0:n], ALU.mult, ALU.add, [PS[pb], racc, R_const], [racc])
                            STT("dve", acc[:, 0:n], psum[pb][:, 0:n], vcol(l, "ffn_cw", 0 * 44 + f), acc[:, 0:n], ALU.mult, ALU.add, [PS[pb], racc, R_const], [racc])
                        else:
                            STT("dve", acc[:, 1:n], psum[pb][:, 0:n - 1], vcol(l, "ffn_cw", 1 * 44 + f), acc[:, 1:n], ALU.mult, ALU.add, [PS[pb], racc, R_const], [racc])
                            STT("dve", acc[:, 2:n], psum[pb][:, 0:n - 2], vcol(l, "ffn_cw", 0 * 44 + f), acc[:, 2:n], ALU.mult, ALU.add, [PS[pb], racc, R_const], [racc])
                        res.append((acc, racc))
                    si = frot["s"] % 2
                    frot["s"] += 1
                    ACT(fsg[si][:, 0:n], res[0][0][:, 0:n], AF.Silu, [res[0][1]], [R_fsg[si]])
                    TTOP("dve", actT[:, p, t0 - t_lo:t0 - t_lo + n], fsg[si][:, 0:n], res[1][0][:, 0:n], ALU.mult, [R_fsg[si], res[1][1]], [R_act[p][ti]])
            if pi + 1 < len(FFN_PASSES):
                width = sum(TT[t][1] for t in tiles)
                rd_all = [R_up[c][len(tiles) - 1] for c in range(8)]
                CP("dve", up[:, :, 0:2], up[:, :, width:width + 2], rd_all, [R_halo])
            bgflush()
            S.ph = "f_down"
            src = [(actT[:, k], {t: R_act[k][ti] for ti, t in enumerate(tiles)}) for k in range(22)]
            st = proj_mm(l, ["wdn%d" % i for i in range(8)], 1, src, tiles, o_yt, t_lo, (6, 7), "ffn_post_g")
            PD.retire([x for row in R_up for x in row] + [x for row in R_act for x in row])
            if pi + 1 < len(FFN_PASSES):
                R_up, R_act, ru = new_pass_res(FFN_PASSES[pi + 1])
                do_prenorm(FFN_PASSES[pi + 1], ru)
            S.ph = "f_tail"
            if pi + 1 < len(FFN_PASSES):
                proj_tail(l, "ffn_post_g", st)
            else:
                last_tail.extend(proj_tail_steps(l, "ffn_post_g", st))
        PD.retire(R_facc + R_fsg + [R_halo] + [x for t_, x in enumerate(R_rstd["r"].r[0]) if t_ != 4])
        return last_tail, R_rstd["r"].r[0][4]

    def load_tiles(s, tiles):
        for t in tiles:
            t0, n = TT[t]
            lo, hi = max(t0, NMETA), t0 + n
            S.dmaop("xin", hT[:, :, lo:hi], xT[s, :, :, lo - NMETA:hi - NMETA], writes=[R_h.r[c][t] for c in range(8)], nsem=4)
            if t == 0:
                S.dmaop("xin", hT[:, :, 0:NMETA], metaT, writes=[R_h.r[c][0] for c in range(8)], nsem=4)

    hook = None
    rstd4 = None
    load_tiles(0, range(5))
    for s in range(NSEQ):
        for l in range(NLAYER):
            mixer(s, l, hook, rstd4)
            hook, rstd4 = None, None
            if STOP in ("prenorm", "conv", "qkv", "attn", "mixer"):
                break
            tail_steps, rstd4 = ffn(s, l)
            last = (l == NLAYER - 1)
            if last:
                tap("h", hT, R_h.all()) if False else None
                store_seq(s, range(4))
                if s + 1 < NSEQ:
                    load_tiles(s + 1, range(4))

            def hook(tail_steps=tail_steps, last=last, s=s):
                for f in tail_steps:
                    f()
                if last:
                    store_seq(s, [4])
                    if s + 1 < NSEQ:
                        load_tiles(s + 1, [4])
        if STOP is not None:
            tap("h", hT, R_h.all())
            store_seq(s, range(5))
            hook = None
    if hook is not None:
        hook()
        if "h" in taps:
            tap("h", hT, R_h.all())

    S.finalize()
    esems = {}
    dsems = {}
    for e in Sched.COMPUTE:
        esems[e] = es.enter_context(nc.semaphore("s_" + e))
    for stname, st in S.streams.items():
        for i in range(st["nsem"]):
            dsems[(stname, i)] = es.enter_context(nc.semaphore("d_%s%d" % (stname, i)))
    block = es.enter_context(nc.Block())

    @block.tensor
    def _(eng):
        S.emit("pe", eng, esems, dsems)

    @block.scalar
    def _(eng):
        S.emit("act", eng, esems, dsems)

    @block.vector
    def _(eng):
        S.emit("dve", eng, esems, dsems)

    @block.gpsimd
    def _(eng):
        S.emit("pool", eng, esems, dsems)

    @block.sync
    def _(eng):
        S.emit("sp", eng, esems, dsems)
        for stname, st in S.streams.items():
            n = len(st["recs"])
            for i in range(st["nsem"]):
                k = len([1 for j in range(n) if j % st["nsem"] == i])
                if k:
                    eng.wait_ge(dsems[(stname, i)], 16 * k)
    es.close()
    stats = {e: len(v) for e, v in S.ins.items()}
    global _LAST_SCHED
    _LAST_SCHED = S
    return nc, stats


def make_in_maps(inputs, cfg):
    NSEQ = cfg["nseq"]
    x = np.asarray(inputs["x"], np.float32)
    f32 = lambda a: np.asarray(a, np.float32)
    wfull = host_weights(f32(inputs["w_in"]), f32(inputs["a_pw_w"]), f32(inputs["w_out"]),
                         f32(inputs["ffn_w_up"]), f32(inputs["ffn_w_down"]))
    vecs = host_vecs({k: f32(v) for k, v in inputs.items()})
    bfg = np.ascontiguousarray(np.broadcast_to(f32(inputs["b_forget"]).reshape(1, 16), (128, 16)))
    metaT = np.ascontiguousarray(f32(inputs["meta_tokens"]).reshape(NMETA, 8, 128).transpose(2, 1, 0))
    consts = host_consts()
    maps = []
    ncores = cfg.get("ncores", NCORES)
    for c in range(ncores):
        xs = x[c * NSEQ:(c + 1) * NSEQ]
        xTn = np.ascontiguousarray(xs.reshape(NSEQ, SEQ, 8, 128).transpose(0, 3, 2, 1))
        maps.append(dict(xT=xTn, metaT=metaT, wf=wfull, vecs=vecs, bfg=bfg, consts=consts))
    return maps


def kernel(**inputs):
    cfg = dict(CFG)
    nc, _ = build(cfg)
    maps = make_in_maps(inputs, cfg)
    res = run_bass_kernel_spmd(nc, maps, core_ids=list(range(NCORES)))
    outs = []
    for c in range(NCORES):
        oT = res.results[c]["outT"]
        outs.append(np.ascontiguousarray(oT.transpose(0, 3, 2, 1)).reshape(cfg["nseq"], SEQ, DM))
    return np.concatenate(outs, axis=0).astype(np.float32)
```

```python
import numpy as np
import concourse.bass as bass
import concourse.mybir as mybir
from concourse.bass_utils import run_bass_kernel_spmd

F32 = mybir.dt.float32
BF16 = mybir.dt.bfloat16
AF = mybir.ActivationFunctionType
ALU = mybir.AluOpType
AX = mybir.AxisListType

NCORES = 8
SEQ = 2048
NMETA = 16
L = SEQ + NMETA
DM = 1024
DFF = 2816
DIN = 2824
TT = [(0, 416), (416, 412), (828, 412), (1240, 412), (1652, 412)]
BLK = [(128 * b, 128) for b in range(16)] + [(2048, 16)]
FFN_PASSES = [[0, 1], [2, 3], [4]]
RMS_EPS = 1e-6
LN_EPS = 1e-5

ZC = dict(q=0, k=512, v=1024, f=1536, aval=1544, agate=1800, scb=2056, scc=2312, scx=2568)
WIN_FM = [("aval", 0), ("agate", 0), ("aval", 1), ("agate", 1),
          ("scc", 0), ("scx", 0), ("scb", 0), ("scc", 1), ("scx", 1), ("scb", 1),
          ("q", 0), ("q", 1), ("q", 2), ("q", 3), ("k", 0), ("k", 1), ("k", 2), ("k", 3)]
PIECES = []
for i in range(9):
    PIECES.append(("win%d" % i, 2048))
PIECES.append(("pw", 512))
PIECES.append(("vfa", 2048))
PIECES.append(("vfb", 2112))
for i in range(4):
    PIECES.append(("wout%d" % i, 2048))
for i in range(22):
    PIECES.append(("wup%d" % i, 2048))
for i in range(8):
    PIECES.append(("wdn%d" % i, 2816))
POFF = {}
_o = 0
for _n, _s in PIECES:
    POFF[_n] = (_o, _s)
    _o += _s
WL = _o
assert WL == 98880
SLOT = 2816
NSLOT = 4
PRE_BLK = 4120

VEC = {}
_v = 0
for _n, _s in [("mix_pre_g", 8), ("mix_post_g", 8), ("ffn_pre_g", 8), ("ffn_post_g", 8), ("head_g", 8),
               ("dw_w", 62), ("dw_b", 2), ("ln_g", 2), ("ln_b", 2), ("sc_w", 6), ("ffn_cw", 132)]:
    VEC[_n] = _v
    _v += _s
NVEC = _v

C_ID, C_ONES, C_TRI, C_NEG, C_BO = 0, 128, 256, 384, 512
NCONST = 640


def host_consts():
    c = np.zeros((128, NCONST), np.float32)
    i = np.arange(128)
    c[:, C_ID:C_ID + 128] = np.eye(128, dtype=np.float32)
    c[:, C_ONES:C_ONES + 128] = 1.0
    c[:, C_TRI:C_TRI + 128] = (i[:, None] <= i[None, :]).astype(np.float32)
    c[:, C_NEG:C_NEG + 128] = np.where(i[:, None] > i[None, :], -30000.0, 0.0).astype(np.float32)
    c[:, C_BO:C_BO + 128] = ((i[:, None] // 64) == (i[None, :] // 64)).astype(np.float32)
    return c


def host_weights(w_in, a_pw_w, w_out, ffn_w_up, ffn_w_down):
    out = np.empty((128, 2 * WL), np.float32)
    for l in range(2):
        base = l * WL
        wi = w_in[l].reshape(8, 128, DIN)
        for i in range(9):
            o, s = POFF["win%d" % i]
            blk = np.empty((128, 2, 8, 128), np.float32)
            for j in range(2):
                nm, cc = WIN_FM[2 * i + j]
                c0 = ZC[nm] + 128 * cc
                blk[:, j] = wi[:, :, c0:c0 + 128].transpose(1, 0, 2)
            out[:, base + o:base + o + s] = blk.reshape(128, -1)
        o, s = POFF["pw"]
        out[:, base + o:base + o + s] = a_pw_w[l].reshape(2, 128, 256).transpose(1, 0, 2).reshape(128, -1)
        o, s = POFF["vfa"]
        out[:, base + o:base + o + s] = wi[:, :, 1024:1280].transpose(1, 0, 2).reshape(128, -1)
        o, s = POFF["vfb"]
        out[:, base + o:base + o + s] = wi[:, :, 1280:1544].transpose(1, 0, 2).reshape(128, -1)
        wo = w_out[l].reshape(8, 128, DM)
        for i in range(4):
            o, s = POFF["wout%d" % i]
            blk = np.empty((128, 2, 8, 128), np.float32)
            for j in range(2):
                c0 = 128 * (2 * i + j)
                blk[:, j] = wo[:, :, c0:c0 + 128].transpose(1, 0, 2)
            out[:, base + o:base + o + s] = blk.reshape(128, -1)
        wu = ffn_w_up[l].reshape(8, 128, 2 * DFF)
        for i in range(22):
            o, s = POFF["wup%d" % i]
            blk = np.empty((128, 2, 8, 128), np.float32)
            blk[:, 0] = wu[:, :, 128 * i:128 * i + 128].transpose(1, 0, 2)
            blk[:, 1] = wu[:, :, DFF + 128 * i:DFF + 128 * i + 128].transpose(1, 0, 2)
            out[:, base + o:base + o + s] = blk.reshape(128, -1)
        wd = ffn_w_down[l].reshape(22, 128, DM)
        for i in range(8):
            o, s = POFF["wdn%d" % i]
            out[:, base + o:base + o + s] = wd[:, :, 128 * i:128 * i + 128].transpose(1, 0, 2).reshape(128, -1)
    return out


def host_vecs(inp):
    v = np.zeros((128, 2 * NVEC), np.float32)

    def cols(a):
        return np.ascontiguousarray(a.reshape(-1, 128).T)
    for l in range(2):
        b = l * NVEC
        for nm in ["mix_pre_g", "mix_post_g", "ffn_pre_g", "ffn_post_g", "head_g"]:
            v[:, b + VEC[nm]:b + VEC[nm] + 8] = cols(inp[nm][l])
        dw = inp["a_dw_w"][l]
        v[:, b + VEC["dw_w"]:b + VEC["dw_w"] + 62] = dw.reshape(31, 2, 128).transpose(2, 0, 1).reshape(128, 62)
        v[:, b + VEC["dw_b"]:b + VEC["dw_b"] + 2] = cols(inp["a_dw_b"][l])
        v[:, b + VEC["ln_g"]:b + VEC["ln_g"] + 2] = cols(inp["a_ln_g"][l])
        v[:, b + VEC["ln_b"]:b + VEC["ln_b"] + 2] = cols(inp["a_ln_b"][l])
        sc = inp["sc_conv_w"][l]
        v[:, b + VEC["sc_w"]:b + VEC["sc_w"] + 6] = sc.reshape(3, 2, 128).transpose(2, 0, 1).reshape(128, 6)
        fc = inp["ffn_conv_w"][l]
        v[:, b + VEC["ffn_cw"]:b + VEC["ffn_cw"] + 132] = fc.reshape(3, 44, 128).transpose(2, 0, 1).reshape(128, 132)
    return v


class Rec:
    __slots__ = ("eng", "idx", "fn", "deps", "needs_inc", "waits", "clock", "sem", "semval", "cnt", "is_dma", "ph")


class Res:
    __slots__ = ("ws", "rs")

    def __init__(self):
        self.ws = {}
        self.rs = {}


class TRes:
    def __init__(self, nch, tiles=TT):
        self.tiles = tiles
        self.r = [[Res() for _ in tiles] for _ in range(nch)]

    def get(self, c, lo, hi):
        return [self.r[c][t] for t, (s, n) in enumerate(self.tiles) if s < hi and lo < s + n]

    def all(self):
        return [x for row in self.r for x in row]


class Sched:
    COMPUTE = ("pe", "act", "dve", "pool")

    def __init__(self, nc):
        self.nc = nc
        self.ins = {e: [] for e in ("pe", "act", "dve", "pool", "sp")}
        self.order = []
        self.streams = {}
        self.ph = ""

    def add(self, eng, fn, reads=(), writes=()):
        r = Rec()
        r.eng = eng
        r.fn = fn
        r.is_dma = False
        r.needs_inc = False
        r.sem = None
        r.ph = self.ph
        lst = self.ins[eng]
        r.idx = len(lst) + 1
        deps = set()
        key = eng
        for x in reads:
            deps.update(x.ws.values())
        for x in writes:
            deps.update(x.ws.values())
            deps.update(x.rs.values())
        for x in reads:
            x.rs[key] = r
        for x in writes:
            x.ws = {key: r}
            x.rs = {}
        deps.discard(r)
        r.deps = deps
        lst.append(r)
        self.order.append(r)
        return r

    def dma(self, stream, fn, reads=(), writes=(), nsem=4):
        st = self.streams.setdefault(stream, dict(recs=[], nsem=nsem, sems=None))
        r = Rec()
        r.eng = "sp"
        r.fn = fn
        r.is_dma = True
        r.needs_inc = True
        r.ph = self.ph
        lst = self.ins["sp"]
        r.idx = len(lst) + 1
        n = len(st["recs"])
        r.sem = (stream, n % st["nsem"])
        r.semval = 16 * (n // st["nsem"] + 1)
        deps = set()
        if n >= st["nsem"]:
            deps.add(st["recs"][n - st["nsem"]])
        key = ("d", id(r))
        for x in reads:
            deps.update(x.ws.values())
        for x in writes:
            deps.update(x.ws.values())
            deps.update(x.rs.values())
        for x in reads:
            x.rs[key] = r
        for x in writes:
            x.ws = {key: r}
            x.rs = {}
        r.deps = deps
        st["recs"].append(r)
        lst.append(r)
        self.order.append(r)
        return r

    def op(self, eng, method, reads=(), writes=(), **kw):
        r = self.add(eng, (lambda e: getattr(e, method)(**kw)), reads, writes)
        r.ph = r.ph + ":" + method + ":" + str(kw.get("func", kw.get("op", kw.get("op0", ""))))
        return r

    def dmaop(self, stream, out, in_, reads=(), writes=(), nsem=4):
        return self.dma(stream, (lambda e: e.dma_start(out=out, in_=in_)), reads, writes, nsem)

    def finalize(self):
        run = {e: {} for e in self.ins}
        for r in self.order:
            clock = run[r.eng]
            waits = []
            for d in sorted(r.deps, key=lambda d: (d.eng, d.idx)):
                if d.is_dma:
                    k = ("d", d.sem)
                    if clock.get(k, 0) < d.semval:
                        waits.append(d)
                        for kk, vv in d.clock.items():
                            if clock.get(kk, 0) < vv:
                                clock[kk] = vv
                        clock[k] = d.semval
                elif d.eng == r.eng:
                    if r.eng in ("pe", "sp") or r.idx - d.idx > 2:
                        continue
                    d.needs_inc = True
                    waits.append(d)
                else:
                    if clock.get(d.eng, 0) < d.idx:
                        d.needs_inc = True
                        waits.append(d)
                        for kk, vv in d.clock.items():
                            if clock.get(kk, 0) < vv:
                                clock[kk] = vv
            r.waits = waits
            if r.is_dma:
                r.clock = dict(clock)
            else:
                clock[r.eng] = r.idx
                r.clock = dict(clock)
        for e in self.COMPUTE:
            c = 0
            for r in self.ins[e]:
                if r.needs_inc:
                    c += 1
                r.cnt = c

    def emit(self, eng_name, eng, esems, dsems):
        for r in self.ins[eng_name]:
            for d in r.waits:
                if d.is_dma:
                    eng.wait_ge(dsems[d.sem], d.semval)
                else:
                    eng.wait_ge(esems[d.eng], d.cnt)
            ins = r.fn(eng)
            if r.is_dma:
                ins.then_inc(dsems[r.sem], 16)
            elif r.needs_inc:
                ins.then_inc(esems[r.eng], 1)


class Arena:
    def __init__(self, tensor, nbytes):
        self.t = tensor
        self.nbytes = nbytes
        self.off = 0

    def alloc(self, nbytes):
        nbytes = (nbytes + 63) // 64 * 64
        o = self.off
        self.off += nbytes
        assert self.off <= self.nbytes, (self.off, self.nbytes)
        return o

    def f32(self, off, n):
        return self.t[:, off // 4: off // 4 + n]

    def bf16(self, off, n):
        v = self.t[:, off // 4: off // 4 + (n + 1) // 2].bitcast(BF16)
        return v[:, 0:n]


CFG = dict(nseq=4, nlayer=2, stop=None, taps=())


class Pend:
    def __init__(self):
        self.p = {}

    def retire(self, res_list):
        for r in res_list:
            for dct in (r.ws, r.rs):
                for k, rec in dct.items():
                    o = self.p.get(k)
                    if o is None or o.idx < rec.idx:
                        self.p[k] = rec
            r.ws = {}
            r.rs = {}

    def new(self):
        r = Res()
        r.rs = dict(self.p)
        return r

    def tres(self, nch, tiles=TT):
        t = TRes.__new__(TRes)
        t.tiles = tiles
        t.r = [[self.new() for _ in tiles] for _ in range(nch)]
        return t


def build(cfg):
    NSEQ = cfg["nseq"]
    NLAYER = cfg["nlayer"]
    STOP = cfg["stop"]
    nc = bass.Bass("TRN2", target_bir_lowering=False, dynamic_dma_scratch_size=64)
    xT = nc.dram_tensor("xT", [NSEQ, 128, 8, SEQ], F32, kind="ExternalInput").ap()
    metaT = nc.dram_tensor("metaT", [128, 8, NMETA], F32, kind="ExternalInput").ap()
    wf = nc.dram_tensor("wf", [128, 2 * WL], F32, kind="ExternalInput").ap()
    vecs_d = nc.dram_tensor("vecs", [128, 2 * NVEC], F32, kind="ExternalInput").ap()
    bfg_d = nc.dram_tensor("bfg", [128, 16], F32, kind="ExternalInput").ap()
    consts_d = nc.dram_tensor("consts", [128, NCONST], F32, kind="ExternalInput").ap()
    outT = nc.dram_tensor("outT", [NSEQ, 128, 8, SEQ], F32, kind="ExternalOutput").ap()
    wb = nc.dram_tensor("wb", [128, 2 * WL], BF16, kind="Internal").ap()
    taps = {}
    for nm, shape, dt in cfg["taps"]:
        taps[nm] = nc.dram_tensor("tap_" + nm, list(shape), BF16 if dt == "bf16" else F32, kind="ExternalOutput").ap()

    S = Sched(nc)
    ARENA_BYTES = 229056
    from contextlib import ExitStack
    es = ExitStack()
    arena_t = es.enter_context(nc.sbuf_tensor("arena", [128, ARENA_BYTES // 4], F32))
    A = Arena(arena_t, ARENA_BYTES)
    psum = [es.enter_context(nc.psum_tensor("ps%d" % i, [128, 512], F32)) for i in range(8)]
    PS = [Res() for _ in range(8)]
    psrot = dict(i=0)

    def nb(banks=(0, 1, 2, 3, 4, 5)):
        b = banks[psrot["i"] % len(banks)]
        psrot["i"] += 1
        return b

    o_h = A.alloc(8 * L * 4)
    o_w = A.alloc(NSLOT * SLOT * 2)
    o_rstd = A.alloc(8320)
    o_vec = A.alloc(2 * NVEC * 4)
    o_bfg = A.alloc(16 * 4)
    o_c32 = A.alloc(NCONST * 4)
    o_c16 = A.alloc(NCONST * 2)
    NT32, NT16 = 5, 3
    o_t32 = A.alloc(NT32 * 416 * 4)
    o_t16 = A.alloc(NT16 * 416 * 2)
    o_eps = A.alloc(64)
    o_tr = A.alloc(2 * 416 * 4)
    o_P = A.off
    P_BYTES = ARENA_BYTES - o_P
    assert P_BYTES >= 100240, P_BYTES
    o_u = o_P
    o_qk = o_u + 8 * L * 2
    o_v = o_qk + 8 * L * 2
    o_yb = o_v + 17 * 520 * 2
    o_sp = o_yb + 4 * L * 2
    o_dg1 = o_sp
    o_qz = o_dg1 + 31 * 128 * 2
    assert o_qz + 2 * 1024 <= ARENA_BYTES, (o_qz, ARENA_BYTES)

    hT = A.f32(o_h, 8 * L).rearrange("p (c t) -> p c t", c=8)
    rstd = A.f32(o_rstd, L)
    vec = A.f32(o_vec, 2 * NVEC)
    bfg = A.f32(o_bfg, 16)
    c32 = A.f32(o_c32, NCONST)
    c16 = A.bf16(o_c16, NCONST)
    wslot = [A.bf16(o_w + i * SLOT * 2, SLOT) for i in range(NSLOT)]
    tmp32 = [A.f32(o_t32 + i * 416 * 4, 416) for i in range(NT32)]
    tmp16 = [A.bf16(o_t16 + i * 416 * 2, 416) for i in range(NT16)]
    epsv = A.f32(o_eps, 8)
    tmpr = [A.f32(o_tr + i * 416 * 4, 416) for i in range(2)]
    R_tmpr = [Res(), Res()]
    epsc, lnepsc, onec = epsv[:, 0:1], epsv[:, 1:2], epsv[:, 2:3]

    PD = Pend()
    R_h = TRes(8)
    R_const = Res()
    R_wb = Res()
    R_wslot = [Res() for _ in range(NSLOT)]
    R_tmp32 = [Res() for _ in range(NT32)]
    R_tmp16 = [Res() for _ in range(NT16)]
    rot = dict(t32=0, t16=0, w=0, tr=0)

    def vcol(l, name, j=0):
        c = l * NVEC + VEC[name] + j
        return vec[:, c:c + 1]

    def MM(out, lhsT, rhs, start, stop, reads, writes, **kw):
        S.op("pe", "matmul", reads, writes, out=out, lhsT=lhsT, rhs=rhs, start=start, stop=stop, **kw)

    def ACT(out, in_, func, reads, writes, bias=None, scale=None):
        kw = dict(out=out, in_=in_, func=func)
        if bias is not None:
            kw["bias"] = bias
        if scale is not None:
            kw["scale"] = scale
        S.op("act", "activation", reads, writes, **kw)

    def STT(eng, out, in0, scalar, in1, op0, op1, reads, writes):
        S.op(eng, "scalar_tensor_tensor", reads, writes, out=out, in0=in0, scalar=scalar, in1=in1, op0=op0, op1=op1)

    def TTOP(eng, out, in0, in1, op, reads, writes):
        S.op(eng, "tensor_tensor", reads, writes, out=out, in0=in0, in1=in1, op=op)

    def TS(eng, out, in0, s1, op0, reads, writes, s2=None, op1=None):
        kw = dict(out=out, in0=in0, scalar1=s1, scalar2=s2, op0=op0)
        if op1 is not None:
            kw["op1"] = op1
        S.op(eng, "tensor_scalar", reads, writes, **kw)

    def CP(eng, out, in_, reads, writes):
        if eng == "act":
            S.op("act", "activation", reads, writes, out=out, in_=in_, func=AF.Copy)
        else:
            S.op(eng, "tensor_copy", reads, writes, out=out, in_=in_)

    def RSQRT(out, in_, scale, eps_ap, reads, writes):
        np_ = in_.shape[0]
        tm, rtm = tmpr[rot["tr"] % 2], R_tmpr[rot["tr"] % 2]
        rot["tr"] += 1
        n_ = in_.shape[-1]
        tv = tm[0:np_, 0:n_]
        if eps_ap is None:
            ACT(tv, in_, AF.Ln, list(reads), [rtm], scale=scale)
        else:
            ACT(tv, in_, AF.Ln, list(reads) + [R_const], [rtm], bias=eps_ap, scale=scale)
        ACT(out, tv, AF.Exp, [rtm], writes, scale=-0.5)

    def RECIP(out, in_, reads, writes):
        S.op("dve", "reciprocal", reads, writes, out=out, in_=in_)

    def MEMSET(eng, ap, val, writes):
        S.op(eng, "memset", (), writes, ap=ap, constant=val)

    S.dmaop("misc", c32, consts_d, writes=[R_const])
    S.dmaop("misc", vec, vecs_d, writes=[R_const])
    S.dmaop("misc", bfg, bfg_d, writes=[R_const])
    CP("dve", c16, c32, [R_const], [R_const])
    MEMSET("dve", epsv[:, 0:1], RMS_EPS, [R_const])
    MEMSET("dve", epsv[:, 1:2], LN_EPS, [R_const])
    MEMSET("dve", epsv[:, 2:3], 1.0, [R_const])
    MEMSET("dve", epsv[:, 3:4], 0.0, [R_const])
    ident16 = c16[:, C_ID:C_ID + 128]
    ones16 = c16[:, C_ONES:C_ONES + 128]
    negtri16 = c16[:, C_NEG:C_NEG + 128]
    bo16 = c16[:, C_BO:C_BO + 128]
    ones32 = c32[:, C_ONES:C_ONES + 128]
    tri32 = c32[:, C_TRI:C_TRI + 128]

    dgd = nc.dram_tensor("dgd", [128, 2 * 2 * 3968], BF16, kind="Internal").ap()
    R_dgd = Res()
    dg1 = A.bf16(o_dg1, 31 * 128).rearrange("p (k m) -> p k m", k=31)
    R_dg1 = Res()
    qz = [A.bf16(o_qz + par * 1024, 512) for par in range(2)]
    R_qz = [None, None]
    dgst = [A.bf16(o_P + i * 7936, 3968).rearrange("p (k m) -> p k m", k=31) for i in range(2)]
    R_dgst = [PD.new(), PD.new()]
    for l_ in range(NLAYER):
        for cc_ in range(2):
            i_ = (l_ * 2 + cc_) % 2
            for k in range(31):
                TS("dve", dgst[i_][:, k, :], ident16, vcol(l_, "dw_w", 2 * k + cc_), ALU.mult, [R_const], [R_dgst[i_]])
            S.dmaop("dgo", dgd[:, (l_ * 2 + cc_) * 3968:(l_ * 2 + cc_ + 1) * 3968], dgst[i_].rearrange("p k m -> p (k m)"), reads=[R_dgst[i_]], writes=[R_dgd], nsem=2)
    PD.retire(R_dgst)
    st32 = [A.f32(o_P + i * PRE_BLK * 4, PRE_BLK) for i in range(2)]
    st16 = [A.bf16(o_P + 2 * PRE_BLK * 4 + i * PRE_BLK * 2, PRE_BLK) for i in range(2)]
    R_st32 = [PD.new(), PD.new()]
    R_st16 = [PD.new(), PD.new()]
    nblk_used = (NLAYER * WL + PRE_BLK - 1) // PRE_BLK
    for b in range(nblk_used):
        i = b % 2
        sl = slice(b * PRE_BLK, (b + 1) * PRE_BLK)
        S.dmaop("prein", st32[i], wf[:, sl], writes=[R_st32[i]], nsem=2)
        CP(("dve", "act")[b % 2], st16[i], st32[i], [R_st32[i]], [R_st16[i]])
        S.dmaop("preout", wb[:, sl], st16[i], reads=[R_st16[i]], writes=[R_wb], nsem=2)
    PD.retire(R_st32 + R_st16)

    def wload(l, name):
        o, s = POFF[name]
        i = rot["w"] % NSLOT
        rot["w"] += 1
        S.dmaop("w", wslot[i][:, 0:s], wb[:, l * WL + o: l * WL + o + s], reads=[R_wb], writes=[R_wslot[i]], nsem=NSLOT)
        return wslot[i], R_wslot[i]

    def t32():
        i = rot["t32"] % NT32
        rot["t32"] += 1
        return tmp32[i], R_tmp32[i]

    def t16():
        i = rot["t16"] % NT16
        rot["t16"] += 1
        return tmp16[i], R_tmp16[i]

    def tap(name, src_ap, reads):
        if name in taps:
            S.dmaop("tap", taps[name], src_ap, reads=reads, nsem=1)

    def load_seq(s):
        for t, (t0, n) in enumerate(TT):
            lo, hi = max(t0, NMETA), t0 + n
            S.dmaop("xin", hT[:, :, lo:hi], xT[s, :, :, lo - NMETA:hi - NMETA], writes=[R_h.r[c][t] for c in range(8)], nsem=4)
            if t == 0:
                S.dmaop("xin", hT[:, :, 0:NMETA], metaT, writes=[R_h.r[c][0] for c in range(8)], nsem=4)

    def store_seq(s, tiles):
        for t in tiles:
            t0, n = TT[t]
            lo, hi = max(t0, NMETA), t0 + n
            S.dmaop("xout", outT[s, :, :, lo - NMETA:hi - NMETA], hT[:, :, lo:hi], reads=[R_h.r[c][t] for c in range(8)], nsem=4)

    R_rstd = dict(r=None)

    def prenorm(l, gname, tiles, u_fn, Ru):
        for t in tiles:
            t0, n = TT[t]
            pb = nb()
            for c in range(8):
                sq, rsq = t16()
                ACT(sq[:, 0:n], hT[:, c, t0:t0 + n], AF.Square, [R_h.r[c][t]], [rsq])
                MM(psum[pb][:, 0:n], ones16, sq[:, 0:n], c == 0, c == 7, [rsq, R_const], [PS[pb]])
            RSQRT(rstd[:, t0:t0 + n], psum[pb][:, 0:n], 1.0 / DM, epsc, [PS[pb]], [R_rstd["r"].r[0][t]])
            for c in range(8):
                STT("dve", u_fn(c, t0, n), hT[:, c, t0:t0 + n], vcol(l, gname, c), rstd[:, t0:t0 + n], ALU.mult, ALU.mult,
                    [R_h.r[c][t], R_rstd["r"].r[0][t], R_const], [Ru.r[c][t]])

    def hn_a(l, ychunk, src, src_reads, n, dst, dst_writes):
        sq, rsq = t16()
        ACT(sq[:, 0:n], src, AF.Square, src_reads, [rsq])
        return (l, ychunk, src, src_reads, n, dst, dst_writes, sq, rsq)

    def hn_b(st):
        l, ychunk, src, src_reads, n, dst, dst_writes, sq, rsq = st
        pb = nb()
        MM(psum[pb][:, 0:n], bo16, sq[:, 0:n], True, True, [rsq, R_const], [PS[pb]])
        sd, rsd = t32()
        RSQRT(sd[:, 0:n], psum[pb][:, 0:n], 1.0 / 64, epsc, [PS[pb]], [rsd])
        STT("dve", dst, src, vcol(l, "head_g", ychunk), sd[:, 0:n], ALU.mult, ALU.mult, src_reads + [rsd, R_const], dst_writes)

    def mixer(s, l, pre_hook=None, rstd4=None):
        uT = A.bf16(o_u, 8 * L).rearrange("p (c t) -> p c t", c=8)
        R_u = PD.tres(8)
        R_rstd["r"] = PD.tres(1)
        yB = A.bf16(o_yb, 4 * L).rearrange("p (c t) -> p c t", c=4)
        R_yb = PD.tres(4)
        S.ph = "m_prenorm"
        if rstd4 is not None:
            R_rstd["r"].r[0][4] = rstd4
        prenorm(l, "mix_pre_g", range(4), (lambda c, t0, n: uT[:, c, t0:t0 + n]), R_u)
        if pre_hook is not None:
            S.ph = "f_tail"
            pre_hook()
            S.ph = "m_prenorm"
        prenorm(l, "mix_pre_g", [4], (lambda c, t0, n: uT[:, c, t0:t0 + n]), R_u)
        S.ph = "m_conv"
        if STOP == "prenorm":
            tap("u", uT, R_u.all())
            return
        oc = o_qk
        apad = A.bf16(oc, 2 * (L + 30)).rearrange("p (c t) -> p c t", c=2)
        oc += 2 * (L + 30) * 2
        oc = (oc + 63) // 64 * 64
        cv = A.f32(oc, 2 * L).rearrange("p (c t) -> p c t", c=2)
        oc += 2 * L * 4
        a2 = A.bf16(oc, 2 * L).rearrange("p (c t) -> p c t", c=2)
        oc += 2 * L * 2
        cxpad = A.f32(oc, L + 2)
        oc += (L + 2) * 4
        oc = (oc + 63) // 64 * 64
        dg = A.bf16(o_rstd, 31 * 128).rearrange("p (k m) -> p k m", k=31)
        PD.retire(R_rstd["r"].all())
        lnt = [A.f32(oc + i * 416 * 4, 416) for i in range(3)]
        oc += 3 * 416 * 4
        assert oc <= o_yb, (oc, o_yb)
        R_apad = PD.tres(2)
        R_apad0 = PD.new()
        R_cv = PD.tres(2)
        R_a2 = PD.tres(2)
        R_cx = PD.tres(1)
        R_cx0 = PD.new()
        R_dg = PD.new()
        R_lnt = [PD.new() for _ in range(3)]
        MEMSET("dve", apad[:, :, 0:30], 0.0, [R_apad0])
        MEMSET("dve", cxpad[:, 0:2], 0.0, [R_cx0])

        def fm_mm(wv, rw, j, t, pb, halo=0):
            t0, n = TT[t]
            for k in range(8):
                MM(psum[pb][:, 0:n], wv[:, j, k, :], uT[:, k, t0:t0 + n], k == 0, k == 7, [rw, R_u.r[k][t]], [PS[pb]])

        dgs = [dg, dg1]
        R_dg1 = PD.new()
        R_dgs = [R_dg, R_dg1]
        for cc_ in range(2):
            S.dmaop("dg", dgs[cc_].rearrange("p k m -> p (k m)"), dgd[:, (l * 2 + cc_) * 3968:(l * 2 + cc_ + 1) * 3968], reads=[R_dgd], writes=[R_dgs[cc_]], nsem=2)
        for cc in range(2):
            wt, rw = wload(l, "win%d" % cc)
            wv = wt[:, 0:2048].rearrange("p (j k m) -> p j k m", j=2, k=8)
            for t, (t0, n) in enumerate(TT):
                pv, pg = nb(), nb()
                fm_mm(wv, rw, 0, t, pv)
                fm_mm(wv, rw, 1, t, pg)
                sg, rsg = t32()
                ACT(sg[:, 0:n], psum[pg][:, 0:n], AF.Sigmoid, [PS[pg]], [rsg])
                TTOP("dve", apad[:, cc, 30 + t0:30 + t0 + n], psum[pv][:, 0:n], sg[:, 0:n], ALU.mult, [PS[pv], rsg], [R_apad.r[cc][t]])
            if cc == 0:
                pass
            for t, (t0, n) in enumerate(TT):
                pb = nb()
                rd = R_apad.get(cc, t0 - 30, t0 + n) + [R_dgs[cc], R_apad0]
                for k in range(31):
                    MM(psum[pb][:, 0:n], dgs[cc][:, k, :], apad[:, cc, t0 + k:t0 + k + n], k == 0, k == 30, rd, [PS[pb]])
                ACT(cv[:, cc, t0:t0 + n], psum[pb][:, 0:n], AF.Identity, [PS[pb], R_const], [R_cv.r[cc][t]], bias=vcol(l, "dw_b", cc))
        tap("cv", cv, R_cv.all())
        S.ph = "m_ln_pw"
        for t, (t0, n) in enumerate(TT):
            pm, pq = nb(), nb()
            for cc in range(2):
                MM(psum[pm][:, 0:n], ones32, cv[:, cc, t0:t0 + n], cc == 0, cc == 1, [R_cv.r[cc][t], R_const], [PS[pm]])
            for cc in range(2):
                sq, rsq = t32()
                ACT(sq[:, 0:n], cv[:, cc, t0:t0 + n], AF.Square, [R_cv.r[cc][t]], [rsq])
                MM(psum[pq][:, 0:n], ones32, sq[:, 0:n], cc == 0, cc == 1, [rsq, R_const], [PS[pq]])
            mu, m2, rs_ = lnt[0], lnt[1], lnt[2]
            ACT(mu[:, 0:n], psum[pm][:, 0:n], AF.Copy, [PS[pm]], [R_lnt[0]], scale=1.0 / 256)
            TTOP("dve", m2[:, 0:n], mu[:, 0:n], mu[:, 0:n], ALU.mult, [R_lnt[0]], [R_lnt[1]])
            STT("dve", m2[:, 0:n], psum[pq][:, 0:n], 1.0 / 256, m2[:, 0:n], ALU.mult, ALU.subtract, [PS[pq], R_lnt[1]], [R_lnt[1]])
            RSQRT(rs_[:, 0:n], m2[:, 0:n], 1.0, lnepsc, [R_lnt[1]], [R_lnt[2]])
            for cc in range(2):
                xc, rxc = t32()
                TTOP("dve", xc[:, 0:n], cv[:, cc, t0:t0 + n], mu[:, 0:n], ALU.subtract, [R_cv.r[cc][t], R_lnt[0]], [rxc])
                STT("dve", xc[:, 0:n], xc[:, 0:n], vcol(l, "ln_g", cc), rs_[:, 0:n], ALU.mult, ALU.mult, [rxc, R_lnt[2], R_const], [rxc])
                ACT(a2[:, cc, t0:t0 + n], xc[:, 0:n], AF.Silu, [rxc, R_const], [R_a2.r[cc][t]], bias=vcol(l, "ln_b", cc))
        wt, rw = wload(l, "pw")
        wv = wt[:, 0:512].rearrange("p (k m) -> p k m", k=2)
        pend = None
        for oc_ in range(2):
            for t, (t0, n) in enumerate(TT):
                pb = nb()
                for k in range(2):
                    MM(psum[pb][:, 0:n], wv[:, k, oc_ * 128:(oc_ + 1) * 128], a2[:, k, t0:t0 + n], k == 0, k == 1,
                       [rw, R_a2.r[k][t]], [PS[pb]])
                st = hn_a(l, 4 + oc_, psum[pb][:, 0:n], [PS[pb]], n, yB[:, oc_, t0:t0 + n], [R_yb.r[oc_][t]])
                if pend is not None:
                    hn_b(pend)
                pend = st
        hn_b(pend)
        S.ph = "m_sc"
        w2, rw2 = wload(l, "win2")
        w3, rw3 = wload(l, "win3")
        w4, rw4 = wload(l, "win4")
        v2 = w2[:, 0:2048].rearrange("p (j k m) -> p j k m", j=2, k=8)
        v3 = w3[:, 0:2048].rearrange("p (j k m) -> p j k m", j=2, k=8)
        v4 = w4[:, 0:2048].rearrange("p (j k m) -> p j k m", j=2, k=8)
        scw = [[(v2, rw2, 0), (v2, rw2, 1), (v3, rw3, 0)], [(v3, rw3, 1), (v4, rw4, 0), (v4, rw4, 1)]]
        pend = None
        for cc in range(2):
            for t, (t0, n) in enumerate(TT):
                pc, px, pbb = nb(), nb(), nb()
                fm_mm(scw[cc][0][0], scw[cc][0][1], scw[cc][0][2], t, pc)
                fm_mm(scw[cc][1][0], scw[cc][1][1], scw[cc][1][2], t, px)
                fm_mm(scw[cc][2][0], scw[cc][2][1], scw[cc][2][2], t, pbb)
                xs, rxs = t32()
                CP("act", xs[:, 0:n], psum[px][:, 0:n], [PS[px]], [rxs])
                TTOP("dve", cxpad[:, 2 + t0:2 + t0 + n], psum[pc][:, 0:n], xs[:, 0:n], ALU.mult, [PS[pc], rxs], [R_cx.r[0][t]])
                acc, racc = t32()
                rd = R_cx.get(0, t0 - 2, t0 + n) + [R_cx0, R_const]
                TS("dve", acc[:, 0:n], cxpad[:, t0:t0 + n], vcol(l, "sc_w", 0 * 2 + cc), ALU.mult, rd, [racc])
                STT("dve", acc[:, 0:n], cxpad[:, t0 + 1:t0 + 1 + n], vcol(l, "sc_w", 1 * 2 + cc), acc[:, 0:n], ALU.mult, ALU.add, rd + [racc], [racc])
                STT("dve", acc[:, 0:n], cxpad[:, t0 + 2:t0 + 2 + n], vcol(l, "sc_w", 2 * 2 + cc), acc[:, 0:n], ALU.mult, ALU.add, rd + [racc], [racc])
                TTOP("dve", acc[:, 0:n], psum[pbb][:, 0:n], acc[:, 0:n], ALU.mult, [PS[pbb], racc], [racc])
                st = hn_a(l, 6 + cc, acc[:, 0:n], [racc], n, yB[:, 2 + cc, t0:t0 + n], [R_yb.r[2 + cc][t]])
                if pend is not None:
                    hn_b(pend)
                pend = st
        hn_b(pend)
        tap("yb", yB, R_yb.all())
        if STOP == "conv":
            return
        S.ph = "m_qkv"
        PD.retire(R_apad.all() + [R_apad0, R_cx0, R_dg, R_dg1] + R_cv.all() + R_a2.all() + R_cx.all() + R_lnt)
        for par in range(2):
            R_qz[par] = PD.new()
            MEMSET("dve", qz[par], 0.0, [R_qz[par]])
        qk = A.bf16(o_qk, 8 * L).rearrange("p (c t) -> p c t", c=8)
        R_qk = PD.tres(8)
        vaug = A.bf16(o_v, 17 * 520).rearrange("p (b h d) -> p b h d", b=17, h=8)
        R_v = [PD.new() for _ in range(17)]
        R_vones = PD.new()
        MEMSET("dve", vaug[:, :, :, 64:65], 1.0, [R_vones])
        for i in range(4):
            wt, rw = wload(l, "win%d" % (5 + i))
            wv = wt[:, 0:2048].rearrange("p (j k m) -> p j k m", j=2, k=8)
            for j in range(2):
                ch = 2 * i + j
                for t, (t0, n) in enumerate(TT):
                    pb = nb()
                    fm_mm(wv, rw, j, t, pb)
                    CP("act", qk[:, ch, t0:t0 + n], psum[pb][:, 0:n], [PS[pb]], [R_qk.r[ch][t]])
        wa, rwa = wload(l, "vfa")
        wbb, rwb = wload(l, "vfb")
        va = wa[:, 0:2048].rearrange("p (k m) -> p k m", k=8)
        vb = wbb[:, 0:2112].rearrange("p (k m) -> p k m", k=8)
        R_fl = PD.new()
        og = o_rstd
        fl = A.f32(og, 136).rearrange("p (b h) -> p b h", b=17)
        og += 136 * 4
        lf = A.f32(og, 136).rearrange("p (b h) -> p b h", b=17)
        og += 136 * 4
        Cx = A.f32(og, 18 * 8).rearrange("p (b h) -> p b h", b=18)
        og += 18 * 8 * 4
        Gt = A.f32(og, 136).rearrange("p (b h) -> p b h", b=17)
        og += 136 * 4
        biasv = A.f32(og, 5 * 136).rearrange("p (i b h) -> p i b h", i=5, b=17)
        og += 5 * 136 * 4
        assert og <= o_rstd + 8320
        MEMSET("dve", fl, 0.0, [R_fl])
        for b, (b0, nt) in enumerate(BLK):
            pa, pbk = nb(), nb()
            rdu = []
            for k in range(8):
                rdu += R_u.get(k, b0, b0 + nt)
            for k in range(8):
                MM(psum[pa][0:nt, 0:256], uT[:, k, b0:b0 + nt], va[:, k, :], k == 0, k == 7, [rwa] + rdu, [PS[pa]])
            for k in range(8):
                MM(psum[pbk][0:nt, 0:264], uT[:, k, b0:b0 + nt], vb[:, k, :], k == 0, k == 7, [rwb] + rdu, [PS[pbk]])
            CP("act", vaug[0:nt, b, 0:4, 0:64], psum[pa][0:nt, 0:256].rearrange("p (h d) -> p h d", h=4), [PS[pa]], [R_v[b]])
            CP("dve", vaug[0:nt, b, 4:8, 0:64], psum[pbk][0:nt, 0:256].rearrange("p (h d) -> p h d", h=4), [PS[pbk]], [R_v[b]])
            TTOP("dve", fl[0:nt, b, :], psum[pbk][0:nt, 256:264], bfg[0:nt, l * 8:l * 8 + 8], ALU.add, [PS[pbk], R_const], [R_fl])
        flf = fl.rearrange("p b h -> p (b h)")
        lff = lf.rearrange("p b h -> p (b h)")
        ACT(lff, flf, AF.Exp, [R_fl], [R_fl], scale=-1.0)
        ACT(lff, lff, AF.Ln, [R_fl, R_const], [R_fl], bias=onec, scale=1.0)
        pc_ = nb()
        MM(psum[pc_][:, 0:136], tri32, lff, True, True, [R_fl, R_const], [PS[pc_]])
        MM(psum[pc_][:, 136:272], ones32, lff, True, True, [R_fl, R_const], [PS[pc_]])
        MEMSET("dve", Cx[:, 0, :], 0.0, [R_fl])
        for b in range(16):
            TTOP("dve", Cx[:, b + 1, :], Cx[:, b, :], psum[pc_][:, 136 + 8 * b:136 + 8 * b + 8], ALU.add, [PS[pc_], R_fl], [R_fl])
        TTOP("dve", Gt.rearrange("p b h -> p (b h)"), psum[pc_][:, 0:136], Cx[:, 0:17, :].rearrange("p b h -> p (b h)"), ALU.add,
             [PS[pc_], R_fl], [R_fl])
        for i in range(5):
            nbk = 4 * i + 4 if i < 4 else 17
            e_i = 4 * i + 4 if i < 4 else 16
            TTOP("dve", biasv[:, i, 0:nbk, :], Gt[:, 0:nbk, :], Cx[:, e_i:e_i + 1, :].broadcast_to([128, nbk, 8]), ALU.subtract, [R_fl], [R_fl])
        tap("gt", Gt.rearrange("p b h -> p (b h)"), [R_fl])
        if STOP == "qkv":
            tap("qk", qk, R_qk.all())
            tap("v", vaug.rearrange("p b h d -> p (b h d)"), R_v)
            return
        S.ph = "m_attn"
        PD.retire(R_u.all())
        oa = o_u
        yA = A.bf16(oa, 4 * L).rearrange("p (c t) -> p c t", c=4)
        oa += 4 * L * 2
        NPT = 4
        pT = [A.bf16(oa + i * 1024, 512) for i in range(NPT)]
        oa += NPT * 1024
        ost = A.f32(oa, 4 * 8 * 65).rearrange("p (r h d) -> p r h d", r=4, h=8)
        oa += 4 * 8 * 65 * 4
        ynt = [A.bf16(oa + i * 1024, 512) for i in range(4)]
        oa += 4096
        SSb, RDb, TTb, RRb, SCb = [A.f32(og + 128 * r_, 32).rearrange("p (r h) -> p r h", r=4) for r_ in range(5)]
        og += 640
        sqt = A.f32(og, 256).rearrange("p (r d) -> p r d", r=4)
        og += 1024
        assert og <= o_rstd + 8320
        assert oa <= o_qk, (oa, o_qk)
        R_yA = PD.tres(4)
        R_pT = [PD.new() for _ in range(NPT)]
        R_ost = PD.new()
        R_sqt = PD.new()
        R_ynt = [PD.new() for _ in range(4)]
        R_sm = PD.new()
        SB, OB, TB = (0, 1, 2, 6), (3, 4), 5
        NSB = len(SB)
        steps = []
        for i in range(5):
            q0, qn = (512 * i, 512) if i < 4 else (2048, 16)
            jmax = 4 * i + 3 if i < 4 else 16
            for h in range(8):
                for j in range(jmax + 1):
                    steps.append((i, h, j, q0, qn, jmax))
        LA = 3
        nsteps = len(steps)
        ynrot = dict(i=0)
        kzrot = [0, 0]

        def stage_q(i, h):
            q0, qn = (512 * i, 512) if i < 4 else (2048, 16)
            kc, po = h // 2, (h % 2) * 64
            CP("dve", qz[h % 2][po:po + 64, 0:qn], qk[po:po + 64, kc, q0:q0 + qn], R_qk.get(kc, q0, q0 + qn), [R_qz[h % 2]])

        def emit_S(n):
            i, h, j, q0, qn, jmax = steps[n]
            kc, po = h // 2, (h % 2) * 64
            b0, kb = BLK[j]
            lo = max(q0, b0)
            N = q0 + qn - lo
            diag = (b0 >= q0)
            sb = SB[n % NSB]
            rq = R_qk.get(kc, lo, q0 + qn)
            rk = R_qk.get(4 + kc, b0, b0 + kb)
            par = h % 2
            if j == 0:
                if (i, h) == (0, 0):
                    stage_q(0, 0)
                nxt = (i, h + 1) if h < 7 else ((i + 1, 0) if i < 4 else None)
                if nxt is not None:
                    stage_q(*nxt)
            MM(psum[sb][0:kb, 0:N], qk[:, 4 + kc, b0:b0 + kb], qz[par][:, lo - q0:lo - q0 + N], True, not diag, rk + [R_qz[par]], [PS[sb]])
            if diag:
                MM(psum[sb][0:kb, 0:kb], ident16[0:kb, 0:kb], negtri16[0:kb, 0:kb], False, True, [R_const], [PS[sb]])
            ACT(pT[n % NPT][0:kb, 0:N], psum[sb][0:kb, 0:N], AF.Exp, [PS[sb], R_fl], [R_pT[n % NPT]], bias=biasv[0:kb, i, j, h:h + 1], scale=0.125)

        def emit_PV(n):
            i, h, j, q0, qn, jmax = steps[n]
            b0, kb = BLK[j]
            lo = max(q0, b0)
            ob = OB[h % 2]
            if i < 4:
                for r in range(4):
                    q = 4 * i + r
                    if q < j:
                        continue
                    off = 128 * q - lo
                    MM(psum[ob][:, r * 65:(r + 1) * 65], pT[n % NPT][0:kb, off:off + 128], vaug[0:kb, j, h, :], (j == 0 and r == 0), j == q,
                       [R_pT[n % NPT], R_v[j], R_vones], [PS[ob]], skip_group_check=True)
            else:
                MM(psum[ob][0:16, 0:65], pT[n % NPT][0:kb, 0:16], vaug[0:kb, j, h, :], j == 0, j == 16, [R_pT[n % NPT], R_v[j], R_vones], [PS[ob]])
            if j == jmax:
                if i < 4:
                    CP("dve", ost[:, :, h, :], psum[ob][:, 0:260].rearrange("p (r d) -> p r d", r=4), [PS[ob]], [R_ost])
                    nt_, R_ = 128, 4
                else:
                    CP("dve", ost[0:16, 0, h, :], psum[ob][0:16, 0:65], [PS[ob]], [R_ost])
                    nt_, R_ = 16, 1
                Oh = ost[0:nt_, 0:R_, h, 0:64]
                TTOP("dve", sqt[0:nt_, 0:R_], Oh, Oh, ALU.mult, [R_ost], [R_sqt])
                S.op("dve", "tensor_reduce", [R_sqt], [R_sm], out=SSb[0:nt_, 0:R_, h], in_=sqt[0:nt_, 0:R_], axis=AX.X, op=ALU.add)
                if h == 7:
                    finish_tile(i)

        DQ = []
        tk = [0]

        def defer(delay, fn):
            DQ.append([tk[0] + delay, fn])

        def tick():
            tk[0] += 1
            for x in [x for x in DQ if x[0] <= tk[0]]:
                DQ.remove(x)
                x[1]()

        def flushq():
            while DQ:
                DQ.pop(0)[1]()

        PS_T = [Res(), Res()]
        PS_T[0].rs = dict(PS[TB].rs)
        PS_T[0].ws = dict(PS[TB].ws)
        PS_T[1].rs = dict(PS[TB].rs)
        PS_T[1].ws = dict(PS[TB].ws)
        pst = psum[TB][:, 0:512].bitcast(BF16)

        def finish_tile(i):
            blocks = [4 * i + r for r in range(4)] if i < 4 else [16]

            nt_, R_ = (128, 4) if i < 4 else (16, 1)
            SS, RD, T_, RR, SC = [b[0:nt_, 0:R_] for b in (SSb, RDb, TTb, RRb, SCb)]

            def stA():
                RECIP(RD, ost[0:nt_, 0:R_, :, 64], [R_ost], [R_sm])
                TTOP("dve", T_, SS, RD, ALU.mult, [R_sm], [R_sm])
                TTOP("dve", T_, T_, RD, ALU.mult, [R_sm], [R_sm])
                TS("dve", T_, T_, 1.0 / 64, ALU.mult, [R_sm], [R_sm], s2=RMS_EPS, op1=ALU.add)

            def stB():
                tm, rtm = tmpr[rot["tr"] % 2], R_tmpr[rot["tr"] % 2]
                rot["tr"] += 1
                tv = tm[0:nt_, 0:R_ * 8].rearrange("p (r h) -> p r h", r=R_)
                ACT(tv, T_, AF.Ln, [R_sm], [rtm])
                ACT(RR, tv, AF.Exp, [rtm], [R_sm], scale=-0.5)

            def stC():
                TTOP("dve", SC, RR, RD, ALU.mult, [R_sm], [R_sm])
                for r, q in enumerate(blocks):
                    b0, nt = BLK[q]
                    TTOP("dve", ynt[r][0:nt].rearrange("p (h d) -> p h d", h=8), ost[0:nt, r, :, 0:64], SCb[0:nt, r, :].unsqueeze(2).broadcast_to([nt, 8, 64]),
                         ALU.mult, [R_ost, R_sm], [R_ynt[r]])

            def stD(r, q):
                b0, nt = BLK[q]
                hb = r % 2
                for c in range(4):
                    S.op("pe", "transpose", [R_ynt[r], R_const], [PS_T[hb]], out=pst[:, hb * 512 + c * 128:hb * 512 + c * 128 + nt],
                         in_=ynt[r][0:nt, c * 128:(c + 1) * 128], identity=ident16[0:nt, 0:nt])

            def stE(r, q):
                b0, nt = BLK[q]
                hb = r % 2
                for c in range(4):
                    ACT(yA[:, c, b0:b0 + nt], pst[:, hb * 512 + c * 128:hb * 512 + c * 128 + nt], AF.Copy, [PS_T[hb], R_const], R_yA.get(c, b0, b0 + nt),
                        scale=vcol(l, "head_g", c))

            stA()
            defer(3, stB)
            defer(6, stC)
            for r, q in enumerate(blocks):
                defer(13 + 2 * r, (lambda r=r, q=q: stD(r, q)))
                defer(16 + 2 * r, (lambda r=r, q=q: stE(r, q)))

        for n in range(nsteps + LA):
            if n < nsteps:
                emit_S(n)
            if n - LA >= 0:
                emit_PV(n - LA)
            tick()
        flushq()
        for hb in range(2):
            for dct in (PS_T[hb].ws, PS_T[hb].rs):
                for k_, rec_ in dct.items():
                    o_ = PS[TB].rs.get(k_)
                    if o_ is None or o_.idx < rec_.idx:
                        PS[TB].rs[k_] = rec_
        tap("ya", yA, R_yA.all())
        if STOP == "attn":
            return
        S.ph = "m_wout"
        PD.retire(R_qk.all() + R_pT + [R_ost, R_sqt, R_sm, R_fl] + R_ynt)
        R_rstd["r"] = PD.tres(1)
        ysrc = [(yA[:, k], R_yA.r[k]) for k in range(4)] + [(yB[:, k], R_yb.r[k]) for k in range(4)]
        pend = None
        wouts = [wload(l, "wout%d" % i) for i in range(4)]
        for t in range(5):
            st = proj_mm(l, None, 2, ysrc, [t], o_qk + (t % 2) * 4 * L * 2, 0, ((6, 7)[t % 2],), "mix_post_g", preloaded=wouts, bgk=2)
            bgflush()
            BG.extend(proj_tail_steps(l, "mix_post_g", st))
        bgflush()
        PD.retire(R_yA.all() + R_yb.all() + R_v + [R_vones] + R_rstd["r"].all() + R_qz)

    BG = []

    def bgpop(k):
        for _ in range(k):
            if BG:
                BG.pop(0)()

    def bgflush():
        while BG:
            BG.pop(0)()

    def proj_mm(l, pieces, cpp, src, tiles, o_ytmp, src_off, statbanks, gname, preloaded=None, bgk=0):
        K = len(src)
        t_lo = TT[tiles[0]][0]
        YW = 416 * len(tiles)
        ytmp = A.f32(o_ytmp, 8 * YW).rearrange("p (c t) -> p c t", c=8)
        R_yt = [[PD.new() for _ in tiles] for _ in range(8)]
        stat = {t: statbanks[i] for i, t in enumerate(tiles)}
        pend_stat = []
        for c in range(8):
            if c % cpp == 0:
                wt, rw = preloaded[c // cpp] if preloaded is not None else wload(l, pieces[c // cpp])
                if cpp == 2:
                    wv = wt[:, 0:2048].rearrange("p (j k m) -> p j k m", j=2, k=8)
                else:
                    wv = wt[:, 0:K * 128].rearrange("p (j k m) -> p j k m", j=1, k=K)
            j = c % cpp
            for ti, t in enumerate(tiles):
                t0, n = TT[t]
                pb = nb((0, 1, 2, 3, 4, 5))
                for k in range(K):
                    sap, sres = src[k]
                    MM(psum[pb][:, 0:n], wv[:, j, k, :], sap[:, t0 - src_off:t0 - src_off + n], k == 0, k == K - 1, [rw, sres[t]], [PS[pb]])
                sq, rsq = t16()
                ACT(sq[:, 0:n], psum[pb][:, 0:n], AF.Square, [PS[pb]], [rsq])
                ACT(ytmp[:, c, t0 - t_lo:t0 - t_lo + n], psum[pb][:, 0:n], AF.Copy, [PS[pb], R_const], [R_yt[c][ti]], scale=vcol(l, gname, c))
                pend_stat.append((psum[stat[t]][:, 0:n], sq[:, 0:n], c == 0, c == 7, rsq, PS[stat[t]]))
                if len(pend_stat) > 2:
                    o_, s_, a_, b_, r_, p_ = pend_stat.pop(0)
                    MM(o_, ones16, s_, a_, b_, [r_, R_const], [p_])
                bgpop(bgk)
        for o_, s_, a_, b_, r_, p_ in pend_stat:
            MM(o_, ones16, s_, a_, b_, [r_, R_const], [p_])
        return (tiles, t_lo, ytmp, R_yt, stat)

    def proj_tail_steps(l, gname, st):
        tiles, t_lo, ytmp, R_yt, stat = st
        steps = []
        for ti, t in enumerate(tiles):
            t0, n = TT[t]

            def s0(t=t, t0=t0, n=n):
                RSQRT(rstd[:, t0:t0 + n], psum[stat[t]][:, 0:n], 1.0 / DM, epsc, [PS[stat[t]]], [R_rstd["r"].r[0][t]])
            steps.append(s0)
        for ti, t in enumerate(tiles):
            t0, n = TT[t]
            for c in range(8):
                def s1(c=c, ti=ti, t=t, t0=t0, n=n):
                    yv = ytmp[:, c, t0 - t_lo:t0 - t_lo + n]
                    TTOP("dve", yv, yv, rstd[:, t0:t0 + n], ALU.mult, [R_yt[c][ti], R_rstd["r"].r[0][t]], [R_yt[c][ti]])
                    TTOP("dve", hT[:, c, t0:t0 + n], hT[:, c, t0:t0 + n], yv, ALU.add, [R_yt[c][ti], R_h.r[c][t]], [R_h.r[c][t]])
                steps.append(s1)
        steps.append(lambda: PD.retire([x for row in R_yt for x in row]))
        return steps

    def proj_tail(l, gname, st):
        for f in proj_tail_steps(l, gname, st):
            f()

    def ffn(s, l):
        last_tail = []
        R_rstd["r"] = PD.tres(1)
        o = o_P
        UW = 2 + 828
        up = A.bf16(o, 8 * UW).rearrange("p (c t) -> p c t", c=8)
        o += 8 * UW * 2
        o = (o + 63) // 64 * 64
        actT = A.bf16(o, 22 * 828).rearrange("p (c t) -> p c t", c=22)
        o += 22 * 828 * 2
        o = (o + 63) // 64 * 64
        NFA = 4
        facc = [A.f32(o + i * 416 * 4, 416) for i in range(NFA)]
        o += NFA * 416 * 4
        fsg = [A.bf16(o + i * 416 * 2, 416) for i in range(2)]
        o += 2 * 416 * 2
        o = (o + 63) // 64 * 64
        o_yt = o
        assert o_yt + 2 * 8 * 832 * 4 <= ARENA_BYTES, (o_yt, ARENA_BYTES)
        R_facc = [PD.new() for _ in range(NFA)]
        R_fsg = [PD.new() for _ in range(2)]
        R_halo = PD.new()
        frot = dict(a=0, s=0)

        class _RU:
            pass

        def new_pass_res(tiles):
            R_up = [[PD.new() for _ in tiles] for _ in range(8)]
            R_act = [[PD.new() for _ in tiles] for _ in range(22)]
            ru = _RU()
            ru.r = [{t: R_up[c][ti] for ti, t in enumerate(tiles)} for c in range(8)]
            return R_up, R_act, ru

        def do_prenorm(tiles, ru):
            t_lo = TT[tiles[0]][0]
            S.ph = "f_prenorm"
            prenorm(l, "ffn_pre_g", tiles, (lambda c, t0, n: up[:, c, 2 + t0 - t_lo:2 + t0 - t_lo + n]), ru)

        R_up, R_act, ru = new_pass_res(FFN_PASSES[0])
        do_prenorm(FFN_PASSES[0], ru)
        for pi, tiles in enumerate(FFN_PASSES):
            t_lo = TT[tiles[0]][0]
            S.ph = "f_up"
            for p in range(22):
                wt, rw = wload(l, "wup%d" % p)
                wv = wt[:, 0:2048].rearrange("p (j k m) -> p j k m", j=2, k=8)
                for ti, t in enumerate(tiles):
                    t0, n = TT[t]
                    hal = 0 if t == 0 else 2
                    c0 = 2 + t0 - t_lo - hal
                    rdu = []
                    for k in range(8):
                        rdu.append(R_up[k][ti])
                        if hal and ti > 0:
                            rdu.append(R_up[k][ti - 1])
                    if hal and ti == 0:
                        rdu.append(R_halo)
                    res = []
                    for half in range(2):
                        pb = nb((0, 1, 2, 3, 4, 5))
                        for k in range(8):
                            MM(psum[pb][:, 0:n + hal], wv[:, half, k, :], up[:, k, c0:c0 + n + hal], k == 0, k == 7, [rw] + rdu, [PS[pb]])
                        ai = frot["a"] % NFA
                        frot["a"] += 1
                        acc, racc = facc[ai], R_facc[ai]
                        f = half * 22 + p
                        ACT(acc[:, 0:n], psum[pb][:, hal:hal + n], AF.Copy, [PS[pb], R_const], [racc], scale=vcol(l, "ffn_cw", 2 * 44 + f))
                        if hal:
                            STT("dve", acc[:, 0:n], psum[pb][:, 1:1 + n], vcol(l, "ffn_cw", 1 * 44 + f), acc[:, 0:n], ALU.mult, ALU.add, [PS[pb], racc, R_const], [racc])
                            STT("dve", acc[:, 0:n], psum[pb][:, 0:n], vcol(l, "ffn_cw", 0 * 44 + f), acc[:, 0:n], ALU.mult, ALU.add, [PS[pb], racc, R_const], [racc])
                        else:
                            STT("dve", acc[:, 1:n], psum[pb][:, 0:n - 1], vcol(l, "ffn_cw", 1 * 44 + f), acc[:, 1:n], ALU.mult, ALU.add, [PS[pb], racc, R_const], [racc])
                            STT("dve", acc[:, 2:n], psum[pb][:, 0:n - 2], vcol(l, "ffn_cw", 0 * 44 + f), acc[:, 2:n], ALU.mult, ALU.add, [PS[pb], racc, R_const], [racc])
                        res.append((acc, racc))
                    si = frot["s"] % 2
                    frot["s"] += 1
                    ACT(fsg[si][:, 0:n], res[0][0][:, 0:n], AF.Silu, [res[0][1]], [R_fsg[si]])
                    TTOP("dve", actT[:, p, t0 - t_lo:t0 - t_lo + n], fsg[si][:, 0:n], res[1][0][:, 0:n], ALU.mult, [R_fsg[si], res[1][1]], [R_act[p][ti]])
            if pi + 1 < len(FFN_PASSES):
                width = sum(TT[t][1] for t in tiles)
                rd_all = [R_up[c][len(tiles) - 1] for c in range(8)]
                CP("dve", up[:, :, 0:2], up[:, :, width:width + 2], rd_all, [R_halo])
            S.ph = "f_down"
            src = [(actT[:, k], {t: R_act[k][ti] for ti, t in enumerate(tiles)}) for k in range(22)]
            st = proj_mm(l, ["wdn%d" % i for i in range(8)], 1, src, tiles, o_yt + (pi % 2) * 8 * 832 * 4, t_lo, (6, 7), "ffn_post_g", bgk=2)
            bgflush()
            PD.retire([x for row in R_up for x in row] + [x for row in R_act for x in row])
            if pi + 1 < len(FFN_PASSES):
                R_up, R_act, ru = new_pass_res(FFN_PASSES[pi + 1])
                do_prenorm(FFN_PASSES[pi + 1], ru)
            S.ph = "f_tail"
            if pi + 1 < len(FFN_PASSES):
                BG.extend(proj_tail_steps(l, "ffn_post_g", st))
            else:
                last_tail.extend(proj_tail_steps(l, "ffn_post_g", st))
        PD.retire(R_facc + R_fsg + [R_halo] + [x for t_, x in enumerate(R_rstd["r"].r[0]) if t_ != 4])
        return last_tail, R_rstd["r"].r[0][4]

    def load_tiles(s, tiles):
        for t in tiles:
            t0, n = TT[t]
            lo, hi = max(t0, NMETA), t0 + n
            S.dmaop("xin", hT[:, :, lo:hi], xT[s, :, :, lo - NMETA:hi - NMETA], writes=[R_h.r[c][t] for c in range(8)], nsem=4)
            if t == 0:
                S.dmaop("xin", hT[:, :, 0:NMETA], metaT, writes=[R_h.r[c][0] for c in range(8)], nsem=4)

    hook = None
    rstd4 = None
    load_tiles(0, range(5))
    for s in range(NSEQ):
        for l in range(NLAYER):
            mixer(s, l, hook, rstd4)
            hook, rstd4 = None, None
            if STOP in ("prenorm", "conv", "qkv", "attn", "mixer"):
                break
            tail_steps, rstd4 = ffn(s, l)
            last = (l == NLAYER - 1)
            if last:
                tap("h", hT, R_h.all()) if False else None
                store_seq(s, range(4))
                if s + 1 < NSEQ:
                    load_tiles(s + 1, range(4))

            def hook(tail_steps=tail_steps, last=last, s=s):
                for f in tail_steps:
                    f()
                if last:
                    store_seq(s, [4])
                    if s + 1 < NSEQ:
                        load_tiles(s + 1, [4])
        if STOP is not None:
            tap("h", hT, R_h.all())
            store_seq(s, range(5))
            hook = None
    if hook is not None:
        hook()
        if "h" in taps:
            tap("h", hT, R_h.all())

    S.finalize()
    esems = {}
    dsems = {}
    for e in Sched.COMPUTE:
        esems[e] = es.enter_context(nc.semaphore("s_" + e))
    for stname, st in S.streams.items():
        for i in range(st["nsem"]):
            dsems[(stname, i)] = es.enter_context(nc.semaphore("d_%s%d" % (stname, i)))
    block = es.enter_context(nc.Block())

    @block.tensor
    def _(eng):
        S.emit("pe", eng, esems, dsems)

    @block.scalar
    def _(eng):
        S.emit("act", eng, esems, dsems)

    @block.vector
    def _(eng):
        S.emit("dve", eng, esems, dsems)

    @block.gpsimd
    def _(eng):
        S.emit("pool", eng, esems, dsems)

    @block.sync
    def _(eng):
        S.emit("sp", eng, esems, dsems)
        for stname, st in S.streams.items():
            n = len(st["recs"])
            for i in range(st["nsem"]):
                k = len([1 for j in range(n) if j % st["nsem"] == i])
                if k:
                    eng.wait_ge(dsems[(stname, i)], 16 * k)
    es.close()
    stats = {e: len(v) for e, v in S.ins.items()}
    global _LAST_SCHED
    _LAST_SCHED = S
    return nc, stats


def make_in_maps(inputs, cfg):
    NSEQ = cfg["nseq"]
    x = np.asarray(inputs["x"], np.float32)
    f32 = lambda a: np.asarray(a, np.float32)
    wfull = host_weights(f32(inputs["w_in"]), f32(inputs["a_pw_w"]), f32(inputs["w_out"]),
                         f32(inputs["ffn_w_up"]), f32(inputs["ffn_w_down"]))
    vecs = host_vecs({k: f32(v) for k, v in inputs.items()})
    bfg = np.ascontiguousarray(np.broadcast_to(f32(inputs["b_forget"]).reshape(1, 16), (128, 16)))
    metaT = np.ascontiguousarray(f32(inputs["meta_tokens"]).reshape(NMETA, 8, 128).transpose(2, 1, 0))
    consts = host_consts()
    maps = []
    ncores = cfg.get("ncores", NCORES)
    for c in range(ncores):
        xs = x[c * NSEQ:(c + 1) * NSEQ]
        xTn = np.ascontiguousarray(xs.reshape(NSEQ, SEQ, 8, 128).transpose(0, 3, 2, 1))
        maps.append(dict(xT=xTn, metaT=metaT, wf=wfull, vecs=vecs, bfg=bfg, consts=consts))
    return maps


def kernel(**inputs):
    cfg = dict(CFG)
    nc, _ = build(cfg)
    maps = make_in_maps(inputs, cfg)
    res = run_bass_kernel_spmd(nc, maps, core_ids=list(range(NCORES)))
    outs = []
    for c in range(NCORES):
        oT = res.results[c]["outT"]
        outs.append(np.ascontiguousarray(oT.transpose(0, 3, 2, 1)).reshape(cfg["nseq"], SEQ, DM))
    return np.concatenate(outs, axis=0).astype(np.float32)
```

```python
import numpy as np
import concourse.bass as bass
import concourse.mybir as mybir
from concourse.bass_utils import run_bass_kernel_spmd

F32 = mybir.dt.float32
BF16 = mybir.dt.bfloat16
AF = mybir.ActivationFunctionType
ALU = mybir.AluOpType
AX = mybir.AxisListType

NCORES = 8
SEQ = 2048
NMETA = 16
L = SEQ + NMETA
DM = 1024
DFF = 2816
DIN = 2824
TT = [(0, 416), (416, 412), (828, 412), (1240, 412), (1652, 412)]
BLK = [(128 * b, 128) for b in range(16)] + [(2048, 16)]
FFN_PASSES = [[0, 1], [2, 3], [4]]
RMS_EPS = 1e-6
LN_EPS = 1e-5

ZC = dict(q=0, k=512, v=1024, f=1536, aval=1544, agate=1800, scb=2056, scc=2312, scx=2568)
WIN_FM = [("aval", 0), ("agate", 0), ("aval", 1), ("agate", 1),
          ("scc", 0), ("scx", 0), ("scb", 0), ("scc", 1), ("scx", 1), ("scb", 1),
          ("q", 0), ("q", 1), ("q", 2), ("q", 3), ("k", 0), ("k", 1), ("k", 2), ("k", 3)]
PIECES = []
for i in range(9):
    PIECES.append(("win%d" % i, 2048))
PIECES.append(("pw", 512))
PIECES.append(("vfa", 2048))
PIECES.append(("vfb", 2112))
for i in range(4):
    PIECES.append(("wout%d" % i, 2048))
for i in range(22):
    PIECES.append(("wup%d" % i, 2048))
for i in range(8):
    PIECES.append(("wdn%d" % i, 2816))
POFF = {}
_o = 0
for _n, _s in PIECES:
    POFF[_n] = (_o, _s)
    _o += _s
WL = _o
assert WL == 98880
SLOT = 2816
NSLOT = 4
PRE_BLK = 4120

VEC = {}
_v = 0
for _n, _s in [("mix_pre_g", 8), ("mix_post_g", 8), ("ffn_pre_g", 8), ("ffn_post_g", 8), ("head_g", 8),
               ("dw_w", 62), ("dw_b", 2), ("ln_g", 2), ("ln_b", 2), ("sc_w", 6), ("ffn_cw", 132)]:
    VEC[_n] = _v
    _v += _s
NVEC = _v

C_ID, C_ONES, C_TRI, C_NEG, C_BO = 0, 128, 256, 384, 512
NCONST = 640


def host_consts():
    c = np.zeros((128, NCONST), np.float32)
    i = np.arange(128)
    c[:, C_ID:C_ID + 128] = np.eye(128, dtype=np.float32)
    c[:, C_ONES:C_ONES + 128] = 1.0
    c[:, C_TRI:C_TRI + 128] = (i[:, None] <= i[None, :]).astype(np.float32)
    c[:, C_NEG:C_NEG + 128] = np.where(i[:, None] > i[None, :], -30000.0, 0.0).astype(np.float32)
    c[:, C_BO:C_BO + 128] = ((i[:, None] // 64) == (i[None, :] // 64)).astype(np.float32)
    return c


def host_weights(w_in, a_pw_w, w_out, ffn_w_up, ffn_w_down):
    out = np.empty((128, 2 * WL), np.float32)
    for l in range(2):
        base = l * WL
        wi = w_in[l].reshape(8, 128, DIN)
        for i in range(9):
            o, s = POFF["win%d" % i]
            blk = np.empty((128, 2, 8, 128), np.float32)
            for j in range(2):
                nm, cc = WIN_FM[2 * i + j]
                c0 = ZC[nm] + 128 * cc
                blk[:, j] = wi[:, :, c0:c0 + 128].transpose(1, 0, 2)
            out[:, base + o:base + o + s] = blk.reshape(128, -1)
        o, s = POFF["pw"]
        out[:, base + o:base + o + s] = a_pw_w[l].reshape(2, 128, 256).transpose(1, 0, 2).reshape(128, -1)
        o, s = POFF["vfa"]
        out[:, base + o:base + o + s] = wi[:, :, 1024:1280].transpose(1, 0, 2).reshape(128, -1)
        o, s = POFF["vfb"]
        out[:, base + o:base + o + s] = wi[:, :, 1280:1544].transpose(1, 0, 2).reshape(128, -1)
        wo = w_out[l].reshape(8, 128, DM)
        for i in range(4):
            o, s = POFF["wout%d" % i]
            blk = np.empty((128, 2, 8, 128), np.float32)
            for j in range(2):
                c0 = 128 * (2 * i + j)
                blk[:, j] = wo[:, :, c0:c0 + 128].transpose(1, 0, 2)
            out[:, base + o:base + o + s] = blk.reshape(128, -1)
        wu = ffn_w_up[l].reshape(8, 128, 2 * DFF)
        for i in range(22):
            o, s = POFF["wup%d" % i]
            blk = np.empty((128, 2, 8, 128), np.float32)
            blk[:, 0] = wu[:, :, 128 * i:128 * i + 128].transpose(1, 0, 2)
            blk[:, 1] = wu[:, :, DFF + 128 * i:DFF + 128 * i + 128].transpose(1, 0, 2)
            out[:, base + o:base + o + s] = blk.reshape(128, -1)
        wd = ffn_w_down[l].reshape(22, 128, DM)
        for i in range(8):
            o, s = POFF["wdn%d" % i]
            out[:, base + o:base + o + s] = wd[:, :, 128 * i:128 * i + 128].transpose(1, 0, 2).reshape(128, -1)
    return out


def host_vecs(inp):
    v = np.zeros((128, 2 * NVEC), np.float32)

    def cols(a):
        return np.ascontiguousarray(a.reshape(-1, 128).T)
    for l in range(2):
        b = l * NVEC
        for nm in ["mix_pre_g", "mix_post_g", "ffn_pre_g", "ffn_post_g", "head_g"]:
            v[:, b + VEC[nm]:b + VEC[nm] + 8] = cols(inp[nm][l])
        dw = inp["a_dw_w"][l]
        v[:, b + VEC["dw_w"]:b + VEC["dw_w"] + 62] = dw.reshape(31, 2, 128).transpose(2, 0, 1).reshape(128, 62)
        v[:, b + VEC["dw_b"]:b + VEC["dw_b"] + 2] = cols(inp["a_dw_b"][l])
        v[:, b + VEC["ln_g"]:b + VEC["ln_g"] + 2] = cols(inp["a_ln_g"][l])
        v[:, b + VEC["ln_b"]:b + VEC["ln_b"] + 2] = cols(inp["a_ln_b"][l])
        sc = inp["sc_conv_w"][l]
        v[:, b + VEC["sc_w"]:b + VEC["sc_w"] + 6] = sc.reshape(3, 2, 128).transpose(2, 0, 1).reshape(128, 6)
        fc = inp["ffn_conv_w"][l]
        v[:, b + VEC["ffn_cw"]:b + VEC["ffn_cw"] + 132] = fc.reshape(3, 44, 128).transpose(2, 0, 1).reshape(128, 132)
    return v


class Rec:
    __slots__ = ("eng", "idx", "fn", "deps", "needs_inc", "waits", "clock", "sem", "semval", "cnt", "is_dma", "ph")


class Res:
    __slots__ = ("ws", "rs")

    def __init__(self):
        self.ws = {}
        self.rs = {}


class TRes:
    def __init__(self, nch, tiles=TT):
        self.tiles = tiles
        self.r = [[Res() for _ in tiles] for _ in range(nch)]

    def get(self, c, lo, hi):
        return [self.r[c][t] for t, (s, n) in enumerate(self.tiles) if s < hi and lo < s + n]

    def all(self):
        return [x for row in self.r for x in row]


class Sched:
    COMPUTE = ("pe", "act", "dve", "pool")

    def __init__(self, nc):
        self.nc = nc
        self.ins = {e: [] for e in ("pe", "act", "dve", "pool", "sp")}
        self.order = []
        self.streams = {}
        self.ph = ""

    def add(self, eng, fn, reads=(), writes=()):
        r = Rec()
        r.eng = eng
        r.fn = fn
        r.is_dma = False
        r.needs_inc = False
        r.sem = None
        r.ph = self.ph
        lst = self.ins[eng]
        r.idx = len(lst) + 1
        deps = set()
        key = eng
        for x in reads:
            deps.update(x.ws.values())
        for x in writes:
            deps.update(x.ws.values())
            deps.update(x.rs.values())
        for x in reads:
            x.rs[key] = r
        for x in writes:
            x.ws = {key: r}
            x.rs = {}
        deps.discard(r)
        r.deps = deps
        lst.append(r)
        self.order.append(r)
        return r

    def dma(self, stream, fn, reads=(), writes=(), nsem=4):
        st = self.streams.setdefault(stream, dict(recs=[], nsem=nsem, sems=None))
        r = Rec()
        r.eng = "sp"
        r.fn = fn
        r.is_dma = True
        r.needs_inc = True
        r.ph = self.ph
        lst = self.ins["sp"]
        r.idx = len(lst) + 1
        n = len(st["recs"])
        r.sem = (stream, n % st["nsem"])
        r.semval = 16 * (n // st["nsem"] + 1)
        deps = set()
        if n >= st["nsem"]:
            deps.add(st["recs"][n - st["nsem"]])
        key = ("d", id(r))
        for x in reads:
            deps.update(x.ws.values())
        for x in writes:
            deps.update(x.ws.values())
            deps.update(x.rs.values())
        for x in reads:
            x.rs[key] = r
        for x in writes:
            x.ws = {key: r}
            x.rs = {}
        r.deps = deps
        st["recs"].append(r)
        lst.append(r)
        self.order.append(r)
        return r

    def op(self, eng, method, reads=(), writes=(), **kw):
        r = self.add(eng, (lambda e: getattr(e, method)(**kw)), reads, writes)
        r.ph = r.ph + ":" + method + ":" + str(kw.get("func", kw.get("op", kw.get("op0", ""))))
        return r

    def dmaop(self, stream, out, in_, reads=(), writes=(), nsem=4):
        return self.dma(stream, (lambda e: e.dma_start(out=out, in_=in_)), reads, writes, nsem)

    def finalize(self):
        run = {e: {} for e in self.ins}
        for r in self.order:
            clock = run[r.eng]
            waits = []
            for d in sorted(r.deps, key=lambda d: (d.eng, d.idx)):
                if d.is_dma:
                    k = ("d", d.sem)
                    if clock.get(k, 0) < d.semval:
                        waits.append(d)
                        for kk, vv in d.clock.items():
                            if clock.get(kk, 0) < vv:
                                clock[kk] = vv
                        clock[k] = d.semval
                elif d.eng == r.eng:
                    if r.eng in ("pe", "sp") or r.idx - d.idx > 2:
                        continue
                    d.needs_inc = True
                    waits.append(d)
                else:
                    if clock.get(d.eng, 0) < d.idx:
                        d.needs_inc = True
                        waits.append(d)
                        for kk, vv in d.clock.items():
                            if clock.get(kk, 0) < vv:
                                clock[kk] = vv
            r.waits = waits
            if r.is_dma:
                r.clock = dict(clock)
            else:
                clock[r.eng] = r.idx
                r.clock = dict(clock)
        for e in self.COMPUTE:
            c = 0
            for r in self.ins[e]:
                if r.needs_inc:
                    c += 1
                r.cnt = c

    def emit(self, eng_name, eng, esems, dsems):
        for r in self.ins[eng_name]:
            for d in r.waits:
                if d.is_dma:
                    eng.wait_ge(dsems[d.sem], d.semval)
                else:
                    eng.wait_ge(esems[d.eng], d.cnt)
            ins = r.fn(eng)
            if r.is_dma:
                ins.then_inc(dsems[r.sem], 16)
            elif r.needs_inc:
                ins.then_inc(esems[r.eng], 1)


class Arena:
    def __init__(self, tensor, nbytes):
        self.t = tensor
        self.nbytes = nbytes
        self.off = 0

    def alloc(self, nbytes):
        nbytes = (nbytes + 63) // 64 * 64
        o = self.off
        self.off += nbytes
        assert self.off <= self.nbytes, (self.off, self.nbytes)
        return o

    def f32(self, off, n):
        return self.t[:, off // 4: off // 4 + n]

    def bf16(self, off, n):
        v = self.t[:, off // 4: off // 4 + (n + 1) // 2].bitcast(BF16)
        return v[:, 0:n]


CFG = dict(nseq=4, nlayer=2, stop=None, taps=())


class Pend:
    def __init__(self):
        self.p = {}

    def retire(self, res_list):
        for r in res_list:
            for dct in (r.ws, r.rs):
                for k, rec in dct.items():
                    o = self.p.get(k)
                    if o is None or o.idx < rec.idx:
                        self.p[k] = rec
            r.ws = {}
            r.rs = {}

    def new(self):
        r = Res()
        r.rs = dict(self.p)
        return r

    def tres(self, nch, tiles=TT):
        t = TRes.__new__(TRes)
        t.tiles = tiles
        t.r = [[self.new() for _ in tiles] for _ in range(nch)]
        return t


def build(cfg):
    NSEQ = cfg["nseq"]
    NLAYER = cfg["nlayer"]
    STOP = cfg["stop"]
    nc = bass.Bass("TRN2", target_bir_lowering=False, dynamic_dma_scratch_size=64)
    xT = nc.dram_tensor("xT", [NSEQ, 128, 8, SEQ], F32, kind="ExternalInput").ap()
    metaT = nc.dram_tensor("metaT", [128, 8, NMETA], F32, kind="ExternalInput").ap()
    wf = nc.dram_tensor("wf", [128, 2 * WL], F32, kind="ExternalInput").ap()
    vecs_d = nc.dram_tensor("vecs", [128, 2 * NVEC], F32, kind="ExternalInput").ap()
    bfg_d = nc.dram_tensor("bfg", [128, 16], F32, kind="ExternalInput").ap()
    consts_d = nc.dram_tensor("consts", [128, NCONST], F32, kind="ExternalInput").ap()
    outT = nc.dram_tensor("outT", [NSEQ, 128, 8, SEQ], F32, kind="ExternalOutput").ap()
    wb = nc.dram_tensor("wb", [128, 2 * WL], BF16, kind="Internal").ap()
    taps = {}
    for nm, shape, dt in cfg["taps"]:
        taps[nm] = nc.dram_tensor("tap_" + nm, list(shape), BF16 if dt == "bf16" else F32, kind="ExternalOutput").ap()

    S = Sched(nc)
    ARENA_BYTES = 229056
    from contextlib import ExitStack
    es = ExitStack()
    arena_t = es.enter_context(nc.sbuf_tensor("arena", [128, ARENA_BYTES // 4], F32))
    A = Arena(arena_t, ARENA_BYTES)
    psum = [es.enter_context(nc.psum_tensor("ps%d" % i, [128, 512], F32)) for i in range(8)]
    PS = [Res() for _ in range(8)]
    psrot = dict(i=0)

    def nb(banks=(0, 1, 2, 3, 4, 5)):
        b = banks[psrot["i"] % len(banks)]
        psrot["i"] += 1
        return b

    o_h = A.alloc(8 * L * 4)
    o_w = A.alloc(NSLOT * SLOT * 2)
    o_rstd = A.alloc(8320)
    o_vec = A.alloc(2 * NVEC * 4)
    o_bfg = A.alloc(16 * 4)
    o_c32 = A.alloc(NCONST * 4)
    o_c16 = A.alloc(NCONST * 2)
    NT32, NT16 = 5, 3
    o_t32 = A.alloc(NT32 * 416 * 4)
    o_t16 = A.alloc(NT16 * 416 * 2)
    o_eps = A.alloc(64)
    o_tr = A.alloc(2 * 416 * 4)
    o_P = A.off
    P_BYTES = ARENA_BYTES - o_P
    assert P_BYTES >= 100240, P_BYTES
    o_u = o_P
    o_qk = o_u + 8 * L * 2
    o_v = o_qk + 8 * L * 2
    o_yb = o_v + 17 * 520 * 2
    o_sp = o_yb + 4 * L * 2
    o_dg1 = o_sp
    o_qz = o_dg1 + 31 * 128 * 2
    assert o_qz + 2 * 1024 <= ARENA_BYTES, (o_qz, ARENA_BYTES)

    hT = A.f32(o_h, 8 * L).rearrange("p (c t) -> p c t", c=8)
    rstd = A.f32(o_rstd, L)
    vec = A.f32(o_vec, 2 * NVEC)
    bfg = A.f32(o_bfg, 16)
    c32 = A.f32(o_c32, NCONST)
    c16 = A.bf16(o_c16, NCONST)
    wslot = [A.bf16(o_w + i * SLOT * 2, SLOT) for i in range(NSLOT)]
    tmp32 = [A.f32(o_t32 + i * 416 * 4, 416) for i in range(NT32)]
    tmp16 = [A.bf16(o_t16 + i * 416 * 2, 416) for i in range(NT16)]
    epsv = A.f32(o_eps, 8)
    tmpr = [A.f32(o_tr + i * 416 * 4, 416) for i in range(2)]
    R_tmpr = [Res(), Res()]
    epsc, lnepsc, onec = epsv[:, 0:1], epsv[:, 1:2], epsv[:, 2:3]

    PD = Pend()
    R_h = TRes(8)
    R_const = Res()
    R_wb = Res()
    R_wslot = [Res() for _ in range(NSLOT)]
    R_tmp32 = [Res() for _ in range(NT32)]
    R_tmp16 = [Res() for _ in range(NT16)]
    rot = dict(t32=0, t16=0, w=0, tr=0)

    def vcol(l, name, j=0):
        c = l * NVEC + VEC[name] + j
        return vec[:, c:c + 1]

    def MM(out, lhsT, rhs, start, stop, reads, writes, **kw):
        S.op("pe", "matmul", reads, writes, out=out, lhsT=lhsT, rhs=rhs, start=start, stop=stop, **kw)

    def ACT(out, in_, func, reads, writes, bias=None, scale=None):
        kw = dict(out=out, in_=in_, func=func)
        if bias is not None:
            kw["bias"] = bias
        if scale is not None:
            kw["scale"] = scale
        S.op("act", "activation", reads, writes, **kw)

    def STT(eng, out, in0, scalar, in1, op0, op1, reads, writes):
        S.op(eng, "scalar_tensor_tensor", reads, writes, out=out, in0=in0, scalar=scalar, in1=in1, op0=op0, op1=op1)

    def TTOP(eng, out, in0, in1, op, reads, writes):
        S.op(eng, "tensor_tensor", reads, writes, out=out, in0=in0, in1=in1, op=op)

    def TS(eng, out, in0, s1, op0, reads, writes, s2=None, op1=None):
        kw = dict(out=out, in0=in0, scalar1=s1, scalar2=s2, op0=op0)
        if op1 is not None:
            kw["op1"] = op1
        S.op(eng, "tensor_scalar", reads, writes, **kw)

    def CP(eng, out, in_, reads, writes):
        if eng == "act":
            S.op("act", "activation", reads, writes, out=out, in_=in_, func=AF.Copy)
        else:
            S.op(eng, "tensor_copy", reads, writes, out=out, in_=in_)

    def RSQRT(out, in_, scale, eps_ap, reads, writes):
        np_ = in_.shape[0]
        tm, rtm = tmpr[rot["tr"] % 2], R_tmpr[rot["tr"] % 2]
        rot["tr"] += 1
        n_ = in_.shape[-1]
        tv = tm[0:np_, 0:n_]
        if eps_ap is None:
            ACT(tv, in_, AF.Ln, list(reads), [rtm], scale=scale)
        else:
            ACT(tv, in_, AF.Ln, list(reads) + [R_const], [rtm], bias=eps_ap, scale=scale)
        ACT(out, tv, AF.Exp, [rtm], writes, scale=-0.5)

    def RECIP(out, in_, reads, writes):
        S.op("dve", "reciprocal", reads, writes, out=out, in_=in_)

    def MEMSET(eng, ap, val, writes):
        S.op(eng, "memset", (), writes, ap=ap, constant=val)

    S.dmaop("misc", c32, consts_d, writes=[R_const])
    S.dmaop("misc", vec, vecs_d, writes=[R_const])
    S.dmaop("misc", bfg, bfg_d, writes=[R_const])
    CP("dve", c16, c32, [R_const], [R_const])
    MEMSET("dve", epsv[:, 0:1], RMS_EPS, [R_const])
    MEMSET("dve", epsv[:, 1:2], LN_EPS, [R_const])
    MEMSET("dve", epsv[:, 2:3], 1.0, [R_const])
    MEMSET("dve", epsv[:, 3:4], 0.0, [R_const])
    ident16 = c16[:, C_ID:C_ID + 128]
    ones16 = c16[:, C_ONES:C_ONES + 128]
    negtri16 = c16[:, C_NEG:C_NEG + 128]
    bo16 = c16[:, C_BO:C_BO + 128]
    ones32 = c32[:, C_ONES:C_ONES + 128]
    tri32 = c32[:, C_TRI:C_TRI + 128]

    dgd = nc.dram_tensor("dgd", [128, 2 * 2 * 3968], BF16, kind="Internal").ap()
    R_dgd = Res()
    dg1 = A.bf16(o_dg1, 31 * 128).rearrange("p (k m) -> p k m", k=31)
    R_dg1 = Res()
    qz = [A.bf16(o_qz + par * 1024, 512) for par in range(2)]
    R_qz = [None, None]
    dgst = [A.bf16(o_P + i * 7936, 3968).rearrange("p (k m) -> p k m", k=31) for i in range(2)]
    R_dgst = [PD.new(), PD.new()]
    for l_ in range(NLAYER):
        for cc_ in range(2):
            i_ = (l_ * 2 + cc_) % 2
            for k in range(31):
                TS("dve", dgst[i_][:, k, :], ident16, vcol(l_, "dw_w", 2 * k + cc_), ALU.mult, [R_const], [R_dgst[i_]])
            S.dmaop("dgo", dgd[:, (l_ * 2 + cc_) * 3968:(l_ * 2 + cc_ + 1) * 3968], dgst[i_].rearrange("p k m -> p (k m)"), reads=[R_dgst[i_]], writes=[R_dgd], nsem=2)
    PD.retire(R_dgst)
    st32 = [A.f32(o_P + i * PRE_BLK * 4, PRE_BLK) for i in range(2)]
    st16 = [A.bf16(o_P + 2 * PRE_BLK * 4 + i * PRE_BLK * 2, PRE_BLK) for i in range(2)]
    R_st32 = [PD.new(), PD.new()]
    R_st16 = [PD.new(), PD.new()]
    nblk_used = (NLAYER * WL + PRE_BLK - 1) // PRE_BLK
    for b in range(nblk_used):
        i = b % 2
        sl = slice(b * PRE_BLK, (b + 1) * PRE_BLK)
        S.dmaop("prein", st32[i], wf[:, sl], writes=[R_st32[i]], nsem=2)
        CP(("dve", "act")[b % 2], st16[i], st32[i], [R_st32[i]], [R_st16[i]])
        S.dmaop("preout", wb[:, sl], st16[i], reads=[R_st16[i]], writes=[R_wb], nsem=2)
    PD.retire(R_st32 + R_st16)

    def wload(l, name):
        o, s = POFF[name]
        i = rot["w"] % NSLOT
        rot["w"] += 1
        S.dmaop("w", wslot[i][:, 0:s], wb[:, l * WL + o: l * WL + o + s], reads=[R_wb], writes=[R_wslot[i]], nsem=NSLOT)
        return wslot[i], R_wslot[i]

    def t32():
        i = rot["t32"] % NT32
        rot["t32"] += 1
        return tmp32[i], R_tmp32[i]

    def t16():
        i = rot["t16"] % NT16
        rot["t16"] += 1
        return tmp16[i], R_tmp16[i]

    def tap(name, src_ap, reads):
        if name in taps:
            S.dmaop("tap", taps[name], src_ap, reads=reads, nsem=1)

    def load_seq(s):
        for t, (t0, n) in enumerate(TT):
            lo, hi = max(t0, NMETA), t0 + n
            S.dmaop("xin", hT[:, :, lo:hi], xT[s, :, :, lo - NMETA:hi - NMETA], writes=[R_h.r[c][t] for c in range(8)], nsem=4)
            if t == 0:
                S.dmaop("xin", hT[:, :, 0:NMETA], metaT, writes=[R_h.r[c][0] for c in range(8)], nsem=4)

    def store_seq(s, tiles):
        for t in tiles:
            t0, n = TT[t]
            lo, hi = max(t0, NMETA), t0 + n
            S.dmaop("xout", outT[s, :, :, lo - NMETA:hi - NMETA], hT[:, :, lo:hi], reads=[R_h.r[c][t] for c in range(8)], nsem=4)

    R_rstd = dict(r=None)

    def prenorm(l, gname, tiles, u_fn, Ru):
        for t in tiles:
            t0, n = TT[t]
            pb = nb()
            for c in range(8):
                sq, rsq = t16()
                ACT(sq[:, 0:n], hT[:, c, t0:t0 + n], AF.Square, [R_h.r[c][t]], [rsq])
                MM(psum[pb][:, 0:n], ones16, sq[:, 0:n], c == 0, c == 7, [rsq, R_const], [PS[pb]])
            RSQRT(rstd[:, t0:t0 + n], psum[pb][:, 0:n], 1.0 / DM, epsc, [PS[pb]], [R_rstd["r"].r[0][t]])
            for c in range(8):
                STT("dve", u_fn(c, t0, n), hT[:, c, t0:t0 + n], vcol(l, gname, c), rstd[:, t0:t0 + n], ALU.mult, ALU.mult,
                    [R_h.r[c][t], R_rstd["r"].r[0][t], R_const], [Ru.r[c][t]])

    def prenorm_stats(tiles):
        for t in tiles:
            t0, n = TT[t]
            pb = nb()
            for c in range(8):
                sq, rsq = t16()
                ACT(sq[:, 0:n], hT[:, c, t0:t0 + n], AF.Square, [R_h.r[c][t]], [rsq])
                MM(psum[pb][:, 0:n], ones16, sq[:, 0:n], c == 0, c == 7, [rsq, R_const], [PS[pb]])
            RSQRT(rstd[:, t0:t0 + n], psum[pb][:, 0:n], 1.0 / DM, epsc, [PS[pb]], [R_rstd["r"].r[0][t]])

    def prenorm_apply_steps(l, gname, tiles, u_fn, Ru):
        steps = []
        for t in tiles:
            t0, n = TT[t]
            for c in range(8):
                def f(c=c, t=t, t0=t0, n=n):
                    STT("dve", u_fn(c, t0, n), hT[:, c, t0:t0 + n], vcol(l, gname, c), rstd[:, t0:t0 + n], ALU.mult, ALU.mult,
                        [R_h.r[c][t], R_rstd["r"].r[0][t], R_const], [Ru.r[c][t]])
                steps.append(f)
        return steps

    def hn_a(l, ychunk, src, src_reads, n, dst, dst_writes):
        sq, rsq = t16()
        ACT(sq[:, 0:n], src, AF.Square, src_reads, [rsq])
        return (l, ychunk, src, src_reads, n, dst, dst_writes, sq, rsq)

    def hn_b(st):
        l, ychunk, src, src_reads, n, dst, dst_writes, sq, rsq = st
        pb = nb()
        MM(psum[pb][:, 0:n], bo16, sq[:, 0:n], True, True, [rsq, R_const], [PS[pb]])
        sd, rsd = t32()
        RSQRT(sd[:, 0:n], psum[pb][:, 0:n], 1.0 / 64, epsc, [PS[pb]], [rsd])
        STT("dve", dst, src, vcol(l, "head_g", ychunk), sd[:, 0:n], ALU.mult, ALU.mult, src_reads + [rsd, R_const], dst_writes)

    def mixer(s, l, pre_hook=None, rstd4=None):
        uT = A.bf16(o_u, 8 * L).rearrange("p (c t) -> p c t", c=8)
        R_u = PD.tres(8)
        R_rstd["r"] = PD.tres(1)
        yB = A.bf16(o_yb, 4 * L).rearrange("p (c t) -> p c t", c=4)
        R_yb = PD.tres(4)
        S.ph = "m_prenorm"
        if rstd4 is not None:
            R_rstd["r"].r[0][4] = rstd4
        prenorm(l, "mix_pre_g", range(4), (lambda c, t0, n: uT[:, c, t0:t0 + n]), R_u)
        if pre_hook is not None:
            S.ph = "f_tail"
            pre_hook()
            S.ph = "m_prenorm"
        prenorm(l, "mix_pre_g", [4], (lambda c, t0, n: uT[:, c, t0:t0 + n]), R_u)
        S.ph = "m_conv"
        if STOP == "prenorm":
            tap("u", uT, R_u.all())
            return
        oc = o_qk
        apad = A.bf16(oc, 2 * (L + 30)).rearrange("p (c t) -> p c t", c=2)
        oc += 2 * (L + 30) * 2
        oc = (oc + 63) // 64 * 64
        cv = A.f32(oc, 2 * L).rearrange("p (c t) -> p c t", c=2)
        oc += 2 * L * 4
        a2 = A.bf16(oc, 2 * L).rearrange("p (c t) -> p c t", c=2)
        oc += 2 * L * 2
        cxpad = A.f32(oc, L + 2)
        oc += (L + 2) * 4
        oc = (oc + 63) // 64 * 64
        dg = A.bf16(o_rstd, 31 * 128).rearrange("p (k m) -> p k m", k=31)
        PD.retire(R_rstd["r"].all())
        lnt = [A.f32(oc + i * 416 * 4, 416) for i in range(3)]
        oc += 3 * 416 * 4
        assert oc <= o_yb, (oc, o_yb)
        R_apad = PD.tres(2)
        R_apad0 = PD.new()
        R_cv = PD.tres(2)
        R_a2 = PD.tres(2)
        R_cx = PD.tres(1)
        R_cx0 = PD.new()
        R_dg = PD.new()
        R_lnt = [PD.new() for _ in range(3)]
        MEMSET("dve", apad[:, :, 0:30], 0.0, [R_apad0])
        MEMSET("dve", cxpad[:, 0:2], 0.0, [R_cx0])

        def fm_mm(wv, rw, j, t, pb, halo=0):
            t0, n = TT[t]
            for k in range(8):
                MM(psum[pb][:, 0:n], wv[:, j, k, :], uT[:, k, t0:t0 + n], k == 0, k == 7, [rw, R_u.r[k][t]], [PS[pb]])

        dgs = [dg, dg1]
        R_dg1 = PD.new()
        R_dgs = [R_dg, R_dg1]
        for cc_ in range(2):
            S.dmaop("dg", dgs[cc_].rearrange("p k m -> p (k m)"), dgd[:, (l * 2 + cc_) * 3968:(l * 2 + cc_ + 1) * 3968], reads=[R_dgd], writes=[R_dgs[cc_]], nsem=2)
        for cc in range(2):
            wt, rw = wload(l, "win%d" % cc)
            wv = wt[:, 0:2048].rearrange("p (j k m) -> p j k m", j=2, k=8)
            for t, (t0, n) in enumerate(TT):
                pv, pg = nb(), nb()
                fm_mm(wv, rw, 0, t, pv)
                fm_mm(wv, rw, 1, t, pg)
                sg, rsg = t32()
                ACT(sg[:, 0:n], psum[pg][:, 0:n], AF.Sigmoid, [PS[pg]], [rsg])
                TTOP("dve", apad[:, cc, 30 + t0:30 + t0 + n], psum[pv][:, 0:n], sg[:, 0:n], ALU.mult, [PS[pv], rsg], [R_apad.r[cc][t]])
            if cc == 0:
                pass
            for t, (t0, n) in enumerate(TT):
                pb = nb()
                rd = R_apad.get(cc, t0 - 30, t0 + n) + [R_dgs[cc], R_apad0]
                for k in range(31):
                    MM(psum[pb][:, 0:n], dgs[cc][:, k, :], apad[:, cc, t0 + k:t0 + k + n], k == 0, k == 30, rd, [PS[pb]])
                ACT(cv[:, cc, t0:t0 + n], psum[pb][:, 0:n], AF.Identity, [PS[pb], R_const], [R_cv.r[cc][t]], bias=vcol(l, "dw_b", cc))
        tap("cv", cv, R_cv.all())
        S.ph = "m_ln_pw"
        for t, (t0, n) in enumerate(TT):
            pm, pq = nb(), nb()
            for cc in range(2):
                MM(psum[pm][:, 0:n], ones32, cv[:, cc, t0:t0 + n], cc == 0, cc == 1, [R_cv.r[cc][t], R_const], [PS[pm]])
            for cc in range(2):
                sq, rsq = t32()
                ACT(sq[:, 0:n], cv[:, cc, t0:t0 + n], AF.Square, [R_cv.r[cc][t]], [rsq])
                MM(psum[pq][:, 0:n], ones32, sq[:, 0:n], cc == 0, cc == 1, [rsq, R_const], [PS[pq]])
            mu, m2, rs_ = lnt[0], lnt[1], lnt[2]
            ACT(mu[:, 0:n], psum[pm][:, 0:n], AF.Copy, [PS[pm]], [R_lnt[0]], scale=1.0 / 256)
            TTOP("dve", m2[:, 0:n], mu[:, 0:n], mu[:, 0:n], ALU.mult, [R_lnt[0]], [R_lnt[1]])
            STT("dve", m2[:, 0:n], psum[pq][:, 0:n], 1.0 / 256, m2[:, 0:n], ALU.mult, ALU.subtract, [PS[pq], R_lnt[1]], [R_lnt[1]])
            RSQRT(rs_[:, 0:n], m2[:, 0:n], 1.0, lnepsc, [R_lnt[1]], [R_lnt[2]])
            for cc in range(2):
                xc, rxc = t32()
                TTOP("dve", xc[:, 0:n], cv[:, cc, t0:t0 + n], mu[:, 0:n], ALU.subtract, [R_cv.r[cc][t], R_lnt[0]], [rxc])
                STT("dve", xc[:, 0:n], xc[:, 0:n], vcol(l, "ln_g", cc), rs_[:, 0:n], ALU.mult, ALU.mult, [rxc, R_lnt[2], R_const], [rxc])
                ACT(a2[:, cc, t0:t0 + n], xc[:, 0:n], AF.Silu, [rxc, R_const], [R_a2.r[cc][t]], bias=vcol(l, "ln_b", cc))
        wt, rw = wload(l, "pw")
        wv = wt[:, 0:512].rearrange("p (k m) -> p k m", k=2)
        pend = None
        for oc_ in range(2):
            for t, (t0, n) in enumerate(TT):
                pb = nb()
                for k in range(2):
                    MM(psum[pb][:, 0:n], wv[:, k, oc_ * 128:(oc_ + 1) * 128], a2[:, k, t0:t0 + n], k == 0, k == 1,
                       [rw, R_a2.r[k][t]], [PS[pb]])
                st = hn_a(l, 4 + oc_, psum[pb][:, 0:n], [PS[pb]], n, yB[:, oc_, t0:t0 + n], [R_yb.r[oc_][t]])
                if pend is not None:
                    hn_b(pend)
                pend = st
        hn_b(pend)
        S.ph = "m_sc"
        w2, rw2 = wload(l, "win2")
        w3, rw3 = wload(l, "win3")
        w4, rw4 = wload(l, "win4")
        v2 = w2[:, 0:2048].rearrange("p (j k m) -> p j k m", j=2, k=8)
        v3 = w3[:, 0:2048].rearrange("p (j k m) -> p j k m", j=2, k=8)
        v4 = w4[:, 0:2048].rearrange("p (j k m) -> p j k m", j=2, k=8)
        scw = [[(v2, rw2, 0), (v2, rw2, 1), (v3, rw3, 0)], [(v3, rw3, 1), (v4, rw4, 0), (v4, rw4, 1)]]
        pend = None
        for cc in range(2):
            for t, (t0, n) in enumerate(TT):
                pc, px, pbb = nb(), nb(), nb()
                fm_mm(scw[cc][0][0], scw[cc][0][1], scw[cc][0][2], t, pc)
                fm_mm(scw[cc][1][0], scw[cc][1][1], scw[cc][1][2], t, px)
                fm_mm(scw[cc][2][0], scw[cc][2][1], scw[cc][2][2], t, pbb)
                xs, rxs = t32()
                CP("act", xs[:, 0:n], psum[px][:, 0:n], [PS[px]], [rxs])
                TTOP("dve", cxpad[:, 2 + t0:2 + t0 + n], psum[pc][:, 0:n], xs[:, 0:n], ALU.mult, [PS[pc], rxs], [R_cx.r[0][t]])
                acc, racc = t32()
                rd = R_cx.get(0, t0 - 2, t0 + n) + [R_cx0, R_const]
                TS("dve", acc[:, 0:n], cxpad[:, t0:t0 + n], vcol(l, "sc_w", 0 * 2 + cc), ALU.mult, rd, [racc])
                STT("dve", acc[:, 0:n], cxpad[:, t0 + 1:t0 + 1 + n], vcol(l, "sc_w", 1 * 2 + cc), acc[:, 0:n], ALU.mult, ALU.add, rd + [racc], [racc])
                STT("dve", acc[:, 0:n], cxpad[:, t0 + 2:t0 + 2 + n], vcol(l, "sc_w", 2 * 2 + cc), acc[:, 0:n], ALU.mult, ALU.add, rd + [racc], [racc])
                TTOP("dve", acc[:, 0:n], psum[pbb][:, 0:n], acc[:, 0:n], ALU.mult, [PS[pbb], racc], [racc])
                st = hn_a(l, 6 + cc, acc[:, 0:n], [racc], n, yB[:, 2 + cc, t0:t0 + n], [R_yb.r[2 + cc][t]])
                if pend is not None:
                    hn_b(pend)
                pend = st
        hn_b(pend)
        tap("yb", yB, R_yb.all())
        if STOP == "conv":
            return
        S.ph = "m_qkv"
        PD.retire(R_apad.all() + [R_apad0, R_cx0, R_dg, R_dg1] + R_cv.all() + R_a2.all() + R_cx.all() + R_lnt)
        for par in range(2):
            R_qz[par] = PD.new()
            MEMSET("dve", qz[par], 0.0, [R_qz[par]])
        qk = A.bf16(o_qk, 8 * L).rearrange("p (c t) -> p c t", c=8)
        R_qk = PD.tres(8)
        vaug = A.bf16(o_v, 17 * 520).rearrange("p (b h d) -> p b h d", b=17, h=8)
        R_v = [PD.new() for _ in range(17)]
        R_vones = PD.new()
        MEMSET("dve", vaug[:, :, :, 64:65], 1.0, [R_vones])
        for i in range(4):
            wt, rw = wload(l, "win%d" % (5 + i))
            wv = wt[:, 0:2048].rearrange("p (j k m) -> p j k m", j=2, k=8)
            for j in range(2):
                ch = 2 * i + j
                for t, (t0, n) in enumerate(TT):
                    pb = nb()
                    fm_mm(wv, rw, j, t, pb)
                    CP("act", qk[:, ch, t0:t0 + n], psum[pb][:, 0:n], [PS[pb]], [R_qk.r[ch][t]])
        wa, rwa = wload(l, "vfa")
        wbb, rwb = wload(l, "vfb")
        va = wa[:, 0:2048].rearrange("p (k m) -> p k m", k=8)
        vb = wbb[:, 0:2112].rearrange("p (k m) -> p k m", k=8)
        R_fl = PD.new()
        og = o_rstd
        fl = A.f32(og, 136).rearrange("p (b h) -> p b h", b=17)
        og += 136 * 4
        lf = A.f32(og, 136).rearrange("p (b h) -> p b h", b=17)
        og += 136 * 4
        Cx = A.f32(og, 18 * 8).rearrange("p (b h) -> p b h", b=18)
        og += 18 * 8 * 4
        Gt = A.f32(og, 136).rearrange("p (b h) -> p b h", b=17)
        og += 136 * 4
        biasv = A.f32(og, 5 * 136).rearrange("p (i b h) -> p i b h", i=5, b=17)
        og += 5 * 136 * 4
        assert og <= o_rstd + 8320
        MEMSET("dve", fl, 0.0, [R_fl])
        for b, (b0, nt) in enumerate(BLK):
            pa, pbk = nb(), nb()
            rdu = []
            for k in range(8):
                rdu += R_u.get(k, b0, b0 + nt)
            for k in range(8):
                MM(psum[pa][0:nt, 0:256], uT[:, k, b0:b0 + nt], va[:, k, :], k == 0, k == 7, [rwa] + rdu, [PS[pa]])
            for k in range(8):
                MM(psum[pbk][0:nt, 0:264], uT[:, k, b0:b0 + nt], vb[:, k, :], k == 0, k == 7, [rwb] + rdu, [PS[pbk]])
            CP("act", vaug[0:nt, b, 0:4, 0:64], psum[pa][0:nt, 0:256].rearrange("p (h d) -> p h d", h=4), [PS[pa]], [R_v[b]])
            CP("dve", vaug[0:nt, b, 4:8, 0:64], psum[pbk][0:nt, 0:256].rearrange("p (h d) -> p h d", h=4), [PS[pbk]], [R_v[b]])
            TTOP("dve", fl[0:nt, b, :], psum[pbk][0:nt, 256:264], bfg[0:nt, l * 8:l * 8 + 8], ALU.add, [PS[pbk], R_const], [R_fl])
        flf = fl.rearrange("p b h -> p (b h)")
        lff = lf.rearrange("p b h -> p (b h)")
        ACT(lff, flf, AF.Exp, [R_fl], [R_fl], scale=-1.0)
        ACT(lff, lff, AF.Ln, [R_fl, R_const], [R_fl], bias=onec, scale=1.0)
        pc_ = nb()
        MM(psum[pc_][:, 0:136], tri32, lff, True, True, [R_fl, R_const], [PS[pc_]])
        MM(psum[pc_][:, 136:272], ones32, lff, True, True, [R_fl, R_const], [PS[pc_]])
        MEMSET("dve", Cx[:, 0, :], 0.0, [R_fl])
        for b in range(16):
            TTOP("dve", Cx[:, b + 1, :], Cx[:, b, :], psum[pc_][:, 136 + 8 * b:136 + 8 * b + 8], ALU.add, [PS[pc_], R_fl], [R_fl])
        TTOP("dve", Gt.rearrange("p b h -> p (b h)"), psum[pc_][:, 0:136], Cx[:, 0:17, :].rearrange("p b h -> p (b h)"), ALU.add,
             [PS[pc_], R_fl], [R_fl])
        for i in range(5):
            nbk = 4 * i + 4 if i < 4 else 17
            e_i = 4 * i + 4 if i < 4 else 16
            TTOP("dve", biasv[:, i, 0:nbk, :], Gt[:, 0:nbk, :], Cx[:, e_i:e_i + 1, :].broadcast_to([128, nbk, 8]), ALU.subtract, [R_fl], [R_fl])
        tap("gt", Gt.rearrange("p b h -> p (b h)"), [R_fl])
        if STOP == "qkv":
            tap("qk", qk, R_qk.all())
            tap("v", vaug.rearrange("p b h d -> p (b h d)"), R_v)
            return
        S.ph = "m_attn"
        PD.retire(R_u.all())
        oa = o_u
        yA = A.bf16(oa, 4 * L).rearrange("p (c t) -> p c t", c=4)
        oa += 4 * L * 2
        NPT = 4
        pT = [A.bf16(oa + i * 1024, 512) for i in range(NPT)]
        oa += NPT * 1024
        ost = A.f32(oa, 4 * 8 * 65).rearrange("p (r h d) -> p r h d", r=4, h=8)
        oa += 4 * 8 * 65 * 4
        ynt = [A.bf16(oa + i * 1024, 512) for i in range(4)]
        oa += 4096
        SSb, RDb, TTb, RRb, SCb = [A.f32(og + 128 * r_, 32).rearrange("p (r h) -> p r h", r=4) for r_ in range(5)]
        og += 640
        sqt = A.f32(og, 256).rearrange("p (r d) -> p r d", r=4)
        og += 1024
        assert og <= o_rstd + 8320
        assert oa <= o_qk, (oa, o_qk)
        R_yA = PD.tres(4)
        R_pT = [PD.new() for _ in range(NPT)]
        R_ost = PD.new()
        R_sqt = PD.new()
        R_ynt = [PD.new() for _ in range(4)]
        R_sm = PD.new()
        SB, OB, TB = (0, 1, 2, 6), (3, 4), 5
        NSB = len(SB)
        steps = []
        for i in range(5):
            q0, qn = (512 * i, 512) if i < 4 else (2048, 16)
            jmax = 4 * i + 3 if i < 4 else 16
            for h in range(8):
                for j in range(jmax + 1):
                    steps.append((i, h, j, q0, qn, jmax))
        LA = 3
        nsteps = len(steps)
        ynrot = dict(i=0)
        kzrot = [0, 0]

        def stage_q(i, h):
            q0, qn = (512 * i, 512) if i < 4 else (2048, 16)
            kc, po = h // 2, (h % 2) * 64
            CP("dve", qz[h % 2][po:po + 64, 0:qn], qk[po:po + 64, kc, q0:q0 + qn], R_qk.get(kc, q0, q0 + qn), [R_qz[h % 2]])

        def emit_S(n):
            i, h, j, q0, qn, jmax = steps[n]
            kc, po = h // 2, (h % 2) * 64
            b0, kb = BLK[j]
            lo = max(q0, b0)
            N = q0 + qn - lo
            diag = (b0 >= q0)
            sb = SB[n % NSB]
            rq = R_qk.get(kc, lo, q0 + qn)
            rk = R_qk.get(4 + kc, b0, b0 + kb)
            par = h % 2
            if j == 0:
                if (i, h) == (0, 0):
                    stage_q(0, 0)
                nxt = (i, h + 1) if h < 7 else ((i + 1, 0) if i < 4 else None)
                if nxt is not None:
                    stage_q(*nxt)
            MM(psum[sb][0:kb, 0:N], qk[:, 4 + kc, b0:b0 + kb], qz[par][:, lo - q0:lo - q0 + N], True, not diag, rk + [R_qz[par]], [PS[sb]])
            if diag:
                MM(psum[sb][0:kb, 0:kb], ident16[0:kb, 0:kb], negtri16[0:kb, 0:kb], False, True, [R_const], [PS[sb]])
            ACT(pT[n % NPT][0:kb, 0:N], psum[sb][0:kb, 0:N], AF.Exp, [PS[sb], R_fl], [R_pT[n % NPT]], bias=biasv[0:kb, i, j, h:h + 1], scale=0.125)

        def emit_PV(n):
            i, h, j, q0, qn, jmax = steps[n]
            b0, kb = BLK[j]
            lo = max(q0, b0)
            ob = OB[h % 2]
            if i < 4:
                for r in range(4):
                    q = 4 * i + r
                    if q < j:
                        continue
                    off = 128 * q - lo
                    MM(psum[ob][:, r * 65:(r + 1) * 65], pT[n % NPT][0:kb, off:off + 128], vaug[0:kb, j, h, :], (j == 0 and r == 0), j == q,
                       [R_pT[n % NPT], R_v[j], R_vones], [PS[ob]], skip_group_check=True)
            else:
                MM(psum[ob][0:16, 0:65], pT[n % NPT][0:kb, 0:16], vaug[0:kb, j, h, :], j == 0, j == 16, [R_pT[n % NPT], R_v[j], R_vones], [PS[ob]])
            if j == jmax:
                if i < 4:
                    CP("dve", ost[:, :, h, :], psum[ob][:, 0:260].rearrange("p (r d) -> p r d", r=4), [PS[ob]], [R_ost])
                    nt_, R_ = 128, 4
                else:
                    CP("dve", ost[0:16, 0, h, :], psum[ob][0:16, 0:65], [PS[ob]], [R_ost])
                    nt_, R_ = 16, 1
                Oh = ost[0:nt_, 0:R_, h, 0:64]
                TTOP("dve", sqt[0:nt_, 0:R_], Oh, Oh, ALU.mult, [R_ost], [R_sqt])
                S.op("dve", "tensor_reduce", [R_sqt], [R_sm], out=SSb[0:nt_, 0:R_, h], in_=sqt[0:nt_, 0:R_], axis=AX.X, op=ALU.add)
                if h == 7:
                    finish_tile(i)

        DQ = []
        tk = [0]

        def defer(delay, fn):
            DQ.append([tk[0] + delay, fn])

        def tick():
            tk[0] += 1
            for x in [x for x in DQ if x[0] <= tk[0]]:
                DQ.remove(x)
                x[1]()

        def flushq():
            while DQ:
                DQ.pop(0)[1]()

        PS_T = [Res(), Res()]
        PS_T[0].rs = dict(PS[TB].rs)
        PS_T[0].ws = dict(PS[TB].ws)
        PS_T[1].rs = dict(PS[TB].rs)
        PS_T[1].ws = dict(PS[TB].ws)
        pst = psum[TB][:, 0:512].bitcast(BF16)

        def finish_tile(i):
            blocks = [4 * i + r for r in range(4)] if i < 4 else [16]

            nt_, R_ = (128, 4) if i < 4 else (16, 1)
            SS, RD, T_, RR, SC = [b[0:nt_, 0:R_] for b in (SSb, RDb, TTb, RRb, SCb)]

            def stA():
                RECIP(RD, ost[0:nt_, 0:R_, :, 64], [R_ost], [R_sm])
                TTOP("dve", T_, SS, RD, ALU.mult, [R_sm], [R_sm])
                TTOP("dve", T_, T_, RD, ALU.mult, [R_sm], [R_sm])
                TS("dve", T_, T_, 1.0 / 64, ALU.mult, [R_sm], [R_sm], s2=RMS_EPS, op1=ALU.add)

            def stB():
                tm, rtm = tmpr[rot["tr"] % 2], R_tmpr[rot["tr"] % 2]
                rot["tr"] += 1
                tv = tm[0:nt_, 0:R_ * 8].rearrange("p (r h) -> p r h", r=R_)
                ACT(tv, T_, AF.Ln, [R_sm], [rtm])
                ACT(RR, tv, AF.Exp, [rtm], [R_sm], scale=-0.5)

            def stC():
                TTOP("dve", SC, RR, RD, ALU.mult, [R_sm], [R_sm])
                for r, q in enumerate(blocks):
                    b0, nt = BLK[q]
                    TTOP("dve", ynt[r][0:nt].rearrange("p (h d) -> p h d", h=8), ost[0:nt, r, :, 0:64], SCb[0:nt, r, :].unsqueeze(2).broadcast_to([nt, 8, 64]),
                         ALU.mult, [R_ost, R_sm], [R_ynt[r]])

            def stD(r, q):
                b0, nt = BLK[q]
                hb = r % 2
                for c in range(4):
                    S.op("pe", "transpose", [R_ynt[r], R_const], [PS_T[hb]], out=pst[:, hb * 512 + c * 128:hb * 512 + c * 128 + nt],
                         in_=ynt[r][0:nt, c * 128:(c + 1) * 128], identity=ident16[0:nt, 0:nt])

            def stE(r, q):
                b0, nt = BLK[q]
                hb = r % 2
                for c in range(4):
                    ACT(yA[:, c, b0:b0 + nt], pst[:, hb * 512 + c * 128:hb * 512 + c * 128 + nt], AF.Copy, [PS_T[hb], R_const], R_yA.get(c, b0, b0 + nt),
                        scale=vcol(l, "head_g", c))

            stA()
            defer(3, stB)
            defer(6, stC)
            for r, q in enumerate(blocks):
                defer(13 + 2 * r, (lambda r=r, q=q: stD(r, q)))
                defer(16 + 2 * r, (lambda r=r, q=q: stE(r, q)))

        for n in range(nsteps + LA):
            if n < nsteps:
                emit_S(n)
            if n - LA >= 0:
                emit_PV(n - LA)
            tick()
        flushq()
        for hb in range(2):
            for dct in (PS_T[hb].ws, PS_T[hb].rs):
                for k_, rec_ in dct.items():
                    o_ = PS[TB].rs.get(k_)
                    if o_ is None or o_.idx < rec_.idx:
                        PS[TB].rs[k_] = rec_
        tap("ya", yA, R_yA.all())
        if STOP == "attn":
            return
        S.ph = "m_wout"
        PD.retire(R_qk.all() + R_pT + [R_ost, R_sqt, R_sm, R_fl] + R_ynt)
        R_rstd["r"] = PD.tres(1)
        ysrc = [(yA[:, k], R_yA.r[k]) for k in range(4)] + [(yB[:, k], R_yb.r[k]) for k in range(4)]
        pend = None
        wouts = [wload(l, "wout%d" % i) for i in range(4)]
        for t in range(5):
            st = proj_mm(l, None, 2, ysrc, [t], o_qk + (t % 2) * 4 * L * 2, 0, ((6, 7)[t % 2],), "mix_post_g", preloaded=wouts, bgk=2)
            bgflush()
            BG.extend(proj_tail_steps(l, "mix_post_g", st))
        bgflush()
        PD.retire(R_yA.all() + R_yb.all() + R_v + [R_vones] + R_rstd["r"].all() + R_qz)

    BG = []

    def bgpop(k):
        for _ in range(k):
            if BG:
                BG.pop(0)()

    def bgflush():
        while BG:
            BG.pop(0)()

    def proj_mm(l, pieces, cpp, src, tiles, o_ytmp, src_off, statbanks, gname, preloaded=None, bgk=0):
        K = len(src)
        t_lo = TT[tiles[0]][0]
        YW = 416 * len(tiles)
        ytmp = A.f32(o_ytmp, 8 * YW).rearrange("p (c t) -> p c t", c=8)
        R_yt = [[PD.new() for _ in tiles] for _ in range(8)]
        stat = {t: statbanks[i] for i, t in enumerate(tiles)}
        pend_stat = []
        for c in range(8):
            if c % cpp == 0:
                wt, rw = preloaded[c // cpp] if preloaded is not None else wload(l, pieces[c // cpp])
                if cpp == 2:
                    wv = wt[:, 0:2048].rearrange("p (j k m) -> p j k m", j=2, k=8)
                else:
                    wv = wt[:, 0:K * 128].rearrange("p (j k m) -> p j k m", j=1, k=K)
            j = c % cpp
            for ti, t in enumerate(tiles):
                t0, n = TT[t]
                pb = nb((0, 1, 2, 3, 4, 5))
                for k in range(K):
                    sap, sres = src[k]
                    MM(psum[pb][:, 0:n], wv[:, j, k, :], sap[:, t0 - src_off:t0 - src_off + n], k == 0, k == K - 1, [rw, sres[t]], [PS[pb]])
                sq, rsq = t16()
                ACT(sq[:, 0:n], psum[pb][:, 0:n], AF.Square, [PS[pb]], [rsq])
                ACT(ytmp[:, c, t0 - t_lo:t0 - t_lo + n], psum[pb][:, 0:n], AF.Copy, [PS[pb], R_const], [R_yt[c][ti]], scale=vcol(l, gname, c))
                pend_stat.append((psum[stat[t]][:, 0:n], sq[:, 0:n], c == 0, c == 7, rsq, PS[stat[t]]))
                if len(pend_stat) > 2:
                    o_, s_, a_, b_, r_, p_ = pend_stat.pop(0)
                    MM(o_, ones16, s_, a_, b_, [r_, R_const], [p_])
                bgpop(bgk)
        for o_, s_, a_, b_, r_, p_ in pend_stat:
            MM(o_, ones16, s_, a_, b_, [r_, R_const], [p_])
        return (tiles, t_lo, ytmp, R_yt, stat)

    def proj_tail_steps(l, gname, st):
        tiles, t_lo, ytmp, R_yt, stat = st
        steps = []
        for ti, t in enumerate(tiles):
            t0, n = TT[t]

            def s0(t=t, t0=t0, n=n):
                RSQRT(rstd[:, t0:t0 + n], psum[stat[t]][:, 0:n], 1.0 / DM, epsc, [PS[stat[t]]], [R_rstd["r"].r[0][t]])
            steps.append(s0)
        for ti, t in enumerate(tiles):
            t0, n = TT[t]
            for c in range(8):
                def s1(c=c, ti=ti, t=t, t0=t0, n=n):
                    yv = ytmp[:, c, t0 - t_lo:t0 - t_lo + n]
                    TTOP("dve", yv, yv, rstd[:, t0:t0 + n], ALU.mult, [R_yt[c][ti], R_rstd["r"].r[0][t]], [R_yt[c][ti]])
                    TTOP("dve", hT[:, c, t0:t0 + n], hT[:, c, t0:t0 + n], yv, ALU.add, [R_yt[c][ti], R_h.r[c][t]], [R_h.r[c][t]])
                steps.append(s1)
        steps.append(lambda: PD.retire([x for row in R_yt for x in row]))
        return steps

    def proj_tail(l, gname, st):
        for f in proj_tail_steps(l, gname, st):
            f()

    def ffn(s, l):
        last_tail = []
        R_rstd["r"] = PD.tres(1)
        o = o_P
        UW = 2 + 828
        up = A.bf16(o, 8 * UW).rearrange("p (c t) -> p c t", c=8)
        o += 8 * UW * 2
        o = (o + 63) // 64 * 64
        actT = A.bf16(o, 22 * 828).rearrange("p (c t) -> p c t", c=22)
        o += 22 * 828 * 2
        o = (o + 63) // 64 * 64
        NFA = 4
        facc = [A.f32(o + i * 416 * 4, 416) for i in range(NFA)]
        o += NFA * 416 * 4
        fsg = [A.bf16(o + i * 416 * 2, 416) for i in range(2)]
        o += 2 * 416 * 2
        o = (o + 63) // 64 * 64
        o_yt = o
        assert o_yt + 2 * 8 * 832 * 4 <= ARENA_BYTES, (o_yt, ARENA_BYTES)
        R_facc = [PD.new() for _ in range(NFA)]
        R_fsg = [PD.new() for _ in range(2)]
        R_halo = PD.new()
        frot = dict(a=0, s=0)

        class _RU:
            pass

        def new_up_res(tiles):
            R_up = [[PD.new() for _ in tiles] for _ in range(8)]
            ru = _RU()
            ru.r = [{t: R_up[c][ti] for ti, t in enumerate(tiles)} for c in range(8)]
            return R_up, ru

        def new_act_res(tiles):
            return [[PD.new() for _ in tiles] for _ in range(22)]

        def apply_steps(tiles, ru):
            t_lo = TT[tiles[0]][0]
            return prenorm_apply_steps(l, "ffn_pre_g", tiles, (lambda c, t0, n: up[:, c, 2 + t0 - t_lo:2 + t0 - t_lo + n]), ru)

        S.ph = "f_prenorm"
        prenorm_stats(range(5))
        R_up, ru = new_up_res(FFN_PASSES[0])
        R_act = new_act_res(FFN_PASSES[0])
        for f in apply_steps(FFN_PASSES[0], ru):
            f()
        for pi, tiles in enumerate(FFN_PASSES):
            t_lo = TT[tiles[0]][0]
            S.ph = "f_up"
            for p in range(22):
                wt, rw = wload(l, "wup%d" % p)
                wv = wt[:, 0:2048].rearrange("p (j k m) -> p j k m", j=2, k=8)
                for ti, t in enumerate(tiles):
                    t0, n = TT[t]
                    hal = 0 if t == 0 else 2
                    c0 = 2 + t0 - t_lo - hal
                    rdu = []
                    for k in range(8):
                        rdu.append(R_up[k][ti])
                        if hal and ti > 0:
                            rdu.append(R_up[k][ti - 1])
                    if hal and ti == 0:
                        rdu.append(R_halo)
                    res = []
                    for half in range(2):
                        pb = nb((0, 1, 2, 3, 4, 5))
                        for k in range(8):
                            MM(psum[pb][:, 0:n + hal], wv[:, half, k, :], up[:, k, c0:c0 + n + hal], k == 0, k == 7, [rw] + rdu, [PS[pb]])
                        ai = frot["a"] % NFA
                        frot["a"] += 1
                        acc, racc = facc[ai], R_facc[ai]
                        f = half * 22 + p
                        ACT(acc[:, 0:n], psum[pb][:, hal:hal + n], AF.Copy, [PS[pb], R_const], [racc], scale=vcol(l, "ffn_cw", 2 * 44 + f))
                        if hal:
                            STT("dve", acc[:, 0:n], psum[pb][:, 1:1 + n], vcol(l, "ffn_cw", 1 * 44 + f), acc[:, 0:n], ALU.mult, ALU.add, [PS[pb], racc, R_const], [racc])
                            STT("dve", acc[:, 0:n], psum[pb][:, 0:n], vcol(l, "ffn_cw", 0 * 44 + f), acc[:, 0:n], ALU.mult, ALU.add, [PS[pb], racc, R_const], [racc])
                        else:
                            STT("dve", acc[:, 1:n], psum[pb][:, 0:n - 1], vcol(l, "ffn_cw", 1 * 44 + f), acc[:, 1:n], ALU.mult, ALU.add, [PS[pb], racc, R_const], [racc])
                            STT("dve", acc[:, 2:n], psum[pb][:, 0:n - 2], vcol(l, "ffn_cw", 0 * 44 + f), acc[:, 2:n], ALU.mult, ALU.add, [PS[pb], racc, R_const], [racc])
                        res.append((acc, racc))
                    si = frot["s"] % 2
                    frot["s"] += 1
                    ACT(fsg[si][:, 0:n], res[0][0][:, 0:n], AF.Silu, [res[0][1]], [R_fsg[si]])
                    TTOP("dve", actT[:, p, t0 - t_lo:t0 - t_lo + n], fsg[si][:, 0:n], res[1][0][:, 0:n], ALU.mult, [R_fsg[si], res[1][1]], [R_act[p][ti]])
            if pi + 1 < len(FFN_PASSES):
                width = sum(TT[t][1] for t in tiles)
                rd_all = [R_up[c][len(tiles) - 1] for c in range(8)]
                CP("dve", up[:, :, 0:2], up[:, :, width:width + 2], rd_all, [R_halo])
            S.ph = "f_down"
            src = [(actT[:, k], {t: R_act[k][ti] for ti, t in enumerate(tiles)}) for k in range(22)]
            PD.retire([x for row in R_up for x in row])
            if pi + 1 < len(FFN_PASSES):
                R_up, ru = new_up_res(FFN_PASSES[pi + 1])
                nfront = len(FFN_PASSES[pi - 1]) if (pi > 0 and BG) else 0
                BG[nfront:nfront] = apply_steps(FFN_PASSES[pi + 1], ru)
            st = proj_mm(l, ["wdn%d" % i for i in range(8)], 1, src, tiles, o_yt + (pi % 2) * 8 * 832 * 4, t_lo, (6, 7), "ffn_post_g", bgk=3)
            bgflush()
            PD.retire([x for row in R_act for x in row])
            if pi + 1 < len(FFN_PASSES):
                R_act = new_act_res(FFN_PASSES[pi + 1])
            S.ph = "f_tail"
            if pi + 1 < len(FFN_PASSES):
                BG.extend(proj_tail_steps(l, "ffn_post_g", st))
            else:
                last_tail.extend(proj_tail_steps(l, "ffn_post_g", st))
        PD.retire(R_facc + R_fsg + [R_halo] + [x for t_, x in enumerate(R_rstd["r"].r[0]) if t_ != 4])
        return last_tail, R_rstd["r"].r[0][4]

    def load_tiles(s, tiles):
        for t in tiles:
            t0, n = TT[t]
            lo, hi = max(t0, NMETA), t0 + n
            S.dmaop("xin", hT[:, :, lo:hi], xT[s, :, :, lo - NMETA:hi - NMETA], writes=[R_h.r[c][t] for c in range(8)], nsem=4)
            if t == 0:
                S.dmaop("xin", hT[:, :, 0:NMETA], metaT, writes=[R_h.r[c][0] for c in range(8)], nsem=4)

    hook = None
    rstd4 = None
    load_tiles(0, range(5))
    for s in range(NSEQ):
        for l in range(NLAYER):
            mixer(s, l, hook, rstd4)
            hook, rstd4 = None, None
            if STOP in ("prenorm", "conv", "qkv", "attn", "mixer"):
                break
            tail_steps, rstd4 = ffn(s, l)
            last = (l == NLAYER - 1)
            if last:
                tap("h", hT, R_h.all()) if False else None
                store_seq(s, range(4))
                if s + 1 < NSEQ:
                    load_tiles(s + 1, range(4))

            def hook(tail_steps=tail_steps, last=last, s=s):
                for f in tail_steps:
                    f()
                if last:
                    store_seq(s, [4])
                    if s + 1 < NSEQ:
                        load_tiles(s + 1, [4])
        if STOP is not None:
            tap("h", hT, R_h.all())
            store_seq(s, range(5))
            hook = None
    if hook is not None:
        hook()
        if "h" in taps:
            tap("h", hT, R_h.all())

    S.finalize()
    esems = {}
    dsems = {}
    for e in Sched.COMPUTE:
        esems[e] = es.enter_context(nc.semaphore("s_" + e))
    for stname, st in S.streams.items():
        for i in range(st["nsem"]):
            dsems[(stname, i)] = es.enter_context(nc.semaphore("d_%s%d" % (stname, i)))
    block = es.enter_context(nc.Block())

    @block.tensor
    def _(eng):
        S.emit("pe", eng, esems, dsems)

    @block.scalar
    def _(eng):
        S.emit("act", eng, esems, dsems)

    @block.vector
    def _(eng):
        S.emit("dve", eng, esems, dsems)

    @block.gpsimd
    def _(eng):
        S.emit("pool", eng, esems, dsems)

    @block.sync
    def _(eng):
        S.emit("sp", eng, esems, dsems)
        for stname, st in S.streams.items():
            n = len(st["recs"])
            for i in range(st["nsem"]):
                k = len([1 for j in range(n) if j % st["nsem"] == i])
                if k:
                    eng.wait_ge(dsems[(stname, i)], 16 * k)
    es.close()
    stats = {e: len(v) for e, v in S.ins.items()}
    global _LAST_SCHED
    _LAST_SCHED = S
    return nc, stats


def make_in_maps(inputs, cfg):
    NSEQ = cfg["nseq"]
    x = np.asarray(inputs["x"], np.float32)
    f32 = lambda a: np.asarray(a, np.float32)
    wfull = host_weights(f32(inputs["w_in"]), f32(inputs["a_pw_w"]), f32(inputs["w_out"]),
                         f32(inputs["ffn_w_up"]), f32(inputs["ffn_w_down"]))
    vecs = host_vecs({k: f32(v) for k, v in inputs.items()})
    bfg = np.ascontiguousarray(np.broadcast_to(f32(inputs["b_forget"]).reshape(1, 16), (128, 16)))
    metaT = np.ascontiguousarray(f32(inputs["meta_tokens"]).reshape(NMETA, 8, 128).transpose(2, 1, 0))
    consts = host_consts()
    maps = []
    ncores = cfg.get("ncores", NCORES)
    for c in range(ncores):
        xs = x[c * NSEQ:(c + 1) * NSEQ]
        xTn = np.ascontiguousarray(xs.reshape(NSEQ, SEQ, 8, 128).transpose(0, 3, 2, 1))
        maps.append(dict(xT=xTn, metaT=metaT, wf=wfull, vecs=vecs, bfg=bfg, consts=consts))
    return maps


def kernel(**inputs):
    cfg = dict(CFG)
    nc, _ = build(cfg)
    maps = make_in_maps(inputs, cfg)
    res = run_bass_kernel_spmd(nc, maps, core_ids=list(range(NCORES)))
    outs = []
    for c in range(NCORES):
        oT = res.results[c]["outT"]
        outs.append(np.ascontiguousarray(oT.transpose(0, 3, 2, 1)).reshape(cfg["nseq"], SEQ, DM))
    return np.concatenate(outs, axis=0).astype(np.float32)
```

```python
import numpy as np
import concourse.bass as bass
import concourse.mybir as mybir
from concourse.bass_utils import run_bass_kernel_spmd

F32 = mybir.dt.float32
BF16 = mybir.dt.bfloat16
AF = mybir.ActivationFunctionType
ALU = mybir.AluOpType
AX = mybir.AxisListType

NCORES = 8
SEQ = 2048
NMETA = 16
L = SEQ + NMETA
DM = 1024
DFF = 2816
DIN = 2824
TT = [(0, 416), (416, 412), (828, 412), (1240, 412), (1652, 412)]
BLK = [(128 * b, 128) for b in range(16)] + [(2048, 16)]
FFN_PASSES = [[0, 1], [2, 3], [4]]
RMS_EPS = 1e-6
LN_EPS = 1e-5

ZC = dict(q=0, k=512, v=1024, f=1536, aval=1544, agate=1800, scb=2056, scc=2312, scx=2568)
WIN_FM = [("aval", 0), ("agate", 0), ("aval", 1), ("agate", 1),
          ("scc", 0), ("scx", 0), ("scb", 0), ("scc", 1), ("scx", 1), ("scb", 1),
          ("q", 0), ("q", 1), ("q", 2), ("q", 3), ("k", 0), ("k", 1), ("k", 2), ("k", 3)]
PIECES = []
for i in range(9):
    PIECES.append(("win%d" % i, 2048))
PIECES.append(("pw", 512))
PIECES.append(("vfa", 2048))
PIECES.append(("vfb", 2112))
for i in range(4):
    PIECES.append(("wout%d" % i, 2048))
for i in range(22):
    PIECES.append(("wup%d" % i, 2048))
for i in range(8):
    PIECES.append(("wdn%d" % i, 2816))
POFF = {}
_o = 0
for _n, _s in PIECES:
    POFF[_n] = (_o, _s)
    _o += _s
WL = _o
assert WL == 98880
SLOT = 2816
NSLOT = 4
PRE_BLK = 4120

VEC = {}
_v = 0
for _n, _s in [("mix_pre_g", 8), ("mix_post_g", 8), ("ffn_pre_g", 8), ("ffn_post_g", 8), ("head_g", 8),
               ("dw_w", 62), ("dw_b", 2), ("ln_g", 2), ("ln_b", 2), ("sc_w", 6), ("ffn_cw", 132)]:
    VEC[_n] = _v
    _v += _s
NVEC = _v

C_ID, C_ONES, C_TRI, C_NEG, C_BO = 0, 128, 256, 384, 512
NCONST = 640


def host_consts():
    c = np.zeros((128, NCONST), np.float32)
    i = np.arange(128)
    c[:, C_ID:C_ID + 128] = np.eye(128, dtype=np.float32)
    c[:, C_ONES:C_ONES + 128] = 1.0
    c[:, C_TRI:C_TRI + 128] = (i[:, None] <= i[None, :]).astype(np.float32)
    c[:, C_NEG:C_NEG + 128] = np.where(i[:, None] > i[None, :], -30000.0, 0.0).astype(np.float32)
    c[:, C_BO:C_BO + 128] = ((i[:, None] // 64) == (i[None, :] // 64)).astype(np.float32)
    return c


def host_weights(w_in, a_pw_w, w_out, ffn_w_up, ffn_w_down):
    out = np.empty((128, 2 * WL), np.float32)
    for l in range(2):
        base = l * WL
        wi = w_in[l].reshape(8, 128, DIN)
        for i in range(9):
            o, s = POFF["win%d" % i]
            blk = np.empty((128, 2, 8, 128), np.float32)
            for j in range(2):
                nm, cc = WIN_FM[2 * i + j]
                c0 = ZC[nm] + 128 * cc
                blk[:, j] = wi[:, :, c0:c0 + 128].transpose(1, 0, 2)
            out[:, base + o:base + o + s] = blk.reshape(128, -1)
        o, s = POFF["pw"]
        out[:, base + o:base + o + s] = a_pw_w[l].reshape(2, 128, 256).transpose(1, 0, 2).reshape(128, -1)
        o, s = POFF["vfa"]
        out[:, base + o:base + o + s] = wi[:, :, 1024:1280].transpose(1, 0, 2).reshape(128, -1)
        o, s = POFF["vfb"]
        out[:, base + o:base + o + s] = wi[:, :, 1280:1544].transpose(1, 0, 2).reshape(128, -1)
        wo = w_out[l].reshape(8, 128, DM)
        for i in range(4):
            o, s = POFF["wout%d" % i]
            blk = np.empty((128, 2, 8, 128), np.float32)
            for j in range(2):
                c0 = 128 * (2 * i + j)
                blk[:, j] = wo[:, :, c0:c0 + 128].transpose(1, 0, 2)
            out[:, base + o:base + o + s] = blk.reshape(128, -1)
        wu = ffn_w_up[l].reshape(8, 128, 2 * DFF)
        for i in range(22):
            o, s = POFF["wup%d" % i]
            blk = np.empty((128, 2, 8, 128), np.float32)
            blk[:, 0] = wu[:, :, 128 * i:128 * i + 128].transpose(1, 0, 2)
            blk[:, 1] = wu[:, :, DFF + 128 * i:DFF + 128 * i + 128].transpose(1, 0, 2)
            out[:, base + o:base + o + s] = blk.reshape(128, -1)
        wd = ffn_w_down[l].reshape(22, 128, DM)
        for i in range(8):
            o, s = POFF["wdn%d" % i]
            out[:, base + o:base + o + s] = wd[:, :, 128 * i:128 * i + 128].transpose(1, 0, 2).reshape(128, -1)
    return out


def host_vecs(inp):
    v = np.zeros((128, 2 * NVEC), np.float32)

    def cols(a):
        return np.ascontiguousarray(a.reshape(-1, 128).T)
    for l in range(2):
        b = l * NVEC
        for nm in ["mix_pre_g", "mix_post_g", "ffn_pre_g", "ffn_post_g", "head_g"]:
            v[:, b + VEC[nm]:b + VEC[nm] + 8] = cols(inp[nm][l])
        dw = inp["a_dw_w"][l]
        v[:, b + VEC["dw_w"]:b + VEC["dw_w"] + 62] = dw.reshape(31, 2, 128).transpose(2, 0, 1).reshape(128, 62)
        v[:, b + VEC["dw_b"]:b + VEC["dw_b"] + 2] = cols(inp["a_dw_b"][l])
        v[:, b + VEC["ln_g"]:b + VEC["ln_g"] + 2] = cols(inp["a_ln_g"][l])
        v[:, b + VEC["ln_b"]:b + VEC["ln_b"] + 2] = cols(inp["a_ln_b"][l])
        sc = inp["sc_conv_w"][l]
        v[:, b + VEC["sc_w"]:b + VEC["sc_w"] + 6] = sc.reshape(3, 2, 128).transpose(2, 0, 1).reshape(128, 6)
        fc = inp["ffn_conv_w"][l]
        v[:, b + VEC["ffn_cw"]:b + VEC["ffn_cw"] + 132] = fc.reshape(3, 44, 128).transpose(2, 0, 1).reshape(128, 132)
    return v


class Rec:
    __slots__ = ("eng", "idx", "fn", "deps", "needs_inc", "waits", "clock", "sem", "semval", "cnt", "is_dma", "ph")


class Res:
    __slots__ = ("ws", "rs")

    def __init__(self):
        self.ws = {}
        self.rs = {}


class TRes:
    def __init__(self, nch, tiles=TT):
        self.tiles = tiles
        self.r = [[Res() for _ in tiles] for _ in range(nch)]

    def get(self, c, lo, hi):
        return [self.r[c][t] for t, (s, n) in enumerate(self.tiles) if s < hi and lo < s + n]

    def all(self):
        return [x for row in self.r for x in row]


class Sched:
    COMPUTE = ("pe", "act", "dve", "pool")

    def __init__(self, nc):
        self.nc = nc
        self.ins = {e: [] for e in ("pe", "act", "dve", "pool", "sp")}
        self.order = []
        self.streams = {}
        self.ph = ""

    def add(self, eng, fn, reads=(), writes=()):
        r = Rec()
        r.eng = eng
        r.fn = fn
        r.is_dma = False
        r.needs_inc = False
        r.sem = None
        r.ph = self.ph
        lst = self.ins[eng]
        r.idx = len(lst) + 1
        deps = set()
        key = eng
        for x in reads:
            deps.update(x.ws.values())
        for x in writes:
            deps.update(x.ws.values())
            deps.update(x.rs.values())
        for x in reads:
            x.rs[key] = r
        for x in writes:
            x.ws = {key: r}
            x.rs = {}
        deps.discard(r)
        r.deps = deps
        lst.append(r)
        self.order.append(r)
        return r

    def dma(self, stream, fn, reads=(), writes=(), nsem=4):
        st = self.streams.setdefault(stream, dict(recs=[], nsem=nsem, sems=None))
        r = Rec()
        r.eng = "sp"
        r.fn = fn
        r.is_dma = True
        r.needs_inc = True
        r.ph = self.ph
        lst = self.ins["sp"]
        r.idx = len(lst) + 1
        n = len(st["recs"])
        r.sem = (stream, n % st["nsem"])
        r.semval = 16 * (n // st["nsem"] + 1)
        deps = set()
        if n >= st["nsem"]:
            deps.add(st["recs"][n - st["nsem"]])
        key = ("d", id(r))
        for x in reads:
            deps.update(x.ws.values())
        for x in writes:
            deps.update(x.ws.values())
            deps.update(x.rs.values())
        for x in reads:
            x.rs[key] = r
        for x in writes:
            x.ws = {key: r}
            x.rs = {}
        r.deps = deps
        st["recs"].append(r)
        lst.append(r)
        self.order.append(r)
        return r

    def op(self, eng, method, reads=(), writes=(), **kw):
        r = self.add(eng, (lambda e: getattr(e, method)(**kw)), reads, writes)
        r.ph = r.ph + ":" + method + ":" + str(kw.get("func", kw.get("op", kw.get("op0", ""))))
        return r

    def dmaop(self, stream, out, in_, reads=(), writes=(), nsem=4):
        return self.dma(stream, (lambda e: e.dma_start(out=out, in_=in_)), reads, writes, nsem)

    def finalize(self):
        run = {e: {} for e in self.ins}
        for r in self.order:
            clock = run[r.eng]
            waits = []
            for d in sorted(r.deps, key=lambda d: (d.eng, d.idx)):
                if d.is_dma:
                    k = ("d", d.sem)
                    if clock.get(k, 0) < d.semval:
                        waits.append(d)
                        for kk, vv in d.clock.items():
                            if clock.get(kk, 0) < vv:
                                clock[kk] = vv
                        clock[k] = d.semval
                elif d.eng == r.eng:
                    if r.eng in ("pe", "sp") or r.idx - d.idx > 2:
                        continue
                    d.needs_inc = True
                    waits.append(d)
                else:
                    if clock.get(d.eng, 0) < d.idx:
                        d.needs_inc = True
                        waits.append(d)
                        for kk, vv in d.clock.items():
                            if clock.get(kk, 0) < vv:
                                clock[kk] = vv
            r.waits = waits
            if r.is_dma:
                r.clock = dict(clock)
            else:
                clock[r.eng] = r.idx
                r.clock = dict(clock)
        for e in self.COMPUTE:
            c = 0
            for r in self.ins[e]:
                if r.needs_inc:
                    c += 1
                r.cnt = c

    def emit(self, eng_name, eng, esems, dsems):
        for r in self.ins[eng_name]:
            for d in r.waits:
                if d.is_dma:
                    eng.wait_ge(dsems[d.sem], d.semval)
                else:
                    eng.wait_ge(esems[d.eng], d.cnt)
            ins = r.fn(eng)
            if r.is_dma:
                ins.then_inc(dsems[r.sem], 16)
            elif r.needs_inc:
                ins.then_inc(esems[r.eng], 1)


class Arena:
    def __init__(self, tensor, nbytes):
        self.t = tensor
        self.nbytes = nbytes
        self.off = 0

    def alloc(self, nbytes):
        nbytes = (nbytes + 63) // 64 * 64
        o = self.off
        self.off += nbytes
        assert self.off <= self.nbytes, (self.off, self.nbytes)
        return o

    def f32(self, off, n):
        return self.t[:, off // 4: off // 4 + n]

    def bf16(self, off, n):
        v = self.t[:, off // 4: off // 4 + (n + 1) // 2].bitcast(BF16)
        return v[:, 0:n]


CFG = dict(nseq=4, nlayer=2, stop=None, taps=())


class Pend:
    def __init__(self):
        self.p = {}

    def retire(self, res_list):
        for r in res_list:
            for dct in (r.ws, r.rs):
                for k, rec in dct.items():
                    o = self.p.get(k)
                    if o is None or o.idx < rec.idx:
                        self.p[k] = rec
            r.ws = {}
            r.rs = {}

    def new(self):
        r = Res()
        r.rs = dict(self.p)
        return r

    def tres(self, nch, tiles=TT):
        t = TRes.__new__(TRes)
        t.tiles = tiles
        t.r = [[self.new() for _ in tiles] for _ in range(nch)]
        return t


def build(cfg):
    NSEQ = cfg["nseq"]
    NLAYER = cfg["nlayer"]
    STOP = cfg["stop"]
    nc = bass.Bass("TRN2", target_bir_lowering=False, dynamic_dma_scratch_size=64)
    xT = nc.dram_tensor("xT", [NSEQ, 128, 8, SEQ], F32, kind="ExternalInput").ap()
    metaT = nc.dram_tensor("metaT", [128, 8, NMETA], F32, kind="ExternalInput").ap()
    wf = nc.dram_tensor("wf", [128, 2 * WL], F32, kind="ExternalInput").ap()
    vecs_d = nc.dram_tensor("vecs", [128, 2 * NVEC], F32, kind="ExternalInput").ap()
    bfg_d = nc.dram_tensor("bfg", [128, 16], F32, kind="ExternalInput").ap()
    consts_d = nc.dram_tensor("consts", [128, NCONST], F32, kind="ExternalInput").ap()
    outT = nc.dram_tensor("outT", [NSEQ, 128, 8, SEQ], F32, kind="ExternalOutput").ap()
    wb = nc.dram_tensor("wb", [128, 2 * WL], BF16, kind="Internal").ap()
    taps = {}
    for nm, shape, dt in cfg["taps"]:
        taps[nm] = nc.dram_tensor("tap_" + nm, list(shape), BF16 if dt == "bf16" else F32, kind="ExternalOutput").ap()

    S = Sched(nc)
    ARENA_BYTES = 229056
    from contextlib import ExitStack
    es = ExitStack()
    arena_t = es.enter_context(nc.sbuf_tensor("arena", [128, ARENA_BYTES // 4], F32))
    A = Arena(arena_t, ARENA_BYTES)
    psum = [es.enter_context(nc.psum_tensor("ps%d" % i, [128, 512], F32)) for i in range(8)]
    PS = [Res() for _ in range(8)]
    psrot = dict(i=0)

    def nb(banks=(0, 1, 2, 3, 4, 5)):
        b = banks[psrot["i"] % len(banks)]
        psrot["i"] += 1
        return b

    o_h = A.alloc(8 * L * 4)
    o_w = A.alloc(NSLOT * SLOT * 2)
    o_rstd = A.alloc(8320)
    o_vec = A.alloc(2 * NVEC * 4)
    o_bfg = A.alloc(16 * 4)
    o_c32 = A.alloc(NCONST * 4)
    o_c16 = A.alloc(NCONST * 2)
    NT32, NT16 = 5, 3
    o_t32 = A.alloc(NT32 * 416 * 4)
    o_t16 = A.alloc(NT16 * 416 * 2)
    o_eps = A.alloc(64)
    o_tr = A.alloc(2 * 416 * 4)
    o_P = A.off
    P_BYTES = ARENA_BYTES - o_P
    assert P_BYTES >= 100240, P_BYTES
    o_u = o_P
    o_qk = o_u + 8 * L * 2
    o_v = o_qk + 8 * L * 2
    o_yb = o_v + 17 * 520 * 2
    o_sp = o_yb + 4 * L * 2
    o_dg1 = o_sp
    o_qz = o_dg1 + 31 * 128 * 2
    assert o_qz + 2 * 1024 <= ARENA_BYTES, (o_qz, ARENA_BYTES)

    hT = A.f32(o_h, 8 * L).rearrange("p (c t) -> p c t", c=8)
    rstd = A.f32(o_rstd, L)
    vec = A.f32(o_vec, 2 * NVEC)
    bfg = A.f32(o_bfg, 16)
    c32 = A.f32(o_c32, NCONST)
    c16 = A.bf16(o_c16, NCONST)
    wslot = [A.bf16(o_w + i * SLOT * 2, SLOT) for i in range(NSLOT)]
    tmp32 = [A.f32(o_t32 + i * 416 * 4, 416) for i in range(NT32)]
    tmp16 = [A.bf16(o_t16 + i * 416 * 2, 416) for i in range(NT16)]
    epsv = A.f32(o_eps, 8)
    tmpr = [A.f32(o_tr + i * 416 * 4, 416) for i in range(2)]
    R_tmpr = [Res(), Res()]
    epsc, lnepsc, onec = epsv[:, 0:1], epsv[:, 1:2], epsv[:, 2:3]

    PD = Pend()
    R_h = TRes(8)
    R_const = Res()
    R_wb = Res()
    R_wslot = [Res() for _ in range(NSLOT)]
    R_tmp32 = [Res() for _ in range(NT32)]
    R_tmp16 = [Res() for _ in range(NT16)]
    rot = dict(t32=0, t16=0, w=0, tr=0)

    def vcol(l, name, j=0):
        c = l * NVEC + VEC[name] + j
        return vec[:, c:c + 1]

    def MM(out, lhsT, rhs, start, stop, reads, writes, **kw):
        S.op("pe", "matmul", reads, writes, out=out, lhsT=lhsT, rhs=rhs, start=start, stop=stop, **kw)

    def ACT(out, in_, func, reads, writes, bias=None, scale=None):
        kw = dict(out=out, in_=in_, func=func)
        if bias is not None:
            kw["bias"] = bias
        if scale is not None:
            kw["scale"] = scale
        S.op("act", "activation", reads, writes, **kw)

    def STT(eng, out, in0, scalar, in1, op0, op1, reads, writes):
        S.op(eng, "scalar_tensor_tensor", reads, writes, out=out, in0=in0, scalar=scalar, in1=in1, op0=op0, op1=op1)

    def TTOP(eng, out, in0, in1, op, reads, writes):
        S.op(eng, "tensor_tensor", reads, writes, out=out, in0=in0, in1=in1, op=op)

    def TS(eng, out, in0, s1, op0, reads, writes, s2=None, op1=None):
        kw = dict(out=out, in0=in0, scalar1=s1, scalar2=s2, op0=op0)
        if op1 is not None:
            kw["op1"] = op1
        S.op(eng, "tensor_scalar", reads, writes, **kw)

    def CP(eng, out, in_, reads, writes):
        if eng == "act":
            S.op("act", "activation", reads, writes, out=out, in_=in_, func=AF.Copy)
        else:
            S.op(eng, "tensor_copy", reads, writes, out=out, in_=in_)

    def RSQRT(out, in_, scale, eps_ap, reads, writes):
        np_ = in_.shape[0]
        tm, rtm = tmpr[rot["tr"] % 2], R_tmpr[rot["tr"] % 2]
        rot["tr"] += 1
        n_ = in_.shape[-1]
        tv = tm[0:np_, 0:n_]
        if eps_ap is None:
            ACT(tv, in_, AF.Ln, list(reads), [rtm], scale=scale)
        else:
            ACT(tv, in_, AF.Ln, list(reads) + [R_const], [rtm], bias=eps_ap, scale=scale)
        ACT(out, tv, AF.Exp, [rtm], writes, scale=-0.5)

    def RECIP(out, in_, reads, writes):
        S.op("dve", "reciprocal", reads, writes, out=out, in_=in_)

    def MEMSET(eng, ap, val, writes):
        S.op(eng, "memset", (), writes, ap=ap, constant=val)

    S.dmaop("misc", c32, consts_d, writes=[R_const])
    S.dmaop("misc", vec, vecs_d, writes=[R_const])
    S.dmaop("misc", bfg, bfg_d, writes=[R_const])
    CP("dve", c16, c32, [R_const], [R_const])
    MEMSET("dve", epsv[:, 0:1], RMS_EPS, [R_const])
    MEMSET("dve", epsv[:, 1:2], LN_EPS, [R_const])
    MEMSET("dve", epsv[:, 2:3], 1.0, [R_const])
    MEMSET("dve", epsv[:, 3:4], 0.0, [R_const])
    ident16 = c16[:, C_ID:C_ID + 128]
    ones16 = c16[:, C_ONES:C_ONES + 128]
    negtri16 = c16[:, C_NEG:C_NEG + 128]
    bo16 = c16[:, C_BO:C_BO + 128]
    ones32 = c32[:, C_ONES:C_ONES + 128]
    tri32 = c32[:, C_TRI:C_TRI + 128]

    dgd = nc.dram_tensor("dgd", [128, 2 * 2 * 3968], BF16, kind="Internal").ap()
    R_dgd = Res()
    dg1 = A.bf16(o_dg1, 31 * 128).rearrange("p (k m) -> p k m", k=31)
    R_dg1 = Res()
    qz = [A.bf16(o_qz + par * 1024, 512) for par in range(2)]
    R_qz = [None, None]
    dgst = [A.bf16(o_P + i * 7936, 3968).rearrange("p (k m) -> p k m", k=31) for i in range(2)]
    R_dgst = [PD.new(), PD.new()]
    for l_ in range(NLAYER):
        for cc_ in range(2):
            i_ = (l_ * 2 + cc_) % 2
            for k in range(31):
                TS("dve", dgst[i_][:, k, :], ident16, vcol(l_, "dw_w", 2 * k + cc_), ALU.mult, [R_const], [R_dgst[i_]])
            S.dmaop("dgo", dgd[:, (l_ * 2 + cc_) * 3968:(l_ * 2 + cc_ + 1) * 3968], dgst[i_].rearrange("p k m -> p (k m)"), reads=[R_dgst[i_]], writes=[R_dgd], nsem=2)
    PD.retire(R_dgst)
    NSTG = 4
    st32 = [A.f32(o_P + i * PRE_BLK * 4, PRE_BLK) for i in range(NSTG)]
    st16 = [A.bf16(o_P + NSTG * PRE_BLK * 4 + i * PRE_BLK * 2, PRE_BLK) for i in range(NSTG)]
    assert NSTG * PRE_BLK * 6 <= P_BYTES
    R_st32 = [PD.new() for _ in range(NSTG)]
    R_st16 = [PD.new() for _ in range(NSTG)]
    nblk_used = (NLAYER * WL + PRE_BLK - 1) // PRE_BLK
    for b in range(nblk_used + 2):
        if b < nblk_used:
            i = b % NSTG
            S.dmaop("prein", st32[i], wf[:, b * PRE_BLK:(b + 1) * PRE_BLK], writes=[R_st32[i]], nsem=NSTG)
        if 0 <= b - 1 < nblk_used:
            i = (b - 1) % NSTG
            CP(("dve", "act")[(b - 1) % 2], st16[i], st32[i], [R_st32[i]], [R_st16[i]])
        if 0 <= b - 2 < nblk_used:
            i = (b - 2) % NSTG
            S.dmaop("preout", wb[:, (b - 2) * PRE_BLK:(b - 1) * PRE_BLK], st16[i], reads=[R_st16[i]], writes=[R_wb], nsem=NSTG)
    PD.retire(R_st32 + R_st16)

    def wload(l, name):
        o, s = POFF[name]
        i = rot["w"] % NSLOT
        rot["w"] += 1
        S.dmaop("w", wslot[i][:, 0:s], wb[:, l * WL + o: l * WL + o + s], reads=[R_wb], writes=[R_wslot[i]], nsem=NSLOT)
        return wslot[i], R_wslot[i]

    def t32():
        i = rot["t32"] % NT32
        rot["t32"] += 1
        return tmp32[i], R_tmp32[i]

    def t16():
        i = rot["t16"] % NT16
        rot["t16"] += 1
        return tmp16[i], R_tmp16[i]

    def tap(name, src_ap, reads):
        if name in taps:
            S.dmaop("tap", taps[name], src_ap, reads=reads, nsem=1)

    def load_seq(s):
        for t, (t0, n) in enumerate(TT):
            lo, hi = max(t0, NMETA), t0 + n
            S.dmaop("xin", hT[:, :, lo:hi], xT[s, :, :, lo - NMETA:hi - NMETA], writes=[R_h.r[c][t] for c in range(8)], nsem=4)
            if t == 0:
                S.dmaop("xin", hT[:, :, 0:NMETA], metaT, writes=[R_h.r[c][0] for c in range(8)], nsem=4)

    def store_seq(s, tiles):
        for t in tiles:
            t0, n = TT[t]
            lo, hi = max(t0, NMETA), t0 + n
            S.dmaop("xout", outT[s, :, :, lo - NMETA:hi - NMETA], hT[:, :, lo:hi], reads=[R_h.r[c][t] for c in range(8)], nsem=4)

    R_rstd = dict(r=None)

    def prenorm(l, gname, tiles, u_fn, Ru):
        for t in tiles:
            t0, n = TT[t]
            pb = nb()
            for c in range(8):
                sq, rsq = t16()
                ACT(sq[:, 0:n], hT[:, c, t0:t0 + n], AF.Square, [R_h.r[c][t]], [rsq])
                MM(psum[pb][:, 0:n], ones16, sq[:, 0:n], c == 0, c == 7, [rsq, R_const], [PS[pb]])
            RSQRT(rstd[:, t0:t0 + n], psum[pb][:, 0:n], 1.0 / DM, epsc, [PS[pb]], [R_rstd["r"].r[0][t]])
            for c in range(8):
                STT("dve", u_fn(c, t0, n), hT[:, c, t0:t0 + n], vcol(l, gname, c), rstd[:, t0:t0 + n], ALU.mult, ALU.mult,
                    [R_h.r[c][t], R_rstd["r"].r[0][t], R_const], [Ru.r[c][t]])

    def prenorm_stats(tiles):
        for t in tiles:
            t0, n = TT[t]
            pb = nb()
            for c in range(8):
                sq, rsq = t16()
                ACT(sq[:, 0:n], hT[:, c, t0:t0 + n], AF.Square, [R_h.r[c][t]], [rsq])
                MM(psum[pb][:, 0:n], ones16, sq[:, 0:n], c == 0, c == 7, [rsq, R_const], [PS[pb]])
            RSQRT(rstd[:, t0:t0 + n], psum[pb][:, 0:n], 1.0 / DM, epsc, [PS[pb]], [R_rstd["r"].r[0][t]])

    def prenorm_apply_steps(l, gname, tiles, u_fn, Ru):
        steps = []
        for t in tiles:
            t0, n = TT[t]
            for c in range(8):
                def f(c=c, t=t, t0=t0, n=n):
                    STT("dve", u_fn(c, t0, n), hT[:, c, t0:t0 + n], vcol(l, gname, c), rstd[:, t0:t0 + n], ALU.mult, ALU.mult,
                        [R_h.r[c][t], R_rstd["r"].r[0][t], R_const], [Ru.r[c][t]])
                steps.append(f)
        return steps

    def hn_a(l, ychunk, src, src_reads, n, dst, dst_writes):
        sq, rsq = t16()
        ACT(sq[:, 0:n], src, AF.Square, src_reads, [rsq])
        return (l, ychunk, src, src_reads, n, dst, dst_writes, sq, rsq)

    def hn_b(st):
        l, ychunk, src, src_reads, n, dst, dst_writes, sq, rsq = st
        pb = nb()
        MM(psum[pb][:, 0:n], bo16, sq[:, 0:n], True, True, [rsq, R_const], [PS[pb]])
        sd, rsd = t32()
        RSQRT(sd[:, 0:n], psum[pb][:, 0:n], 1.0 / 64, epsc, [PS[pb]], [rsd])
        STT("dve", dst, src, vcol(l, "head_g", ychunk), sd[:, 0:n], ALU.mult, ALU.mult, src_reads + [rsd, R_const], dst_writes)

    def mixer(s, l, pre_hook=None, rstd4=None):
        uT = A.bf16(o_u, 8 * L).rearrange("p (c t) -> p c t", c=8)
        R_u = PD.tres(8)
        R_rstd["r"] = PD.tres(1)
        yB = A.bf16(o_yb, 4 * L).rearrange("p (c t) -> p c t", c=4)
        R_yb = PD.tres(4)
        S.ph = "m_prenorm"
        if rstd4 is not None:
            R_rstd["r"].r[0][4] = rstd4
        prenorm(l, "mix_pre_g", range(4), (lambda c, t0, n: uT[:, c, t0:t0 + n]), R_u)
        if pre_hook is not None:
            S.ph = "f_tail"
            pre_hook()
            S.ph = "m_prenorm"
        prenorm(l, "mix_pre_g", [4], (lambda c, t0, n: uT[:, c, t0:t0 + n]), R_u)
        S.ph = "m_conv"
        if STOP == "prenorm":
            tap("u", uT, R_u.all())
            return
        oc = o_qk
        apad = A.bf16(oc, 2 * (L + 30)).rearrange("p (c t) -> p c t", c=2)
        oc += 2 * (L + 30) * 2
        oc = (oc + 63) // 64 * 64
        cv = A.f32(oc, 2 * L).rearrange("p (c t) -> p c t", c=2)
        oc += 2 * L * 4
        a2 = A.bf16(oc, 2 * L).rearrange("p (c t) -> p c t", c=2)
        oc += 2 * L * 2
        cxpad = A.f32(oc, L + 2)
        oc += (L + 2) * 4
        oc = (oc + 63) // 64 * 64
        dg = A.bf16(o_rstd, 31 * 128).rearrange("p (k m) -> p k m", k=31)
        PD.retire(R_rstd["r"].all())
        lnt = [A.f32(oc + i * 416 * 4, 416) for i in range(3)]
        oc += 3 * 416 * 4
        assert oc <= o_yb, (oc, o_yb)
        R_apad = PD.tres(2)
        R_apad0 = PD.new()
        R_cv = PD.tres(2)
        R_a2 = PD.tres(2)
        R_cx = PD.tres(1)
        R_cx0 = PD.new()
        R_dg = PD.new()
        R_lnt = [PD.new() for _ in range(3)]
        MEMSET("dve", apad[:, :, 0:30], 0.0, [R_apad0])
        MEMSET("dve", cxpad[:, 0:2], 0.0, [R_cx0])

        def fm_mm(wv, rw, j, t, pb, halo=0):
            t0, n = TT[t]
            for k in range(8):
                MM(psum[pb][:, 0:n], wv[:, j, k, :], uT[:, k, t0:t0 + n], k == 0, k == 7, [rw, R_u.r[k][t]], [PS[pb]])

        dgs = [dg, dg1]
        R_dg1 = PD.new()
        R_dgs = [R_dg, R_dg1]
        for cc_ in range(2):
            S.dmaop("dg", dgs[cc_].rearrange("p k m -> p (k m)"), dgd[:, (l * 2 + cc_) * 3968:(l * 2 + cc_ + 1) * 3968], reads=[R_dgd], writes=[R_dgs[cc_]], nsem=2)
        for cc in range(2):
            wt, rw = wload(l, "win%d" % cc)
            wv = wt[:, 0:2048].rearrange("p (j k m) -> p j k m", j=2, k=8)
            for t, (t0, n) in enumerate(TT):
                pv, pg = nb(), nb()
                fm_mm(wv, rw, 0, t, pv)
                fm_mm(wv, rw, 1, t, pg)
                sg, rsg = t32()
                ACT(sg[:, 0:n], psum[pg][:, 0:n], AF.Sigmoid, [PS[pg]], [rsg])
                TTOP("dve", apad[:, cc, 30 + t0:30 + t0 + n], psum[pv][:, 0:n], sg[:, 0:n], ALU.mult, [PS[pv], rsg], [R_apad.r[cc][t]])
            if cc == 0:
                pass
            for t, (t0, n) in enumerate(TT):
                pb = nb()
                rd = R_apad.get(cc, t0 - 30, t0 + n) + [R_dgs[cc], R_apad0]
                for k in range(31):
                    MM(psum[pb][:, 0:n], dgs[cc][:, k, :], apad[:, cc, t0 + k:t0 + k + n], k == 0, k == 30, rd, [PS[pb]])
                ACT(cv[:, cc, t0:t0 + n], psum[pb][:, 0:n], AF.Identity, [PS[pb], R_const], [R_cv.r[cc][t]], bias=vcol(l, "dw_b", cc))
        tap("cv", cv, R_cv.all())
        S.ph = "m_ln_pw"
        for t, (t0, n) in enumerate(TT):
            pm, pq = nb(), nb()
            for cc in range(2):
                MM(psum[pm][:, 0:n], ones32, cv[:, cc, t0:t0 + n], cc == 0, cc == 1, [R_cv.r[cc][t], R_const], [PS[pm]])
            for cc in range(2):
                sq, rsq = t32()
                ACT(sq[:, 0:n], cv[:, cc, t0:t0 + n], AF.Square, [R_cv.r[cc][t]], [rsq])
                MM(psum[pq][:, 0:n], ones32, sq[:, 0:n], cc == 0, cc == 1, [rsq, R_const], [PS[pq]])
            mu, m2, rs_ = lnt[0], lnt[1], lnt[2]
            ACT(mu[:, 0:n], psum[pm][:, 0:n], AF.Copy, [PS[pm]], [R_lnt[0]], scale=1.0 / 256)
            TTOP("dve", m2[:, 0:n], mu[:, 0:n], mu[:, 0:n], ALU.mult, [R_lnt[0]], [R_lnt[1]])
            STT("dve", m2[:, 0:n], psum[pq][:, 0:n], 1.0 / 256, m2[:, 0:n], ALU.mult, ALU.subtract, [PS[pq], R_lnt[1]], [R_lnt[1]])
            RSQRT(rs_[:, 0:n], m2[:, 0:n], 1.0, lnepsc, [R_lnt[1]], [R_lnt[2]])
            for cc in range(2):
                xc, rxc = t32()
                TTOP("dve", xc[:, 0:n], cv[:, cc, t0:t0 + n], mu[:, 0:n], ALU.subtract, [R_cv.r[cc][t], R_lnt[0]], [rxc])
                STT("dve", xc[:, 0:n], xc[:, 0:n], vcol(l, "ln_g", cc), rs_[:, 0:n], ALU.mult, ALU.mult, [rxc, R_lnt[2], R_const], [rxc])
                ACT(a2[:, cc, t0:t0 + n], xc[:, 0:n], AF.Silu, [rxc, R_const], [R_a2.r[cc][t]], bias=vcol(l, "ln_b", cc))
        wt, rw = wload(l, "pw")
        wv = wt[:, 0:512].rearrange("p (k m) -> p k m", k=2)
        pend = None
        for oc_ in range(2):
            for t, (t0, n) in enumerate(TT):
                pb = nb()
                for k in range(2):
                    MM(psum[pb][:, 0:n], wv[:, k, oc_ * 128:(oc_ + 1) * 128], a2[:, k, t0:t0 + n], k == 0, k == 1,
                       [rw, R_a2.r[k][t]], [PS[pb]])
                st = hn_a(l, 4 + oc_, psum[pb][:, 0:n], [PS[pb]], n, yB[:, oc_, t0:t0 + n], [R_yb.r[oc_][t]])
                if pend is not None:
                    hn_b(pend)
                pend = st
        hn_b(pend)
        S.ph = "m_sc"
        w2, rw2 = wload(l, "win2")
        w3, rw3 = wload(l, "win3")
        w4, rw4 = wload(l, "win4")
        v2 = w2[:, 0:2048].rearrange("p (j k m) -> p j k m", j=2, k=8)
        v3 = w3[:, 0:2048].rearrange("p (j k m) -> p j k m", j=2, k=8)
        v4 = w4[:, 0:2048].rearrange("p (j k m) -> p j k m", j=2, k=8)
        scw = [[(v2, rw2, 0), (v2, rw2, 1), (v3, rw3, 0)], [(v3, rw3, 1), (v4, rw4, 0), (v4, rw4, 1)]]
        pend = None
        for cc in range(2):
            for t, (t0, n) in enumerate(TT):
                pc, px, pbb = nb(), nb(), nb()
                fm_mm(scw[cc][0][0], scw[cc][0][1], scw[cc][0][2], t, pc)
                fm_mm(scw[cc][1][0], scw[cc][1][1], scw[cc][1][2], t, px)
                fm_mm(scw[cc][2][0], scw[cc][2][1], scw[cc][2][2], t, pbb)
                xs, rxs = t32()
                CP("act", xs[:, 0:n], psum[px][:, 0:n], [PS[px]], [rxs])
                TTOP("dve", cxpad[:, 2 + t0:2 + t0 + n], psum[pc][:, 0:n], xs[:, 0:n], ALU.mult, [PS[pc], rxs], [R_cx.r[0][t]])
                acc, racc = t32()
                rd = R_cx.get(0, t0 - 2, t0 + n) + [R_cx0, R_const]
                TS("dve", acc[:, 0:n], cxpad[:, t0:t0 + n], vcol(l, "sc_w", 0 * 2 + cc), ALU.mult, rd, [racc])
                STT("dve", acc[:, 0:n], cxpad[:, t0 + 1:t0 + 1 + n], vcol(l, "sc_w", 1 * 2 + cc), acc[:, 0:n], ALU.mult, ALU.add, rd + [racc], [racc])
                STT("dve", acc[:, 0:n], cxpad[:, t0 + 2:t0 + 2 + n], vcol(l, "sc_w", 2 * 2 + cc), acc[:, 0:n], ALU.mult, ALU.add, rd + [racc], [racc])
                TTOP("dve", acc[:, 0:n], psum[pbb][:, 0:n], acc[:, 0:n], ALU.mult, [PS[pbb], racc], [racc])
                st = hn_a(l, 6 + cc, acc[:, 0:n], [racc], n, yB[:, 2 + cc, t0:t0 + n], [R_yb.r[2 + cc][t]])
                if pend is not None:
                    hn_b(pend)
                pend = st
        hn_b(pend)
        tap("yb", yB, R_yb.all())
        if STOP == "conv":
            return
        S.ph = "m_qkv"
        PD.retire(R_apad.all() + [R_apad0, R_cx0, R_dg, R_dg1] + R_cv.all() + R_a2.all() + R_cx.all() + R_lnt)
        for par in range(2):
            R_qz[par] = PD.new()
            MEMSET("dve", qz[par], 0.0, [R_qz[par]])
        qk = A.bf16(o_qk, 8 * L).rearrange("p (c t) -> p c t", c=8)
        R_qk = PD.tres(8)
        vaug = A.bf16(o_v, 17 * 520).rearrange("p (b h d) -> p b h d", b=17, h=8)
        R_v = [PD.new() for _ in range(17)]
        R_vones = PD.new()
        MEMSET("dve", vaug[:, :, :, 64:65], 1.0, [R_vones])
        for i in range(4):
            wt, rw = wload(l, "win%d" % (5 + i))
            wv = wt[:, 0:2048].rearrange("p (j k m) -> p j k m", j=2, k=8)
            for j in range(2):
                ch = 2 * i + j
                for t, (t0, n) in enumerate(TT):
                    pb = nb()
                    fm_mm(wv, rw, j, t, pb)
                    CP("act", qk[:, ch, t0:t0 + n], psum[pb][:, 0:n], [PS[pb]], [R_qk.r[ch][t]])
        wa, rwa = wload(l, "vfa")
        wbb, rwb = wload(l, "vfb")
        va = wa[:, 0:2048].rearrange("p (k m) -> p k m", k=8)
        vb = wbb[:, 0:2112].rearrange("p (k m) -> p k m", k=8)
        R_fl = PD.new()
        og = o_rstd
        fl = A.f32(og, 136).rearrange("p (b h) -> p b h", b=17)
        og += 136 * 4
        lf = A.f32(og, 136).rearrange("p (b h) -> p b h", b=17)
        og += 136 * 4
        Cx = A.f32(og, 18 * 8).rearrange("p (b h) -> p b h", b=18)
        og += 18 * 8 * 4
        Gt = A.f32(og, 136).rearrange("p (b h) -> p b h", b=17)
        og += 136 * 4
        biasv = A.f32(og, 5 * 136).rearrange("p (i b h) -> p i b h", i=5, b=17)
        og += 5 * 136 * 4
        assert og <= o_rstd + 8320
        MEMSET("dve", fl, 0.0, [R_fl])
        for b, (b0, nt) in enumerate(BLK):
            pa, pbk = nb(), nb()
            rdu = []
            for k in range(8):
                rdu += R_u.get(k, b0, b0 + nt)
            for k in range(8):
                MM(psum[pa][0:nt, 0:256], uT[:, k, b0:b0 + nt], va[:, k, :], k == 0, k == 7, [rwa] + rdu, [PS[pa]])
            for k in range(8):
                MM(psum[pbk][0:nt, 0:264], uT[:, k, b0:b0 + nt], vb[:, k, :], k == 0, k == 7, [rwb] + rdu, [PS[pbk]])
            CP("act", vaug[0:nt, b, 0:4, 0:64], psum[pa][0:nt, 0:256].rearrange("p (h d) -> p h d", h=4), [PS[pa]], [R_v[b]])
            CP("dve", vaug[0:nt, b, 4:8, 0:64], psum[pbk][0:nt, 0:256].rearrange("p (h d) -> p h d", h=4), [PS[pbk]], [R_v[b]])
            TTOP("dve", fl[0:nt, b, :], psum[pbk][0:nt, 256:264], bfg[0:nt, l * 8:l * 8 + 8], ALU.add, [PS[pbk], R_const], [R_fl])
        flf = fl.rearrange("p b h -> p (b h)")
        lff = lf.rearrange("p b h -> p (b h)")
        ACT(lff, flf, AF.Exp, [R_fl], [R_fl], scale=-1.0)
        ACT(lff, lff, AF.Ln, [R_fl, R_const], [R_fl], bias=onec, scale=1.0)
        pc_ = nb()
        MM(psum[pc_][:, 0:136], tri32, lff, True, True, [R_fl, R_const], [PS[pc_]])
        MM(psum[pc_][:, 136:272], ones32, lff, True, True, [R_fl, R_const], [PS[pc_]])
        MEMSET("dve", Cx[:, 0, :], 0.0, [R_fl])
        for b in range(16):
            TTOP("dve", Cx[:, b + 1, :], Cx[:, b, :], psum[pc_][:, 136 + 8 * b:136 + 8 * b + 8], ALU.add, [PS[pc_], R_fl], [R_fl])
        TTOP("dve", Gt.rearrange("p b h -> p (b h)"), psum[pc_][:, 0:136], Cx[:, 0:17, :].rearrange("p b h -> p (b h)"), ALU.add,
             [PS[pc_], R_fl], [R_fl])
        for i in range(5):
            nbk = 4 * i + 4 if i < 4 else 17
            e_i = 4 * i + 4 if i < 4 else 16
            TTOP("dve", biasv[:, i, 0:nbk, :], Gt[:, 0:nbk, :], Cx[:, e_i:e_i + 1, :].broadcast_to([128, nbk, 8]), ALU.subtract, [R_fl], [R_fl])
        tap("gt", Gt.rearrange("p b h -> p (b h)"), [R_fl])
        if STOP == "qkv":
            tap("qk", qk, R_qk.all())
            tap("v", vaug.rearrange("p b h d -> p (b h d)"), R_v)
            return
        S.ph = "m_attn"
        PD.retire(R_u.all())
        oa = o_u
        yA = A.bf16(oa, 4 * L).rearrange("p (c t) -> p c t", c=4)
        oa += 4 * L * 2
        NPT = 4
        pT = [A.bf16(oa + i * 1024, 512) for i in range(NPT)]
        oa += NPT * 1024
        ost = A.f32(oa, 4 * 8 * 65).rearrange("p (r h d) -> p r h d", r=4, h=8)
        oa += 4 * 8 * 65 * 4
        ynt = [A.bf16(oa + i * 1024, 512) for i in range(4)]
        oa += 4096
        SSb, RDb, TTb, RRb, SCb = [A.f32(og + 128 * r_, 32).rearrange("p (r h) -> p r h", r=4) for r_ in range(5)]
        og += 640
        sqt = A.f32(og, 256).rearrange("p (r d) -> p r d", r=4)
        og += 1024
        assert og <= o_rstd + 8320
        assert oa <= o_qk, (oa, o_qk)
        R_yA = PD.tres(4)
        R_pT = [PD.new() for _ in range(NPT)]
        R_ost = PD.new()
        R_sqt = PD.new()
        R_ynt = [PD.new() for _ in range(4)]
        R_sm = PD.new()
        SB, OB, TB = (0, 1, 2, 6), (3, 4), 5
        NSB = len(SB)
        steps = []
        for i in range(5):
            q0, qn = (512 * i, 512) if i < 4 else (2048, 16)
            jmax = 4 * i + 3 if i < 4 else 16
            for h in range(8):
                for j in range(jmax + 1):
                    steps.append((i, h, j, q0, qn, jmax))
        LA = 3
        nsteps = len(steps)
        ynrot = dict(i=0)
        kzrot = [0, 0]

        def stage_q(i, h):
            q0, qn = (512 * i, 512) if i < 4 else (2048, 16)
            kc, po = h // 2, (h % 2) * 64
            CP("dve", qz[h % 2][po:po + 64, 0:qn], qk[po:po + 64, kc, q0:q0 + qn], R_qk.get(kc, q0, q0 + qn), [R_qz[h % 2]])

        def emit_S(n):
            i, h, j, q0, qn, jmax = steps[n]
            kc, po = h // 2, (h % 2) * 64
            b0, kb = BLK[j]
            lo = max(q0, b0)
            N = q0 + qn - lo
            diag = (b0 >= q0)
            sb = SB[n % NSB]
            rq = R_qk.get(kc, lo, q0 + qn)
            rk = R_qk.get(4 + kc, b0, b0 + kb)
            par = h % 2
            if j == 0:
                if (i, h) == (0, 0):
                    stage_q(0, 0)
                nxt = (i, h + 1) if h < 7 else ((i + 1, 0) if i < 4 else None)
                if nxt is not None:
                    stage_q(*nxt)
            MM(psum[sb][0:kb, 0:N], qk[:, 4 + kc, b0:b0 + kb], qz[par][:, lo - q0:lo - q0 + N], True, not diag, rk + [R_qz[par]], [PS[sb]])
            if diag:
                MM(psum[sb][0:kb, 0:kb], ident16[0:kb, 0:kb], negtri16[0:kb, 0:kb], False, True, [R_const], [PS[sb]])
            ACT(pT[n % NPT][0:kb, 0:N], psum[sb][0:kb, 0:N], AF.Exp, [PS[sb], R_fl], [R_pT[n % NPT]], bias=biasv[0:kb, i, j, h:h + 1], scale=0.125)

        def emit_PV(n):
            i, h, j, q0, qn, jmax = steps[n]
            b0, kb = BLK[j]
            lo = max(q0, b0)
            ob = OB[h % 2]
            if i < 4:
                for r in range(4):
                    q = 4 * i + r
                    if q < j:
                        continue
                    off = 128 * q - lo
                    MM(psum[ob][:, r * 65:(r + 1) * 65], pT[n % NPT][0:kb, off:off + 128], vaug[0:kb, j, h, :], (j == 0 and r == 0), j == q,
                       [R_pT[n % NPT], R_v[j], R_vones], [PS[ob]], skip_group_check=True)
            else:
                MM(psum[ob][0:16, 0:65], pT[n % NPT][0:kb, 0:16], vaug[0:kb, j, h, :], j == 0, j == 16, [R_pT[n % NPT], R_v[j], R_vones], [PS[ob]])
            if j == jmax:
                if i < 4:
                    CP("dve", ost[:, :, h, :], psum[ob][:, 0:260].rearrange("p (r d) -> p r d", r=4), [PS[ob]], [R_ost])
                    nt_, R_ = 128, 4
                else:
                    CP("dve", ost[0:16, 0, h, :], psum[ob][0:16, 0:65], [PS[ob]], [R_ost])
                    nt_, R_ = 16, 1
                Oh = ost[0:nt_, 0:R_, h, 0:64]
                TTOP("dve", sqt[0:nt_, 0:R_], Oh, Oh, ALU.mult, [R_ost], [R_sqt])
                S.op("dve", "tensor_reduce", [R_sqt], [R_sm], out=SSb[0:nt_, 0:R_, h], in_=sqt[0:nt_, 0:R_], axis=AX.X, op=ALU.add)
                if h == 7:
                    finish_tile(i)

        DQ = []
        tk = [0]

        def defer(delay, fn):
            DQ.append([tk[0] + delay, fn])

        def tick():
            tk[0] += 1
            for x in [x for x in DQ if x[0] <= tk[0]]:
                DQ.remove(x)
                x[1]()

        def flushq():
            while DQ:
                DQ.pop(0)[1]()

        PS_T = [Res(), Res()]
        PS_T[0].rs = dict(PS[TB].rs)
        PS_T[0].ws = dict(PS[TB].ws)
        PS_T[1].rs = dict(PS[TB].rs)
        PS_T[1].ws = dict(PS[TB].ws)
        pst = psum[TB][:, 0:512].bitcast(BF16)

        def finish_tile(i):
            blocks = [4 * i + r for r in range(4)] if i < 4 else [16]

            nt_, R_ = (128, 4) if i < 4 else (16, 1)
            SS, RD, T_, RR, SC = [b[0:nt_, 0:R_] for b in (SSb, RDb, TTb, RRb, SCb)]

            def stA():
                RECIP(RD, ost[0:nt_, 0:R_, :, 64], [R_ost], [R_sm])
                TTOP("dve", T_, SS, RD, ALU.mult, [R_sm], [R_sm])
                TTOP("dve", T_, T_, RD, ALU.mult, [R_sm], [R_sm])
                TS("dve", T_, T_, 1.0 / 64, ALU.mult, [R_sm], [R_sm], s2=RMS_EPS, op1=ALU.add)

            def stB():
                tm, rtm = tmpr[rot["tr"] % 2], R_tmpr[rot["tr"] % 2]
                rot["tr"] += 1
                tv = tm[0:nt_, 0:R_ * 8].rearrange("p (r h) -> p r h", r=R_)
                ACT(tv, T_, AF.Ln, [R_sm], [rtm])
                ACT(RR, tv, AF.Exp, [rtm], [R_sm], scale=-0.5)

            def stC():
                TTOP("dve", SC, RR, RD, ALU.mult, [R_sm], [R_sm])
                for r, q in enumerate(blocks):
                    b0, nt = BLK[q]
                    TTOP("dve", ynt[r][0:nt].rearrange("p (h d) -> p h d", h=8), ost[0:nt, r, :, 0:64], SCb[0:nt, r, :].unsqueeze(2).broadcast_to([nt, 8, 64]),
                         ALU.mult, [R_ost, R_sm], [R_ynt[r]])

            def stD(r, q):
                b0, nt = BLK[q]
                hb = r % 2
                for c in range(4):
                    S.op("pe", "transpose", [R_ynt[r], R_const], [PS_T[hb]], out=pst[:, hb * 512 + c * 128:hb * 512 + c * 128 + nt],
                         in_=ynt[r][0:nt, c * 128:(c + 1) * 128], identity=ident16[0:nt, 0:nt])

            def stE(r, q):
                b0, nt = BLK[q]
                hb = r % 2
                for c in range(4):
                    ACT(yA[:, c, b0:b0 + nt], pst[:, hb * 512 + c * 128:hb * 512 + c * 128 + nt], AF.Copy, [PS_T[hb], R_const], R_yA.get(c, b0, b0 + nt),
                        scale=vcol(l, "head_g", c))

            stA()
            defer(3, stB)
            defer(6, stC)
            for r, q in enumerate(blocks):
                defer(13 + 2 * r, (lambda r=r, q=q: stD(r, q)))
                defer(16 + 2 * r, (lambda r=r, q=q: stE(r, q)))

        for n in range(nsteps + LA):
            if n < nsteps:
                emit_S(n)
            if n - LA >= 0:
                emit_PV(n - LA)
            tick()
        flushq()
        for hb in range(2):
            for dct in (PS_T[hb].ws, PS_T[hb].rs):
                for k_, rec_ in dct.items():
                    o_ = PS[TB].rs.get(k_)
                    if o_ is None or o_.idx < rec_.idx:
                        PS[TB].rs[k_] = rec_
        tap("ya", yA, R_yA.all())
        if STOP == "attn":
            return
        S.ph = "m_wout"
        PD.retire(R_qk.all() + R_pT + [R_ost, R_sqt, R_sm, R_fl] + R_ynt)
        R_rstd["r"] = PD.tres(1)
        ysrc = [(yA[:, k], R_yA.r[k]) for k in range(4)] + [(yB[:, k], R_yb.r[k]) for k in range(4)]
        pend = None
        wouts = [wload(l, "wout%d" % i) for i in range(4)]
        for t in range(5):
            st = proj_mm(l, None, 2, ysrc, [t], o_qk + (t % 2) * 4 * L * 2, 0, ((6, 7)[t % 2],), "mix_post_g", preloaded=wouts, bgk=2)
            bgflush()
            BG.extend(proj_tail_steps(l, "mix_post_g", st))
        bgflush()
        PD.retire(R_yA.all() + R_yb.all() + R_v + [R_vones] + R_rstd["r"].all() + R_qz)

    BG = []

    def bgpop(k):
        for _ in range(k):
            if BG:
                BG.pop(0)()

    def bgflush():
        while BG:
            BG.pop(0)()

    def proj_mm(l, pieces, cpp, src, tiles, o_ytmp, src_off, statbanks, gname, preloaded=None, bgk=0):
        K = len(src)
        t_lo = TT[tiles[0]][0]
        YW = 416 * len(tiles)
        ytmp = A.f32(o_ytmp, 8 * YW).rearrange("p (c t) -> p c t", c=8)
        R_yt = [[PD.new() for _ in tiles] for _ in range(8)]
        stat = {t: statbanks[i] for i, t in enumerate(tiles)}
        pend_stat = []
        for c in range(8):
            if c % cpp == 0:
                wt, rw = preloaded[c // cpp] if preloaded is not None else wload(l, pieces[c // cpp])
                if cpp == 2:
                    wv = wt[:, 0:2048].rearrange("p (j k m) -> p j k m", j=2, k=8)
                else:
                    wv = wt[:, 0:K * 128].rearrange("p (j k m) -> p j k m", j=1, k=K)
            j = c % cpp
            for ti, t in enumerate(tiles):
                t0, n = TT[t]
                pb = nb((0, 1, 2, 3, 4, 5))
                for k in range(K):
                    sap, sres = src[k]
                    MM(psum[pb][:, 0:n], wv[:, j, k, :], sap[:, t0 - src_off:t0 - src_off + n], k == 0, k == K - 1, [rw, sres[t]], [PS[pb]])
                sq, rsq = t16()
                ACT(sq[:, 0:n], psum[pb][:, 0:n], AF.Square, [PS[pb]], [rsq])
                ACT(ytmp[:, c, t0 - t_lo:t0 - t_lo + n], psum[pb][:, 0:n], AF.Copy, [PS[pb], R_const], [R_yt[c][ti]], scale=vcol(l, gname, c))
                pend_stat.append((psum[stat[t]][:, 0:n], sq[:, 0:n], c == 0, c == 7, rsq, PS[stat[t]]))
                if len(pend_stat) > 2:
                    o_, s_, a_, b_, r_, p_ = pend_stat.pop(0)
                    MM(o_, ones16, s_, a_, b_, [r_, R_const], [p_])
                bgpop(bgk)
        for o_, s_, a_, b_, r_, p_ in pend_stat:
            MM(o_, ones16, s_, a_, b_, [r_, R_const], [p_])
        return (tiles, t_lo, ytmp, R_yt, stat)

    def proj_tail_steps(l, gname, st):
        tiles, t_lo, ytmp, R_yt, stat = st
        steps = []
        for ti, t in enumerate(tiles):
            t0, n = TT[t]

            def s0(t=t, t0=t0, n=n):
                RSQRT(rstd[:, t0:t0 + n], psum[stat[t]][:, 0:n], 1.0 / DM, epsc, [PS[stat[t]]], [R_rstd["r"].r[0][t]])
            steps.append(s0)
        for ti, t in enumerate(tiles):
            t0, n = TT[t]
            for c in range(8):
                def s1(c=c, ti=ti, t=t, t0=t0, n=n):
                    yv = ytmp[:, c, t0 - t_lo:t0 - t_lo + n]
                    TTOP("dve", yv, yv, rstd[:, t0:t0 + n], ALU.mult, [R_yt[c][ti], R_rstd["r"].r[0][t]], [R_yt[c][ti]])
                    TTOP("dve", hT[:, c, t0:t0 + n], hT[:, c, t0:t0 + n], yv, ALU.add, [R_yt[c][ti], R_h.r[c][t]], [R_h.r[c][t]])
                steps.append(s1)
        steps.append(lambda: PD.retire([x for row in R_yt for x in row]))
        return steps

    def proj_tail(l, gname, st):
        for f in proj_tail_steps(l, gname, st):
            f()

    def ffn(s, l):
        last_tail = []
        R_rstd["r"] = PD.tres(1)
        o = o_P
        UW = 2 + 828
        up = A.bf16(o, 8 * UW).rearrange("p (c t) -> p c t", c=8)
        o += 8 * UW * 2
        o = (o + 63) // 64 * 64
        actT = A.bf16(o, 22 * 828).rearrange("p (c t) -> p c t", c=22)
        o += 22 * 828 * 2
        o = (o + 63) // 64 * 64
        NFA = 4
        facc = [A.f32(o + i * 416 * 4, 416) for i in range(NFA)]
        o += NFA * 416 * 4
        fsg = [A.bf16(o + i * 416 * 2, 416) for i in range(2)]
        o += 2 * 416 * 2
        o = (o + 63) // 64 * 64
        o_yt = o
        assert o_yt + 2 * 8 * 832 * 4 <= ARENA_BYTES, (o_yt, ARENA_BYTES)
        R_facc = [PD.new() for _ in range(NFA)]
        R_fsg = [PD.new() for _ in range(2)]
        R_halo = PD.new()
        frot = dict(a=0, s=0)

        class _RU:
            pass

        def new_up_res(tiles):
            R_up = [[PD.new() for _ in tiles] for _ in range(8)]
            ru = _RU()
            ru.r = [{t: R_up[c][ti] for ti, t in enumerate(tiles)} for c in range(8)]
            return R_up, ru

        def new_act_res(tiles):
            return [[PD.new() for _ in tiles] for _ in range(22)]

        def apply_steps(tiles, ru):
            t_lo = TT[tiles[0]][0]
            return prenorm_apply_steps(l, "ffn_pre_g", tiles, (lambda c, t0, n: up[:, c, 2 + t0 - t_lo:2 + t0 - t_lo + n]), ru)

        S.ph = "f_prenorm"
        prenorm_stats(range(5))
        R_up, ru = new_up_res(FFN_PASSES[0])
        R_act = new_act_res(FFN_PASSES[0])
        for f in apply_steps(FFN_PASSES[0], ru):
            f()
        for pi, tiles in enumerate(FFN_PASSES):
            t_lo = TT[tiles[0]][0]
            S.ph = "f_up"
            for p in range(22):
                wt, rw = wload(l, "wup%d" % p)
                wv = wt[:, 0:2048].rearrange("p (j k m) -> p j k m", j=2, k=8)
                for ti, t in enumerate(tiles):
                    t0, n = TT[t]
                    hal = 0 if t == 0 else 2
                    c0 = 2 + t0 - t_lo - hal
                    rdu = []
                    for k in range(8):
                        rdu.append(R_up[k][ti])
                        if hal and ti > 0:
                            rdu.append(R_up[k][ti - 1])
                    if hal and ti == 0:
                        rdu.append(R_halo)
                    res = []
                    for half in range(2):
                        pb = nb((0, 1, 2, 3, 4, 5))
                        for k in range(8):
                            MM(psum[pb][:, 0:n + hal], wv[:, half, k, :], up[:, k, c0:c0 + n + hal], k == 0, k == 7, [rw] + rdu, [PS[pb]])
                        ai = frot["a"] % NFA
                        frot["a"] += 1
                        acc, racc = facc[ai], R_facc[ai]
                        f = half * 22 + p
                        ACT(acc[:, 0:n], psum[pb][:, hal:hal + n], AF.Copy, [PS[pb], R_const], [racc], scale=vcol(l, "ffn_cw", 2 * 44 + f))
                        if hal:
                            STT("dve", acc[:, 0:n], psum[pb][:, 1:1 + n], vcol(l, "ffn_cw", 1 * 44 + f), acc[:, 0:n], ALU.mult, ALU.add, [PS[pb], racc, R_const], [racc])
                            STT("dve", acc[:, 0:n], psum[pb][:, 0:n], vcol(l, "ffn_cw", 0 * 44 + f), acc[:, 0:n], ALU.mult, ALU.add, [PS[pb], racc, R_const], [racc])
                        else:
                            STT("dve", acc[:, 1:n], psum[pb][:, 0:n - 1], vcol(l, "ffn_cw", 1 * 44 + f), acc[:, 1:n], ALU.mult, ALU.add, [PS[pb], racc, R_const], [racc])
                            STT("dve", acc[:, 2:n], psum[pb][:, 0:n - 2], vcol(l, "ffn_cw", 0 * 44 + f), acc[:, 2:n], ALU.mult, ALU.add, [PS[pb], racc, R_const], [racc])
                        res.append((acc, racc))
                    si = frot["s"] % 2
                    frot["s"] += 1
                    ACT(fsg[si][:, 0:n], res[0][0][:, 0:n], AF.Silu, [res[0][1]], [R_fsg[si]])
                    TTOP("dve", actT[:, p, t0 - t_lo:t0 - t_lo + n], fsg[si][:, 0:n], res[1][0][:, 0:n], ALU.mult, [R_fsg[si], res[1][1]], [R_act[p][ti]])
            if pi + 1 < len(FFN_PASSES):
                width = sum(TT[t][1] for t in tiles)
                rd_all = [R_up[c][len(tiles) - 1] for c in range(8)]
                CP("dve", up[:, :, 0:2], up[:, :, width:width + 2], rd_all, [R_halo])
            S.ph = "f_down"
            src = [(actT[:, k], {t: R_act[k][ti] for ti, t in enumerate(tiles)}) for k in range(22)]
            PD.retire([x for row in R_up for x in row])
            if pi + 1 < len(FFN_PASSES):
                R_up, ru = new_up_res(FFN_PASSES[pi + 1])
                nfront = len(FFN_PASSES[pi - 1]) if (pi > 0 and BG) else 0
                BG[nfront:nfront] = apply_steps(FFN_PASSES[pi + 1], ru)
            st = proj_mm(l, ["wdn%d" % i for i in range(8)], 1, src, tiles, o_yt + (pi % 2) * 8 * 832 * 4, t_lo, (6, 7), "ffn_post_g", bgk=3)
            bgflush()
            PD.retire([x for row in R_act for x in row])
            if pi + 1 < len(FFN_PASSES):
                R_act = new_act_res(FFN_PASSES[pi + 1])
            S.ph = "f_tail"
            if pi + 1 < len(FFN_PASSES):
                BG.extend(proj_tail_steps(l, "ffn_post_g", st))
            else:
                last_tail.extend(proj_tail_steps(l, "ffn_post_g", st))
        PD.retire(R_facc + R_fsg + [R_halo] + [x for t_, x in enumerate(R_rstd["r"].r[0]) if t_ != 4])
        return last_tail, R_rstd["r"].r[0][4]

    def load_tiles(s, tiles):
        for t in tiles:
            t0, n = TT[t]
            lo, hi = max(t0, NMETA), t0 + n
            S.dmaop("xin", hT[:, :, lo:hi], xT[s, :, :, lo - NMETA:hi - NMETA], writes=[R_h.r[c][t] for c in range(8)], nsem=4)
            if t == 0:
                S.dmaop("xin", hT[:, :, 0:NMETA], metaT, writes=[R_h.r[c][0] for c in range(8)], nsem=4)

    hook = None
    rstd4 = None
    load_tiles(0, range(5))
    for s in range(NSEQ):
        for l in range(NLAYER):
            mixer(s, l, hook, rstd4)
            hook, rstd4 = None, None
            if STOP in ("prenorm", "conv", "qkv", "attn", "mixer"):
                break
            tail_steps, rstd4 = ffn(s, l)
            last = (l == NLAYER - 1)
            if last:
                tap("h", hT, R_h.all()) if False else None
                store_seq(s, range(4))
                if s + 1 < NSEQ:
                    load_tiles(s + 1, range(4))

            def hook(tail_steps=tail_steps, last=last, s=s):
                for f in tail_steps:
                    f()
                if last:
                    store_seq(s, [4])
                    if s + 1 < NSEQ:
                        load_tiles(s + 1, [4])
        if STOP is not None:
            tap("h", hT, R_h.all())
            store_seq(s, range(5))
            hook = None
    if hook is not None:
        hook()
        if "h" in taps:
            tap("h", hT, R_h.all())

    S.finalize()
    esems = {}
    dsems = {}
    for e in Sched.COMPUTE:
        esems[e] = es.enter_context(nc.semaphore("s_" + e))
    for stname, st in S.streams.items():
        for i in range(st["nsem"]):
            dsems[(stname, i)] = es.enter_context(nc.semaphore("d_%s%d" % (stname, i)))
    block = es.enter_context(nc.Block())

    @block.tensor
    def _(eng):
        S.emit("pe", eng, esems, dsems)

    @block.scalar
    def _(eng):
        S.emit("act", eng, esems, dsems)

    @block.vector
    def _(eng):
        S.emit("dve", eng, esems, dsems)

    @block.gpsimd
    def _(eng):
        S.emit("pool", eng, esems, dsems)

    @block.sync
    def _(eng):
        S.emit("sp", eng, esems, dsems)
        for stname, st in S.streams.items():
            n = len(st["recs"])
            for i in range(st["nsem"]):
                k = len([1 for j in range(n) if j % st["nsem"] == i])
                if k:
                    eng.wait_ge(dsems[(stname, i)], 16 * k)
    es.close()
    stats = {e: len(v) for e, v in S.ins.items()}
    global _LAST_SCHED
    _LAST_SCHED = S
    return nc, stats


def make_in_maps(inputs, cfg):
    NSEQ = cfg["nseq"]
    x = np.asarray(inputs["x"], np.float32)
    f32 = lambda a: np.asarray(a, np.float32)
    wfull = host_weights(f32(inputs["w_in"]), f32(inputs["a_pw_w"]), f32(inputs["w_out"]),
                         f32(inputs["ffn_w_up"]), f32(inputs["ffn_w_down"]))
    vecs = host_vecs({k: f32(v) for k, v in inputs.items()})
    bfg = np.ascontiguousarray(np.broadcast_to(f32(inputs["b_forget"]).reshape(1, 16), (128, 16)))
    metaT = np.ascontiguousarray(f32(inputs["meta_tokens"]).reshape(NMETA, 8, 128).transpose(2, 1, 0))
    consts = host_consts()
    maps = []
    ncores = cfg.get("ncores", NCORES)
    for c in range(ncores):
        xs = x[c * NSEQ:(c + 1) * NSEQ]
        xTn = np.ascontiguousarray(xs.reshape(NSEQ, SEQ, 8, 128).transpose(0, 3, 2, 1))
        maps.append(dict(xT=xTn, metaT=metaT, wf=wfull, vecs=vecs, bfg=bfg, consts=consts))
    return maps


def kernel(**inputs):
    cfg = dict(CFG)
    nc, _ = build(cfg)
    maps = make_in_maps(inputs, cfg)
    res = run_bass_kernel_spmd(nc, maps, core_ids=list(range(NCORES)))
    outs = []
    for c in range(NCORES):
        oT = res.results[c]["outT"]
        outs.append(np.ascontiguousarray(oT.transpose(0, 3, 2, 1)).reshape(cfg["nseq"], SEQ, DM))
    return np.concatenate(outs, axis=0).astype(np.float32)
```

```python
import numpy as np
import concourse.bass as bass
import concourse.mybir as mybir
from concourse.bass_utils import run_bass_kernel_spmd

F32 = mybir.dt.float32
BF16 = mybir.dt.bfloat16
AF = mybir.ActivationFunctionType
ALU = mybir.AluOpType
AX = mybir.AxisListType

NCORES = 8
SEQ = 2048
NMETA = 16
L = SEQ + NMETA
DM = 1024
DFF = 2816
DIN = 2824
TT = [(0, 416), (416, 412), (828, 412), (1240, 412), (1652, 412)]
BLK = [(128 * b, 128) for b in range(16)] + [(2048, 16)]
FFN_PASSES = [[0, 1], [2, 3], [4]]
RMS_EPS = 1e-6
LN_EPS = 1e-5

ZC = dict(q=0, k=512, v=1024, f=1536, aval=1544, agate=1800, scb=2056, scc=2312, scx=2568)
WIN_FM = [("aval", 0), ("agate", 0), ("aval", 1), ("agate", 1),
          ("scc", 0), ("scx", 0), ("scb", 0), ("scc", 1), ("scx", 1), ("scb", 1),
          ("q", 0), ("q", 1), ("q", 2), ("q", 3), ("k", 0), ("k", 1), ("k", 2), ("k", 3)]
PIECES = []
for i in range(9):
    PIECES.append(("win%d" % i, 2048))
PIECES.append(("pw", 512))
PIECES.append(("vfa", 2048))
PIECES.append(("vfb", 2112))
for i in range(4):
    PIECES.append(("wout%d" % i, 2048))
for i in range(22):
    PIECES.append(("wup%d" % i, 2048))
for i in range(8):
    PIECES.append(("wdn%d" % i, 2816))
POFF = {}
_o = 0
for _n, _s in PIECES:
    POFF[_n] = (_o, _s)
    _o += _s
WL = _o
assert WL == 98880
SLOT = 2816
NSLOT = 4
PRE_BLK = 4120

VEC = {}
_v = 0
for _n, _s in [("mix_pre_g", 8), ("mix_post_g", 8), ("ffn_pre_g", 8), ("ffn_post_g", 8), ("head_g", 8),
               ("dw_w", 62), ("dw_b", 2), ("ln_g", 2), ("ln_b", 2), ("sc_w", 6), ("ffn_cw", 132)]:
    VEC[_n] = _v
    _v += _s
NVEC = _v

C_ID, C_ONES, C_TRI, C_NEG, C_BO = 0, 128, 256, 384, 512
NCONST = 640


def host_consts():
    c = np.zeros((128, NCONST), np.float32)
    i = np.arange(128)
    c[:, C_ID:C_ID + 128] = np.eye(128, dtype=np.float32)
    c[:, C_ONES:C_ONES + 128] = 1.0
    c[:, C_TRI:C_TRI + 128] = (i[:, None] <= i[None, :]).astype(np.float32)
    c[:, C_NEG:C_NEG + 128] = np.where(i[:, None] > i[None, :], -30000.0, 0.0).astype(np.float32)
    c[:, C_BO:C_BO + 128] = ((i[:, None] // 64) == (i[None, :] // 64)).astype(np.float32)
    return c


def host_weights(w_in, a_pw_w, w_out, ffn_w_up, ffn_w_down):
    out = np.empty((128, 2 * WL), np.float32)
    for l in range(2):
        base = l * WL
        wi = w_in[l].reshape(8, 128, DIN)
        for i in range(9):
            o, s = POFF["win%d" % i]
            blk = np.empty((128, 2, 8, 128), np.float32)
            for j in range(2):
                nm, cc = WIN_FM[2 * i + j]
                c0 = ZC[nm] + 128 * cc
                blk[:, j] = wi[:, :, c0:c0 + 128].transpose(1, 0, 2)
            out[:, base + o:base + o + s] = blk.reshape(128, -1)
        o, s = POFF["pw"]
        out[:, base + o:base + o + s] = a_pw_w[l].reshape(2, 128, 256).transpose(1, 0, 2).reshape(128, -1)
        o, s = POFF["vfa"]
        out[:, base + o:base + o + s] = wi[:, :, 1024:1280].transpose(1, 0, 2).reshape(128, -1)
        o, s = POFF["vfb"]
        out[:, base + o:base + o + s] = wi[:, :, 1280:1544].transpose(1, 0, 2).reshape(128, -1)
        wo = w_out[l].reshape(8, 128, DM)
        for i in range(4):
            o, s = POFF["wout%d" % i]
            blk = np.empty((128, 2, 8, 128), np.float32)
            for j in range(2):
                c0 = 128 * (2 * i + j)
                blk[:, j] = wo[:, :, c0:c0 + 128].transpose(1, 0, 2)
            out[:, base + o:base + o + s] = blk.reshape(128, -1)
        wu = ffn_w_up[l].reshape(8, 128, 2 * DFF)
        for i in range(22):
            o, s = POFF["wup%d" % i]
            blk = np.empty((128, 2, 8, 128), np.float32)
            blk[:, 0] = wu[:, :, 128 * i:128 * i + 128].transpose(1, 0, 2)
            blk[:, 1] = wu[:, :, DFF + 128 * i:DFF + 128 * i + 128].transpose(1, 0, 2)
            out[:, base + o:base + o + s] = blk.reshape(128, -1)
        wd = ffn_w_down[l].reshape(22, 128, DM)
        for i in range(8):
            o, s = POFF["wdn%d" % i]
            out[:, base + o:base + o + s] = wd[:, :, 128 * i:128 * i + 128].transpose(1, 0, 2).reshape(128, -1)
    return out


def host_vecs(inp):
    v = np.zeros((128, 2 * NVEC), np.float32)

    def cols(a):
        return np.ascontiguousarray(a.reshape(-1, 128).T)
    for l in range(2):
        b = l * NVEC
        for nm in ["mix_pre_g", "mix_post_g", "ffn_pre_g", "ffn_post_g", "head_g"]:
            v[:, b + VEC[nm]:b + VEC[nm] + 8] = cols(inp[nm][l])
        dw = inp["a_dw_w"][l]
        v[:, b + VEC["dw_w"]:b + VEC["dw_w"] + 62] = dw.reshape(31, 2, 128).transpose(2, 0, 1).reshape(128, 62)
        v[:, b + VEC["dw_b"]:b + VEC["dw_b"] + 2] = cols(inp["a_dw_b"][l])
        v[:, b + VEC["ln_g"]:b + VEC["ln_g"] + 2] = cols(inp["a_ln_g"][l])
        v[:, b + VEC["ln_b"]:b + VEC["ln_b"] + 2] = cols(inp["a_ln_b"][l])
        sc = inp["sc_conv_w"][l]
        v[:, b + VEC["sc_w"]:b + VEC["sc_w"] + 6] = sc.reshape(3, 2, 128).transpose(2, 0, 1).reshape(128, 6)
        fc = inp["ffn_conv_w"][l]
        v[:, b + VEC["ffn_cw"]:b + VEC["ffn_cw"] + 132] = fc.reshape(3, 44, 128).transpose(2, 0, 1).reshape(128, 132)
    return v


class Rec:
    __slots__ = ("eng", "idx", "fn", "deps", "needs_inc", "waits", "clock", "sem", "semval", "cnt", "is_dma", "ph")


class Res:
    __slots__ = ("ws", "rs")

    def __init__(self):
        self.ws = {}
        self.rs = {}


class TRes:
    def __init__(self, nch, tiles=TT):
        self.tiles = tiles
        self.r = [[Res() for _ in tiles] for _ in range(nch)]

    def get(self, c, lo, hi):
        return [self.r[c][t] for t, (s, n) in enumerate(self.tiles) if s < hi and lo < s + n]

    def all(self):
        return [x for row in self.r for x in row]


class Sched:
    COMPUTE = ("pe", "act", "dve", "pool")

    def __init__(self, nc):
        self.nc = nc
        self.ins = {e: [] for e in ("pe", "act", "dve", "pool", "sp")}
        self.order = []
        self.streams = {}
        self.ph = ""

    def add(self, eng, fn, reads=(), writes=()):
        r = Rec()
        r.eng = eng
        r.fn = fn
        r.is_dma = False
        r.needs_inc = False
        r.sem = None
        r.ph = self.ph
        lst = self.ins[eng]
        r.idx = len(lst) + 1
        deps = set()
        key = eng
        for x in reads:
            deps.update(x.ws.values())
        for x in writes:
            deps.update(x.ws.values())
            deps.update(x.rs.values())
        for x in reads:
            x.rs[key] = r
        for x in writes:
            x.ws = {key: r}
            x.rs = {}
        deps.discard(r)
        r.deps = deps
        lst.append(r)
        self.order.append(r)
        return r

    def dma(self, stream, fn, reads=(), writes=(), nsem=4):
        st = self.streams.setdefault(stream, dict(recs=[], nsem=nsem, sems=None))
        r = Rec()
        r.eng = "sp"
        r.fn = fn
        r.is_dma = True
        r.needs_inc = True
        r.ph = self.ph
        lst = self.ins["sp"]
        r.idx = len(lst) + 1
        n = len(st["recs"])
        r.sem = (stream, n % st["nsem"])
        r.semval = 16 * (n // st["nsem"] + 1)
        deps = set()
        if n >= st["nsem"]:
            deps.add(st["recs"][n - st["nsem"]])
        key = ("d", id(r))
        for x in reads:
            deps.update(x.ws.values())
        for x in writes:
            deps.update(x.ws.values())
            deps.update(x.rs.values())
        for x in reads:
            x.rs[key] = r
        for x in writes:
            x.ws = {key: r}
            x.rs = {}
        r.deps = deps
        st["recs"].append(r)
        lst.append(r)
        self.order.append(r)
        return r

    def op(self, eng, method, reads=(), writes=(), **kw):
        r = self.add(eng, (lambda e: getattr(e, method)(**kw)), reads, writes)
        r.ph = r.ph + ":" + method + ":" + str(kw.get("func", kw.get("op", kw.get("op0", ""))))
        return r

    def dmaop(self, stream, out, in_, reads=(), writes=(), nsem=4):
        return self.dma(stream, (lambda e: e.dma_start(out=out, in_=in_)), reads, writes, nsem)

    def finalize(self):
        run = {e: {} for e in self.ins}
        for r in self.order:
            clock = run[r.eng]
            waits = []
            for d in sorted(r.deps, key=lambda d: (d.eng, d.idx)):
                if d.is_dma:
                    k = ("d", d.sem)
                    if clock.get(k, 0) < d.semval:
                        waits.append(d)
                        for kk, vv in d.clock.items():
                            if clock.get(kk, 0) < vv:
                                clock[kk] = vv
                        clock[k] = d.semval
                elif d.eng == r.eng:
                    if r.eng in ("pe", "sp") or r.idx - d.idx > 2:
                        continue
                    d.needs_inc = True
                    waits.append(d)
                else:
                    if clock.get(d.eng, 0) < d.idx:
                        d.needs_inc = True
                        waits.append(d)
                        for kk, vv in d.clock.items():
                            if clock.get(kk, 0) < vv:
                                clock[kk] = vv
            r.waits = waits
            if r.is_dma:
                r.clock = dict(clock)
            else:
                clock[r.eng] = r.idx
                r.clock = dict(clock)
        for e in self.COMPUTE:
            c = 0
            for r in self.ins[e]:
                if r.needs_inc:
                    c += 1
                r.cnt = c

    def emit(self, eng_name, eng, esems, dsems):
        for r in self.ins[eng_name]:
            for d in r.waits:
                if d.is_dma:
                    eng.wait_ge(dsems[d.sem], d.semval)
                else:
                    eng.wait_ge(esems[d.eng], d.cnt)
            ins = r.fn(eng)
            if r.is_dma:
                ins.then_inc(dsems[r.sem], 16)
            elif r.needs_inc:
                ins.then_inc(esems[r.eng], 1)


class Arena:
    def __init__(self, tensor, nbytes):
        self.t = tensor
        self.nbytes = nbytes
        self.off = 0

    def alloc(self, nbytes):
        nbytes = (nbytes + 63) // 64 * 64
        o = self.off
        self.off += nbytes
        assert self.off <= self.nbytes, (self.off, self.nbytes)
        return o

    def f32(self, off, n):
        return self.t[:, off // 4: off // 4 + n]

    def bf16(self, off, n):
        v = self.t[:, off // 4: off // 4 + (n + 1) // 2].bitcast(BF16)
        return v[:, 0:n]


CFG = dict(nseq=4, nlayer=2, stop=None, taps=())


class Pend:
    def __init__(self):
        self.p = {}

    def retire(self, res_list):
        for r in res_list:
            for dct in (r.ws, r.rs):
                for k, rec in dct.items():
                    o = self.p.get(k)
                    if o is None or o.idx < rec.idx:
                        self.p[k] = rec
            r.ws = {}
            r.rs = {}

    def new(self):
        r = Res()
        r.rs = dict(self.p)
        return r

    def tres(self, nch, tiles=TT):
        t = TRes.__new__(TRes)
        t.tiles = tiles
        t.r = [[self.new() for _ in tiles] for _ in range(nch)]
        return t


def build(cfg):
    NSEQ = cfg["nseq"]
    NLAYER = cfg["nlayer"]
    STOP = cfg["stop"]
    nc = bass.Bass("TRN2", target_bir_lowering=False, dynamic_dma_scratch_size=64)
    xT = nc.dram_tensor("xT", [NSEQ, 128, 8, SEQ], F32, kind="ExternalInput").ap()
    metaT = nc.dram_tensor("metaT", [128, 8, NMETA], F32, kind="ExternalInput").ap()
    wf = nc.dram_tensor("wf", [128, 2 * WL], F32, kind="ExternalInput").ap()
    vecs_d = nc.dram_tensor("vecs", [128, 2 * NVEC], F32, kind="ExternalInput").ap()
    bfg_d = nc.dram_tensor("bfg", [128, 16], F32, kind="ExternalInput").ap()
    consts_d = nc.dram_tensor("consts", [128, NCONST], F32, kind="ExternalInput").ap()
    outT = nc.dram_tensor("outT", [NSEQ, 128, 8, SEQ], F32, kind="ExternalOutput").ap()
    wb = nc.dram_tensor("wb", [128, 2 * WL], BF16, kind="Internal").ap()
    taps = {}
    for nm, shape, dt in cfg["taps"]:
        taps[nm] = nc.dram_tensor("tap_" + nm, list(shape), BF16 if dt == "bf16" else F32, kind="ExternalOutput").ap()

    S = Sched(nc)
    ARENA_BYTES = 229056
    from contextlib import ExitStack
    es = ExitStack()
    arena_t = es.enter_context(nc.sbuf_tensor("arena", [128, ARENA_BYTES // 4], F32))
    A = Arena(arena_t, ARENA_BYTES)
    psum = [es.enter_context(nc.psum_tensor("ps%d" % i, [128, 512], F32)) for i in range(8)]
    PS = [Res() for _ in range(8)]
    psrot = dict(i=0)

    def nb(banks=(0, 1, 2, 3, 4, 5)):
        b = banks[psrot["i"] % len(banks)]
        psrot["i"] += 1
        return b

    o_h = A.alloc(8 * L * 4)
    o_w = A.alloc(NSLOT * SLOT * 2)
    o_rstd = A.alloc(8320)
    o_vec = A.alloc(2 * NVEC * 4)
    o_bfg = A.alloc(16 * 4)
    o_c32 = A.alloc(NCONST * 4)
    o_c16 = A.alloc(NCONST * 2)
    NT32, NT16 = 5, 3
    o_t32 = A.alloc(NT32 * 416 * 4)
    o_t16 = A.alloc(NT16 * 416 * 2)
    o_eps = A.alloc(64)
    o_tr = A.alloc(2 * 416 * 4)
    o_P = A.off
    P_BYTES = ARENA_BYTES - o_P
    assert P_BYTES >= 100240, P_BYTES
    o_u = o_P
    o_qk = o_u + 8 * L * 2
    o_v = o_qk + 8 * L * 2
    o_yb = o_v + 17 * 520 * 2
    o_sp = o_yb + 4 * L * 2
    o_dg1 = o_sp
    o_qz = o_dg1 + 31 * 128 * 2
    assert o_qz + 2 * 1024 <= ARENA_BYTES, (o_qz, ARENA_BYTES)

    hT = A.f32(o_h, 8 * L).rearrange("p (c t) -> p c t", c=8)
    rstd = A.f32(o_rstd, L)
    vec = A.f32(o_vec, 2 * NVEC)
    bfg = A.f32(o_bfg, 16)
    c32 = A.f32(o_c32, NCONST)
    c16 = A.bf16(o_c16, NCONST)
    wslot = [A.bf16(o_w + i * SLOT * 2, SLOT) for i in range(NSLOT)]
    tmp32 = [A.f32(o_t32 + i * 416 * 4, 416) for i in range(NT32)]
    tmp16 = [A.bf16(o_t16 + i * 416 * 2, 416) for i in range(NT16)]
    epsv = A.f32(o_eps, 8)
    tmpr = [A.f32(o_tr + i * 416 * 4, 416) for i in range(2)]
    R_tmpr = [Res(), Res()]
    epsc, lnepsc, onec = epsv[:, 0:1], epsv[:, 1:2], epsv[:, 2:3]

    PD = Pend()
    R_h = TRes(8)
    R_const = Res()
    R_wb = Res()
    R_wslot = [Res() for _ in range(NSLOT)]
    R_tmp32 = [Res() for _ in range(NT32)]
    R_tmp16 = [Res() for _ in range(NT16)]
    rot = dict(t32=0, t16=0, w=0, tr=0)

    def vcol(l, name, j=0):
        c = l * NVEC + VEC[name] + j
        return vec[:, c:c + 1]

    def MM(out, lhsT, rhs, start, stop, reads, writes, **kw):
        S.op("pe", "matmul", reads, writes, out=out, lhsT=lhsT, rhs=rhs, start=start, stop=stop, **kw)

    def ACT(out, in_, func, reads, writes, bias=None, scale=None):
        kw = dict(out=out, in_=in_, func=func)
        if bias is not None:
            kw["bias"] = bias
        if scale is not None:
            kw["scale"] = scale
        S.op("act", "activation", reads, writes, **kw)

    def STT(eng, out, in0, scalar, in1, op0, op1, reads, writes):
        S.op(eng, "scalar_tensor_tensor", reads, writes, out=out, in0=in0, scalar=scalar, in1=in1, op0=op0, op1=op1)

    def TTOP(eng, out, in0, in1, op, reads, writes):
        S.op(eng, "tensor_tensor", reads, writes, out=out, in0=in0, in1=in1, op=op)

    def TS(eng, out, in0, s1, op0, reads, writes, s2=None, op1=None):
        kw = dict(out=out, in0=in0, scalar1=s1, scalar2=s2, op0=op0)
        if op1 is not None:
            kw["op1"] = op1
        S.op(eng, "tensor_scalar", reads, writes, **kw)

    def CP(eng, out, in_, reads, writes):
        if eng == "act":
            S.op("act", "activation", reads, writes, out=out, in_=in_, func=AF.Copy)
        else:
            S.op(eng, "tensor_copy", reads, writes, out=out, in_=in_)

    def RSQRT(out, in_, scale, eps_ap, reads, writes):
        np_ = in_.shape[0]
        tm, rtm = tmpr[rot["tr"] % 2], R_tmpr[rot["tr"] % 2]
        rot["tr"] += 1
        n_ = in_.shape[-1]
        tv = tm[0:np_, 0:n_]
        if eps_ap is None:
            ACT(tv, in_, AF.Ln, list(reads), [rtm], scale=scale)
        else:
            ACT(tv, in_, AF.Ln, list(reads) + [R_const], [rtm], bias=eps_ap, scale=scale)
        ACT(out, tv, AF.Exp, [rtm], writes, scale=-0.5)

    def RECIP(out, in_, reads, writes):
        S.op("dve", "reciprocal", reads, writes, out=out, in_=in_)

    def MEMSET(eng, ap, val, writes):
        S.op(eng, "memset", (), writes, ap=ap, constant=val)

    S.dmaop("misc", c32, consts_d, writes=[R_const])
    S.dmaop("misc", vec, vecs_d, writes=[R_const])
    S.dmaop("misc", bfg, bfg_d, writes=[R_const])
    CP("dve", c16, c32, [R_const], [R_const])
    MEMSET("dve", epsv[:, 0:1], RMS_EPS, [R_const])
    MEMSET("dve", epsv[:, 1:2], LN_EPS, [R_const])
    MEMSET("dve", epsv[:, 2:3], 1.0, [R_const])
    MEMSET("dve", epsv[:, 3:4], 0.0, [R_const])
    ident16 = c16[:, C_ID:C_ID + 128]
    ones16 = c16[:, C_ONES:C_ONES + 128]
    negtri16 = c16[:, C_NEG:C_NEG + 128]
    bo16 = c16[:, C_BO:C_BO + 128]
    ones32 = c32[:, C_ONES:C_ONES + 128]
    tri32 = c32[:, C_TRI:C_TRI + 128]

    dgd = nc.dram_tensor("dgd", [128, 2 * 2 * 3968], BF16, kind="Internal").ap()
    R_dgd = Res()
    dg1 = A.bf16(o_dg1, 31 * 128).rearrange("p (k m) -> p k m", k=31)
    R_dg1 = Res()
    qz = [A.bf16(o_qz + par * 1024, 512) for par in range(2)]
    R_qz = [None, None]
    def load_tiles(s, tiles):
        for t in tiles:
            t0, n = TT[t]
            lo, hi = max(t0, NMETA), t0 + n
            S.dmaop("xin", hT[:, :, lo:hi], xT[s, :, :, lo - NMETA:hi - NMETA], writes=[R_h.r[c][t] for c in range(8)], nsem=4)
            if t == 0:
                S.dmaop("xin", hT[:, :, 0:NMETA], metaT, writes=[R_h.r[c][0] for c in range(8)], nsem=4)

    load_tiles(0, range(5))
    NSTG = 4
    st32 = [A.f32(o_P + i * PRE_BLK * 4, PRE_BLK) for i in range(NSTG)]
    st16 = [A.bf16(o_P + NSTG * PRE_BLK * 4 + i * PRE_BLK * 2, PRE_BLK) for i in range(NSTG)]
    assert NSTG * PRE_BLK * 6 <= P_BYTES
    R_st32 = [PD.new() for _ in range(NSTG)]
    R_st16 = [PD.new() for _ in range(NSTG)]
    nblk_used = (NLAYER * WL + PRE_BLK - 1) // PRE_BLK
    for b in range(nblk_used + 2):
        if b < nblk_used:
            i = b % NSTG
            S.dmaop("prein", st32[i], wf[:, b * PRE_BLK:(b + 1) * PRE_BLK], writes=[R_st32[i]], nsem=NSTG)
        if 0 <= b - 1 < nblk_used:
            i = (b - 1) % NSTG
            CP(("dve", "act")[(b - 1) % 2], st16[i], st32[i], [R_st32[i]], [R_st16[i]])
        if 0 <= b - 2 < nblk_used:
            i = (b - 2) % NSTG
            S.dmaop("preout", wb[:, (b - 2) * PRE_BLK:(b - 1) * PRE_BLK], st16[i], reads=[R_st16[i]], writes=[R_wb], nsem=NSTG)
    PD.retire(R_st32 + R_st16)
    dgst = [A.bf16(o_P + i * 7936, 3968).rearrange("p (k m) -> p k m", k=31) for i in range(2)]
    R_dgst = [PD.new(), PD.new()]
    for l_ in range(NLAYER):
        for cc_ in range(2):
            i_ = (l_ * 2 + cc_) % 2
            for k in range(31):
                TS("dve", dgst[i_][:, k, :], ident16, vcol(l_, "dw_w", 2 * k + cc_), ALU.mult, [R_const], [R_dgst[i_]])
            S.dmaop("dgo", dgd[:, (l_ * 2 + cc_) * 3968:(l_ * 2 + cc_ + 1) * 3968], dgst[i_].rearrange("p k m -> p (k m)"), reads=[R_dgst[i_]], writes=[R_dgd], nsem=2)
    PD.retire(R_dgst)

    def wload(l, name):
        o, s = POFF[name]
        i = rot["w"] % NSLOT
        rot["w"] += 1
        S.dmaop("w", wslot[i][:, 0:s], wb[:, l * WL + o: l * WL + o + s], reads=[R_wb], writes=[R_wslot[i]], nsem=NSLOT)
        return wslot[i], R_wslot[i]

    def t32():
        i = rot["t32"] % NT32
        rot["t32"] += 1
        return tmp32[i], R_tmp32[i]

    def t16():
        i = rot["t16"] % NT16
        rot["t16"] += 1
        return tmp16[i], R_tmp16[i]

    def tap(name, src_ap, reads):
        if name in taps:
            S.dmaop("tap", taps[name], src_ap, reads=reads, nsem=1)

    def load_seq(s):
        for t, (t0, n) in enumerate(TT):
            lo, hi = max(t0, NMETA), t0 + n
            S.dmaop("xin", hT[:, :, lo:hi], xT[s, :, :, lo - NMETA:hi - NMETA], writes=[R_h.r[c][t] for c in range(8)], nsem=4)
            if t == 0:
                S.dmaop("xin", hT[:, :, 0:NMETA], metaT, writes=[R_h.r[c][0] for c in range(8)], nsem=4)

    def store_seq(s, tiles):
        for t in tiles:
            t0, n = TT[t]
            lo, hi = max(t0, NMETA), t0 + n
            S.dmaop("xout", outT[s, :, :, lo - NMETA:hi - NMETA], hT[:, :, lo:hi], reads=[R_h.r[c][t] for c in range(8)], nsem=4)

    R_rstd = dict(r=None)

    def prenorm(l, gname, tiles, u_fn, Ru):
        for t in tiles:
            t0, n = TT[t]
            pb = nb()
            for c in range(8):
                sq, rsq = t16()
                ACT(sq[:, 0:n], hT[:, c, t0:t0 + n], AF.Square, [R_h.r[c][t]], [rsq])
                MM(psum[pb][:, 0:n], ones16, sq[:, 0:n], c == 0, c == 7, [rsq, R_const], [PS[pb]])
            RSQRT(rstd[:, t0:t0 + n], psum[pb][:, 0:n], 1.0 / DM, epsc, [PS[pb]], [R_rstd["r"].r[0][t]])
            for c in range(8):
                STT("dve", u_fn(c, t0, n), hT[:, c, t0:t0 + n], vcol(l, gname, c), rstd[:, t0:t0 + n], ALU.mult, ALU.mult,
                    [R_h.r[c][t], R_rstd["r"].r[0][t], R_const], [Ru.r[c][t]])

    def prenorm_stats(tiles):
        for t in tiles:
            t0, n = TT[t]
            pb = nb()
            for c in range(8):
                sq, rsq = t16()
                ACT(sq[:, 0:n], hT[:, c, t0:t0 + n], AF.Square, [R_h.r[c][t]], [rsq])
                MM(psum[pb][:, 0:n], ones16, sq[:, 0:n], c == 0, c == 7, [rsq, R_const], [PS[pb]])
            RSQRT(rstd[:, t0:t0 + n], psum[pb][:, 0:n], 1.0 / DM, epsc, [PS[pb]], [R_rstd["r"].r[0][t]])

    def prenorm_apply_steps(l, gname, tiles, u_fn, Ru):
        steps = []
        for t in tiles:
            t0, n = TT[t]
            for c in range(8):
                def f(c=c, t=t, t0=t0, n=n):
                    STT("dve", u_fn(c, t0, n), hT[:, c, t0:t0 + n], vcol(l, gname, c), rstd[:, t0:t0 + n], ALU.mult, ALU.mult,
                        [R_h.r[c][t], R_rstd["r"].r[0][t], R_const], [Ru.r[c][t]])
                steps.append(f)
        return steps

    def hn_a(l, ychunk, src, src_reads, n, dst, dst_writes):
        sq, rsq = t16()
        ACT(sq[:, 0:n], src, AF.Square, src_reads, [rsq])
        return (l, ychunk, src, src_reads, n, dst, dst_writes, sq, rsq)

    def hn_b(st):
        l, ychunk, src, src_reads, n, dst, dst_writes, sq, rsq = st
        pb = nb()
        MM(psum[pb][:, 0:n], bo16, sq[:, 0:n], True, True, [rsq, R_const], [PS[pb]])
        sd, rsd = t32()
        RSQRT(sd[:, 0:n], psum[pb][:, 0:n], 1.0 / 64, epsc, [PS[pb]], [rsd])
        STT("dve", dst, src, vcol(l, "head_g", ychunk), sd[:, 0:n], ALU.mult, ALU.mult, src_reads + [rsd, R_const], dst_writes)

    def mixer(s, l, pre_hook=None, rstd4=None):
        uT = A.bf16(o_u, 8 * L).rearrange("p (c t) -> p c t", c=8)
        R_u = PD.tres(8)
        R_rstd["r"] = PD.tres(1)
        yB = A.bf16(o_yb, 4 * L).rearrange("p (c t) -> p c t", c=4)
        R_yb = PD.tres(4)
        S.ph = "m_prenorm"
        if rstd4 is not None:
            R_rstd["r"].r[0][4] = rstd4
        prenorm(l, "mix_pre_g", range(4), (lambda c, t0, n: uT[:, c, t0:t0 + n]), R_u)
        if pre_hook is not None:
            S.ph = "f_tail"
            pre_hook()
            S.ph = "m_prenorm"
        prenorm(l, "mix_pre_g", [4], (lambda c, t0, n: uT[:, c, t0:t0 + n]), R_u)
        S.ph = "m_conv"
        if STOP == "prenorm":
            tap("u", uT, R_u.all())
            return
        oc = o_qk
        apad = A.bf16(oc, 2 * (L + 30)).rearrange("p (c t) -> p c t", c=2)
        oc += 2 * (L + 30) * 2
        oc = (oc + 63) // 64 * 64
        cv = A.f32(oc, 2 * L).rearrange("p (c t) -> p c t", c=2)
        oc += 2 * L * 4
        a2 = A.bf16(oc, 2 * L).rearrange("p (c t) -> p c t", c=2)
        oc += 2 * L * 2
        cxpad = A.f32(oc, L + 2)
        oc += (L + 2) * 4
        oc = (oc + 63) // 64 * 64
        dg = A.bf16(o_rstd, 31 * 128).rearrange("p (k m) -> p k m", k=31)
        PD.retire(R_rstd["r"].all())
        lnt = [A.f32(oc + i * 416 * 4, 416) for i in range(3)]
        oc += 3 * 416 * 4
        assert oc <= o_yb, (oc, o_yb)
        R_apad = PD.tres(2)
        R_apad0 = PD.new()
        R_cv = PD.tres(2)
        R_a2 = PD.tres(2)
        R_cx = PD.tres(1)
        R_cx0 = PD.new()
        R_dg = PD.new()
        R_lnt = [PD.new() for _ in range(3)]
        MEMSET("dve", apad[:, :, 0:30], 0.0, [R_apad0])
        MEMSET("dve", cxpad[:, 0:2], 0.0, [R_cx0])

        def fm_mm(wv, rw, j, t, pb, halo=0):
            t0, n = TT[t]
            for k in range(8):
                MM(psum[pb][:, 0:n], wv[:, j, k, :], uT[:, k, t0:t0 + n], k == 0, k == 7, [rw, R_u.r[k][t]], [PS[pb]])

        dgs = [dg, dg1]
        R_dg1 = PD.new()
        R_dgs = [R_dg, R_dg1]
        for cc_ in range(2):
            S.dmaop("dg", dgs[cc_].rearrange("p k m -> p (k m)"), dgd[:, (l * 2 + cc_) * 3968:(l * 2 + cc_ + 1) * 3968], reads=[R_dgd], writes=[R_dgs[cc_]], nsem=2)
        for cc in range(2):
            wt, rw = wload(l, "win%d" % cc)
            wv = wt[:, 0:2048].rearrange("p (j k m) -> p j k m", j=2, k=8)
            for t, (t0, n) in enumerate(TT):
                pv, pg = nb(), nb()
                fm_mm(wv, rw, 0, t, pv)
                fm_mm(wv, rw, 1, t, pg)
                sg, rsg = t32()
                ACT(sg[:, 0:n], psum[pg][:, 0:n], AF.Sigmoid, [PS[pg]], [rsg])
                TTOP("dve", apad[:, cc, 30 + t0:30 + t0 + n], psum[pv][:, 0:n], sg[:, 0:n], ALU.mult, [PS[pv], rsg], [R_apad.r[cc][t]])
            if cc == 0:
                pass
            for t, (t0, n) in enumerate(TT):
                pb = nb()
                rd = R_apad.get(cc, t0 - 30, t0 + n) + [R_dgs[cc], R_apad0]
                for k in range(31):
                    MM(psum[pb][:, 0:n], dgs[cc][:, k, :], apad[:, cc, t0 + k:t0 + k + n], k == 0, k == 30, rd, [PS[pb]])
                ACT(cv[:, cc, t0:t0 + n], psum[pb][:, 0:n], AF.Identity, [PS[pb], R_const], [R_cv.r[cc][t]], bias=vcol(l, "dw_b", cc))
        tap("cv", cv, R_cv.all())
        S.ph = "m_ln_pw"
        for t, (t0, n) in enumerate(TT):
            pm, pq = nb(), nb()
            for cc in range(2):
                MM(psum[pm][:, 0:n], ones32, cv[:, cc, t0:t0 + n], cc == 0, cc == 1, [R_cv.r[cc][t], R_const], [PS[pm]])
            for cc in range(2):
                sq, rsq = t32()
                ACT(sq[:, 0:n], cv[:, cc, t0:t0 + n], AF.Square, [R_cv.r[cc][t]], [rsq])
                MM(psum[pq][:, 0:n], ones32, sq[:, 0:n], cc == 0, cc == 1, [rsq, R_const], [PS[pq]])
            mu, m2, rs_ = lnt[0], lnt[1], lnt[2]
            ACT(mu[:, 0:n], psum[pm][:, 0:n], AF.Copy, [PS[pm]], [R_lnt[0]], scale=1.0 / 256)
            TTOP("dve", m2[:, 0:n], mu[:, 0:n], mu[:, 0:n], ALU.mult, [R_lnt[0]], [R_lnt[1]])
            STT("dve", m2[:, 0:n], psum[pq][:, 0:n], 1.0 / 256, m2[:, 0:n], ALU.mult, ALU.subtract, [PS[pq], R_lnt[1]], [R_lnt[1]])
            RSQRT(rs_[:, 0:n], m2[:, 0:n], 1.0, lnepsc, [R_lnt[1]], [R_lnt[2]])
            for cc in range(2):
                xc, rxc = t32()
                TTOP("dve", xc[:, 0:n], cv[:, cc, t0:t0 + n], mu[:, 0:n], ALU.subtract, [R_cv.r[cc][t], R_lnt[0]], [rxc])
                STT("dve", xc[:, 0:n], xc[:, 0:n], vcol(l, "ln_g", cc), rs_[:, 0:n], ALU.mult, ALU.mult, [rxc, R_lnt[2], R_const], [rxc])
                ACT(a2[:, cc, t0:t0 + n], xc[:, 0:n], AF.Silu, [rxc, R_const], [R_a2.r[cc][t]], bias=vcol(l, "ln_b", cc))
        wt, rw = wload(l, "pw")
        wv = wt[:, 0:512].rearrange("p (k m) -> p k m", k=2)
        pend = None
        for oc_ in range(2):
            for t, (t0, n) in enumerate(TT):
                pb = nb()
                for k in range(2):
                    MM(psum[pb][:, 0:n], wv[:, k, oc_ * 128:(oc_ + 1) * 128], a2[:, k, t0:t0 + n], k == 0, k == 1,
                       [rw, R_a2.r[k][t]], [PS[pb]])
                st = hn_a(l, 4 + oc_, psum[pb][:, 0:n], [PS[pb]], n, yB[:, oc_, t0:t0 + n], [R_yb.r[oc_][t]])
                if pend is not None:
                    hn_b(pend)
                pend = st
        hn_b(pend)
        S.ph = "m_sc"
        w2, rw2 = wload(l, "win2")
        w3, rw3 = wload(l, "win3")
        w4, rw4 = wload(l, "win4")
        v2 = w2[:, 0:2048].rearrange("p (j k m) -> p j k m", j=2, k=8)
        v3 = w3[:, 0:2048].rearrange("p (j k m) -> p j k m", j=2, k=8)
        v4 = w4[:, 0:2048].rearrange("p (j k m) -> p j k m", j=2, k=8)
        scw = [[(v2, rw2, 0), (v2, rw2, 1), (v3, rw3, 0)], [(v3, rw3, 1), (v4, rw4, 0), (v4, rw4, 1)]]
        pend = None
        for cc in range(2):
            for t, (t0, n) in enumerate(TT):
                pc, px, pbb = nb(), nb(), nb()
                fm_mm(scw[cc][0][0], scw[cc][0][1], scw[cc][0][2], t, pc)
                fm_mm(scw[cc][1][0], scw[cc][1][1], scw[cc][1][2], t, px)
                fm_mm(scw[cc][2][0], scw[cc][2][1], scw[cc][2][2], t, pbb)
                xs, rxs = t32()
                CP("act", xs[:, 0:n], psum[px][:, 0:n], [PS[px]], [rxs])
                TTOP("dve", cxpad[:, 2 + t0:2 + t0 + n], psum[pc][:, 0:n], xs[:, 0:n], ALU.mult, [PS[pc], rxs], [R_cx.r[0][t]])
                acc, racc = t32()
                rd = R_cx.get(0, t0 - 2, t0 + n) + [R_cx0, R_const]
                TS("dve", acc[:, 0:n], cxpad[:, t0:t0 + n], vcol(l, "sc_w", 0 * 2 + cc), ALU.mult, rd, [racc])
                STT("dve", acc[:, 0:n], cxpad[:, t0 + 1:t0 + 1 + n], vcol(l, "sc_w", 1 * 2 + cc), acc[:, 0:n], ALU.mult, ALU.add, rd + [racc], [racc])
                STT("dve", acc[:, 0:n], cxpad[:, t0 + 2:t0 + 2 + n], vcol(l, "sc_w", 2 * 2 + cc), acc[:, 0:n], ALU.mult, ALU.add, rd + [racc], [racc])
                TTOP("dve", acc[:, 0:n], psum[pbb][:, 0:n], acc[:, 0:n], ALU.mult, [PS[pbb], racc], [racc])
                st = hn_a(l, 6 + cc, acc[:, 0:n], [racc], n, yB[:, 2 + cc, t0:t0 + n], [R_yb.r[2 + cc][t]])
                if pend is not None:
                    hn_b(pend)
                pend = st
        hn_b(pend)
        tap("yb", yB, R_yb.all())
        if STOP == "conv":
            return
        S.ph = "m_qkv"
        PD.retire(R_apad.all() + [R_apad0, R_cx0, R_dg, R_dg1] + R_cv.all() + R_a2.all() + R_cx.all() + R_lnt)
        for par in range(2):
            R_qz[par] = PD.new()
            MEMSET("dve", qz[par], 0.0, [R_qz[par]])
        qk = A.bf16(o_qk, 8 * L).rearrange("p (c t) -> p c t", c=8)
        R_qk = PD.tres(8)
        vaug = A.bf16(o_v, 17 * 520).rearrange("p (b h d) -> p b h d", b=17, h=8)
        R_v = [PD.new() for _ in range(17)]
        R_vones = PD.new()
        MEMSET("dve", vaug[:, :, :, 64:65], 1.0, [R_vones])
        for i in range(4):
            wt, rw = wload(l, "win%d" % (5 + i))
            wv = wt[:, 0:2048].rearrange("p (j k m) -> p j k m", j=2, k=8)
            for j in range(2):
                ch = 2 * i + j
                for t, (t0, n) in enumerate(TT):
                    pb = nb()
                    fm_mm(wv, rw, j, t, pb)
                    CP("act", qk[:, ch, t0:t0 + n], psum[pb][:, 0:n], [PS[pb]], [R_qk.r[ch][t]])
        wa, rwa = wload(l, "vfa")
        wbb, rwb = wload(l, "vfb")
        va = wa[:, 0:2048].rearrange("p (k m) -> p k m", k=8)
        vb = wbb[:, 0:2112].rearrange("p (k m) -> p k m", k=8)
        R_fl = PD.new()
        og = o_rstd
        fl = A.f32(og, 136).rearrange("p (b h) -> p b h", b=17)
        og += 136 * 4
        lf = A.f32(og, 136).rearrange("p (b h) -> p b h", b=17)
        og += 136 * 4
        Cx = A.f32(og, 18 * 8).rearrange("p (b h) -> p b h", b=18)
        og += 18 * 8 * 4
        Gt = A.f32(og, 136).rearrange("p (b h) -> p b h", b=17)
        og += 136 * 4
        biasv = A.f32(og, 5 * 136).rearrange("p (i b h) -> p i b h", i=5, b=17)
        og += 5 * 136 * 4
        assert og <= o_rstd + 8320
        MEMSET("dve", fl, 0.0, [R_fl])
        for b, (b0, nt) in enumerate(BLK):
            pa, pbk = nb(), nb()
            rdu = []
            for k in range(8):
                rdu += R_u.get(k, b0, b0 + nt)
            for k in range(8):
                MM(psum[pa][0:nt, 0:256], uT[:, k, b0:b0 + nt], va[:, k, :], k == 0, k == 7, [rwa] + rdu, [PS[pa]])
            for k in range(8):
                MM(psum[pbk][0:nt, 0:264], uT[:, k, b0:b0 + nt], vb[:, k, :], k == 0, k == 7, [rwb] + rdu, [PS[pbk]])
            CP("act", vaug[0:nt, b, 0:4, 0:64], psum[pa][0:nt, 0:256].rearrange("p (h d) -> p h d", h=4), [PS[pa]], [R_v[b]])
            CP("dve", vaug[0:nt, b, 4:8, 0:64], psum[pbk][0:nt, 0:256].rearrange("p (h d) -> p h d", h=4), [PS[pbk]], [R_v[b]])
            TTOP("dve", fl[0:nt, b, :], psum[pbk][0:nt, 256:264], bfg[0:nt, l * 8:l * 8 + 8], ALU.add, [PS[pbk], R_const], [R_fl])
        flf = fl.rearrange("p b h -> p (b h)")
        lff = lf.rearrange("p b h -> p (b h)")
        ACT(lff, flf, AF.Exp, [R_fl], [R_fl], scale=-1.0)
        ACT(lff, lff, AF.Ln, [R_fl, R_const], [R_fl], bias=onec, scale=1.0)
        pc_ = nb()
        MM(psum[pc_][:, 0:136], tri32, lff, True, True, [R_fl, R_const], [PS[pc_]])
        MM(psum[pc_][:, 136:272], ones32, lff, True, True, [R_fl, R_const], [PS[pc_]])
        MEMSET("dve", Cx[:, 0, :], 0.0, [R_fl])
        for b in range(16):
            TTOP("dve", Cx[:, b + 1, :], Cx[:, b, :], psum[pc_][:, 136 + 8 * b:136 + 8 * b + 8], ALU.add, [PS[pc_], R_fl], [R_fl])
        TTOP("dve", Gt.rearrange("p b h -> p (b h)"), psum[pc_][:, 0:136], Cx[:, 0:17, :].rearrange("p b h -> p (b h)"), ALU.add,
             [PS[pc_], R_fl], [R_fl])
        for i in range(5):
            nbk = 4 * i + 4 if i < 4 else 17
            e_i = 4 * i + 4 if i < 4 else 16
            TTOP("dve", biasv[:, i, 0:nbk, :], Gt[:, 0:nbk, :], Cx[:, e_i:e_i + 1, :].broadcast_to([128, nbk, 8]), ALU.subtract, [R_fl], [R_fl])
        tap("gt", Gt.rearrange("p b h -> p (b h)"), [R_fl])
        if STOP == "qkv":
            tap("qk", qk, R_qk.all())
            tap("v", vaug.rearrange("p b h d -> p (b h d)"), R_v)
            return
        S.ph = "m_attn"
        PD.retire(R_u.all())
        oa = o_u
        yA = A.bf16(oa, 4 * L).rearrange("p (c t) -> p c t", c=4)
        oa += 4 * L * 2
        NPT = 4
        pT = [A.bf16(oa + i * 1024, 512) for i in range(NPT)]
        oa += NPT * 1024
        ost = A.f32(oa, 4 * 8 * 65).rearrange("p (r h d) -> p r h d", r=4, h=8)
        oa += 4 * 8 * 65 * 4
        ynt = [A.bf16(oa + i * 1024, 512) for i in range(4)]
        oa += 4096
        SSb, RDb, TTb, RRb, SCb = [A.f32(og + 128 * r_, 32).rearrange("p (r h) -> p r h", r=4) for r_ in range(5)]
        og += 640
        sqt = A.f32(og, 256).rearrange("p (r d) -> p r d", r=4)
        og += 1024
        assert og <= o_rstd + 8320
        assert oa <= o_qk, (oa, o_qk)
        R_yA = PD.tres(4)
        R_pT = [PD.new() for _ in range(NPT)]
        R_ost = PD.new()
        R_sqt = PD.new()
        R_ynt = [PD.new() for _ in range(4)]
        R_sm = PD.new()
        SB, OB, TB = (0, 1, 2, 6), (3, 4), 5
        NSB = len(SB)
        steps = []
        for i in range(5):
            q0, qn = (512 * i, 512) if i < 4 else (2048, 16)
            jmax = 4 * i + 3 if i < 4 else 16
            for h in range(8):
                for j in range(jmax + 1):
                    steps.append((i, h, j, q0, qn, jmax))
        LA = 3
        nsteps = len(steps)
        ynrot = dict(i=0)
        kzrot = [0, 0]

        def stage_q(i, h):
            q0, qn = (512 * i, 512) if i < 4 else (2048, 16)
            kc, po = h // 2, (h % 2) * 64
            CP("dve", qz[h % 2][po:po + 64, 0:qn], qk[po:po + 64, kc, q0:q0 + qn], R_qk.get(kc, q0, q0 + qn), [R_qz[h % 2]])

        def emit_S(n):
            i, h, j, q0, qn, jmax = steps[n]
            kc, po = h // 2, (h % 2) * 64
            b0, kb = BLK[j]
            lo = max(q0, b0)
            N = q0 + qn - lo
            diag = (b0 >= q0)
            sb = SB[n % NSB]
            rq = R_qk.get(kc, lo, q0 + qn)
            rk = R_qk.get(4 + kc, b0, b0 + kb)
            par = h % 2
            if j == 0:
                if (i, h) == (0, 0):
                    stage_q(0, 0)
                nxt = (i, h + 1) if h < 7 else ((i + 1, 0) if i < 4 else None)
                if nxt is not None:
                    stage_q(*nxt)
            MM(psum[sb][0:kb, 0:N], qk[:, 4 + kc, b0:b0 + kb], qz[par][:, lo - q0:lo - q0 + N], True, not diag, rk + [R_qz[par]], [PS[sb]])
            if diag:
                MM(psum[sb][0:kb, 0:kb], ident16[0:kb, 0:kb], negtri16[0:kb, 0:kb], False, True, [R_const], [PS[sb]])
            ACT(pT[n % NPT][0:kb, 0:N], psum[sb][0:kb, 0:N], AF.Exp, [PS[sb], R_fl], [R_pT[n % NPT]], bias=biasv[0:kb, i, j, h:h + 1], scale=0.125)

        def emit_PV(n):
            i, h, j, q0, qn, jmax = steps[n]
            b0, kb = BLK[j]
            lo = max(q0, b0)
            ob = OB[h % 2]
            if i < 4:
                for r in range(4):
                    q = 4 * i + r
                    if q < j:
                        continue
                    off = 128 * q - lo
                    MM(psum[ob][:, r * 65:(r + 1) * 65], pT[n % NPT][0:kb, off:off + 128], vaug[0:kb, j, h, :], (j == 0 and r == 0), j == q,
                       [R_pT[n % NPT], R_v[j], R_vones], [PS[ob]], skip_group_check=True)
            else:
                MM(psum[ob][0:16, 0:65], pT[n % NPT][0:kb, 0:16], vaug[0:kb, j, h, :], j == 0, j == 16, [R_pT[n % NPT], R_v[j], R_vones], [PS[ob]])
            if j == jmax:
                if i < 4:
                    CP("dve", ost[:, :, h, :], psum[ob][:, 0:260].rearrange("p (r d) -> p r d", r=4), [PS[ob]], [R_ost])
                    nt_, R_ = 128, 4
                else:
                    CP("dve", ost[0:16, 0, h, :], psum[ob][0:16, 0:65], [PS[ob]], [R_ost])
                    nt_, R_ = 16, 1
                Oh = ost[0:nt_, 0:R_, h, 0:64]
                TTOP("dve", sqt[0:nt_, 0:R_], Oh, Oh, ALU.mult, [R_ost], [R_sqt])
                S.op("dve", "tensor_reduce", [R_sqt], [R_sm], out=SSb[0:nt_, 0:R_, h], in_=sqt[0:nt_, 0:R_], axis=AX.X, op=ALU.add)
                if h == 7:
                    finish_tile(i)

        DQ = []
        tk = [0]

        def defer(delay, fn):
            DQ.append([tk[0] + delay, fn])

        def tick():
            tk[0] += 1
            for x in [x for x in DQ if x[0] <= tk[0]]:
                DQ.remove(x)
                x[1]()

        def flushq():
            while DQ:
                DQ.pop(0)[1]()

        PS_T = [Res(), Res()]
        PS_T[0].rs = dict(PS[TB].rs)
        PS_T[0].ws = dict(PS[TB].ws)
        PS_T[1].rs = dict(PS[TB].rs)
        PS_T[1].ws = dict(PS[TB].ws)
        pst = psum[TB][:, 0:512].bitcast(BF16)

        def finish_tile(i):
            blocks = [4 * i + r for r in range(4)] if i < 4 else [16]

            nt_, R_ = (128, 4) if i < 4 else (16, 1)
            SS, RD, T_, RR, SC = [b[0:nt_, 0:R_] for b in (SSb, RDb, TTb, RRb, SCb)]

            def stA():
                RECIP(RD, ost[0:nt_, 0:R_, :, 64], [R_ost], [R_sm])
                TTOP("dve", T_, SS, RD, ALU.mult, [R_sm], [R_sm])
                TTOP("dve", T_, T_, RD, ALU.mult, [R_sm], [R_sm])
                TS("dve", T_, T_, 1.0 / 64, ALU.mult, [R_sm], [R_sm], s2=RMS_EPS, op1=ALU.add)

            def stB():
                tm, rtm = tmpr[rot["tr"] % 2], R_tmpr[rot["tr"] % 2]
                rot["tr"] += 1
                tv = tm[0:nt_, 0:R_ * 8].rearrange("p (r h) -> p r h", r=R_)
                ACT(tv, T_, AF.Ln, [R_sm], [rtm])
                ACT(RR, tv, AF.Exp, [rtm], [R_sm], scale=-0.5)

            def stC():
                TTOP("dve", SC, RR, RD, ALU.mult, [R_sm], [R_sm])
                for r, q in enumerate(blocks):
                    b0, nt = BLK[q]
                    TTOP("dve", ynt[r][0:nt].rearrange("p (h d) -> p h d", h=8), ost[0:nt, r, :, 0:64], SCb[0:nt, r, :].unsqueeze(2).broadcast_to([nt, 8, 64]),
                         ALU.mult, [R_ost, R_sm], [R_ynt[r]])

            def stD(r, q):
                b0, nt = BLK[q]
                hb = r % 2
                for c in range(4):
                    S.op("pe", "transpose", [R_ynt[r], R_const], [PS_T[hb]], out=pst[:, hb * 512 + c * 128:hb * 512 + c * 128 + nt],
                         in_=ynt[r][0:nt, c * 128:(c + 1) * 128], identity=ident16[0:nt, 0:nt])

            def stE(r, q):
                b0, nt = BLK[q]
                hb = r % 2
                for c in range(4):
                    ACT(yA[:, c, b0:b0 + nt], pst[:, hb * 512 + c * 128:hb * 512 + c * 128 + nt], AF.Copy, [PS_T[hb], R_const], R_yA.get(c, b0, b0 + nt),
                        scale=vcol(l, "head_g", c))

            stA()
            defer(3, stB)
            defer(6, stC)
            for r, q in enumerate(blocks):
                defer(13 + 2 * r, (lambda r=r, q=q: stD(r, q)))
                defer(16 + 2 * r, (lambda r=r, q=q: stE(r, q)))

        for n in range(nsteps + LA):
            if n < nsteps:
                emit_S(n)
            if n - LA >= 0:
                emit_PV(n - LA)
            tick()
        flushq()
        for hb in range(2):
            for dct in (PS_T[hb].ws, PS_T[hb].rs):
                for k_, rec_ in dct.items():
                    o_ = PS[TB].rs.get(k_)
                    if o_ is None or o_.idx < rec_.idx:
                        PS[TB].rs[k_] = rec_
        tap("ya", yA, R_yA.all())
        if STOP == "attn":
            return
        S.ph = "m_wout"
        PD.retire(R_qk.all() + R_pT + [R_ost, R_sqt, R_sm, R_fl] + R_ynt)
        R_rstd["r"] = PD.tres(1)
        ysrc = [(yA[:, k], R_yA.r[k]) for k in range(4)] + [(yB[:, k], R_yb.r[k]) for k in range(4)]
        pend = None
        wouts = [wload(l, "wout%d" % i) for i in range(4)]
        for t in range(5):
            st = proj_mm(l, None, 2, ysrc, [t], o_qk + (t % 2) * 4 * L * 2, 0, ((6, 7)[t % 2],), "mix_post_g", preloaded=wouts, bgk=2)
            bgflush()
            BG.extend(proj_tail_steps(l, "mix_post_g", st))
        bgflush()
        PD.retire(R_yA.all() + R_yb.all() + R_v + [R_vones] + R_rstd["r"].all() + R_qz)

    BG = []

    def bgpop(k):
        for _ in range(k):
            if BG:
                BG.pop(0)()

    def bgflush():
        while BG:
            BG.pop(0)()

    def proj_mm(l, pieces, cpp, src, tiles, o_ytmp, src_off, statbanks, gname, preloaded=None, bgk=0):
        K = len(src)
        t_lo = TT[tiles[0]][0]
        YW = 416 * len(tiles)
        ytmp = A.f32(o_ytmp, 8 * YW).rearrange("p (c t) -> p c t", c=8)
        R_yt = [[PD.new() for _ in tiles] for _ in range(8)]
        stat = {t: statbanks[i] for i, t in enumerate(tiles)}
        pend_stat = []
        for c in range(8):
            if c % cpp == 0:
                wt, rw = preloaded[c // cpp] if preloaded is not None else wload(l, pieces[c // cpp])
                if cpp == 2:
                    wv = wt[:, 0:2048].rearrange("p (j k m) -> p j k m", j=2, k=8)
                else:
                    wv = wt[:, 0:K * 128].rearrange("p (j k m) -> p j k m", j=1, k=K)
            j = c % cpp
            for ti, t in enumerate(tiles):
                t0, n = TT[t]
                pb = nb((0, 1, 2, 3, 4, 5))
                for k in range(K):
                    sap, sres = src[k]
                    MM(psum[pb][:, 0:n], wv[:, j, k, :], sap[:, t0 - src_off:t0 - src_off + n], k == 0, k == K - 1, [rw, sres[t]], [PS[pb]])
                sq, rsq = t16()
                ACT(sq[:, 0:n], psum[pb][:, 0:n], AF.Square, [PS[pb]], [rsq])
                ACT(ytmp[:, c, t0 - t_lo:t0 - t_lo + n], psum[pb][:, 0:n], AF.Copy, [PS[pb], R_const], [R_yt[c][ti]], scale=vcol(l, gname, c))
                pend_stat.append((psum[stat[t]][:, 0:n], sq[:, 0:n], c == 0, c == 7, rsq, PS[stat[t]]))
                if len(pend_stat) > 2:
                    o_, s_, a_, b_, r_, p_ = pend_stat.pop(0)
                    MM(o_, ones16, s_, a_, b_, [r_, R_const], [p_])
                bgpop(bgk)
        for o_, s_, a_, b_, r_, p_ in pend_stat:
            MM(o_, ones16, s_, a_, b_, [r_, R_const], [p_])
        return (tiles, t_lo, ytmp, R_yt, stat)

    def proj_tail_steps(l, gname, st):
        tiles, t_lo, ytmp, R_yt, stat = st
        steps = []
        for ti, t in enumerate(tiles):
            t0, n = TT[t]

            def s0(t=t, t0=t0, n=n):
                RSQRT(rstd[:, t0:t0 + n], psum[stat[t]][:, 0:n], 1.0 / DM, epsc, [PS[stat[t]]], [R_rstd["r"].r[0][t]])
            steps.append(s0)
        for ti, t in enumerate(tiles):
            t0, n = TT[t]
            for c in range(8):
                def s1(c=c, ti=ti, t=t, t0=t0, n=n):
                    yv = ytmp[:, c, t0 - t_lo:t0 - t_lo + n]
                    TTOP("dve", yv, yv, rstd[:, t0:t0 + n], ALU.mult, [R_yt[c][ti], R_rstd["r"].r[0][t]], [R_yt[c][ti]])
                    TTOP("dve", hT[:, c, t0:t0 + n], hT[:, c, t0:t0 + n], yv, ALU.add, [R_yt[c][ti], R_h.r[c][t]], [R_h.r[c][t]])
                steps.append(s1)
        steps.append(lambda: PD.retire([x for row in R_yt for x in row]))
        return steps

    def proj_tail(l, gname, st):
        for f in proj_tail_steps(l, gname, st):
            f()

    def ffn(s, l):
        last_tail = []
        R_rstd["r"] = PD.tres(1)
        o = o_P
        UW = 2 + 828
        up = A.bf16(o, 8 * UW).rearrange("p (c t) -> p c t", c=8)
        o += 8 * UW * 2
        o = (o + 63) // 64 * 64
        actT = A.bf16(o, 22 * 828).rearrange("p (c t) -> p c t", c=22)
        o += 22 * 828 * 2
        o = (o + 63) // 64 * 64
        NFA = 4
        facc = [A.f32(o + i * 416 * 4, 416) for i in range(NFA)]
        o += NFA * 416 * 4
        fsg = [A.bf16(o + i * 416 * 2, 416) for i in range(2)]
        o += 2 * 416 * 2
        o = (o + 63) // 64 * 64
        o_yt = o
        assert o_yt + 2 * 8 * 832 * 4 <= ARENA_BYTES, (o_yt, ARENA_BYTES)
        R_facc = [PD.new() for _ in range(NFA)]
        R_fsg = [PD.new() for _ in range(2)]
        R_halo = PD.new()
        frot = dict(a=0, s=0)

        class _RU:
            pass

        def new_up_res(tiles):
            R_up = [[PD.new() for _ in tiles] for _ in range(8)]
            ru = _RU()
            ru.r = [{t: R_up[c][ti] for ti, t in enumerate(tiles)} for c in range(8)]
            return R_up, ru

        def new_act_res(tiles):
            return [[PD.new() for _ in tiles] for _ in range(22)]

        def apply_steps(tiles, ru):
            t_lo = TT[tiles[0]][0]
            return prenorm_apply_steps(l, "ffn_pre_g", tiles, (lambda c, t0, n: up[:, c, 2 + t0 - t_lo:2 + t0 - t_lo + n]), ru)

        S.ph = "f_prenorm"
        prenorm_stats(range(5))
        R_up, ru = new_up_res(FFN_PASSES[0])
        R_act = new_act_res(FFN_PASSES[0])
        for f in apply_steps(FFN_PASSES[0], ru):
            f()
        for pi, tiles in enumerate(FFN_PASSES):
            t_lo = TT[tiles[0]][0]
            S.ph = "f_up"
            for p in range(22):
                wt, rw = wload(l, "wup%d" % p)
                wv = wt[:, 0:2048].rearrange("p (j k m) -> p j k m", j=2, k=8)
                for ti, t in enumerate(tiles):
                    t0, n = TT[t]
                    hal = 0 if t == 0 else 2
                    c0 = 2 + t0 - t_lo - hal
                    rdu = []
                    for k in range(8):
                        rdu.append(R_up[k][ti])
                        if hal and ti > 0:
                            rdu.append(R_up[k][ti - 1])
                    if hal and ti == 0:
                        rdu.append(R_halo)
                    res = []
                    for half in range(2):
                        pb = nb((0, 1, 2, 3, 4, 5))
                        for k in range(8):
                            MM(psum[pb][:, 0:n + hal], wv[:, half, k, :], up[:, k, c0:c0 + n + hal], k == 0, k == 7, [rw] + rdu, [PS[pb]])
                        ai = frot["a"] % NFA
                        frot["a"] += 1
                        acc, racc = facc[ai], R_facc[ai]
                        f = half * 22 + p
                        ACT(acc[:, 0:n], psum[pb][:, hal:hal + n], AF.Copy, [PS[pb], R_const], [racc], scale=vcol(l, "ffn_cw", 2 * 44 + f))
                        if hal:
                            STT("dve", acc[:, 0:n], psum[pb][:, 1:1 + n], vcol(l, "ffn_cw", 1 * 44 + f), acc[:, 0:n], ALU.mult, ALU.add, [PS[pb], racc, R_const], [racc])
                            STT("dve", acc[:, 0:n], psum[pb][:, 0:n], vcol(l, "ffn_cw", 0 * 44 + f), acc[:, 0:n], ALU.mult, ALU.add, [PS[pb], racc, R_const], [racc])
                        else:
                            STT("dve", acc[:, 1:n], psum[pb][:, 0:n - 1], vcol(l, "ffn_cw", 1 * 44 + f), acc[:, 1:n], ALU.mult, ALU.add, [PS[pb], racc, R_const], [racc])
                            STT("dve", acc[:, 2:n], psum[pb][:, 0:n - 2], vcol(l, "ffn_cw", 0 * 44 + f), acc[:, 2:n], ALU.mult, ALU.add, [PS[pb], racc, R_const], [racc])
                        res.append((acc, racc))
                    si = frot["s"] % 2
                    frot["s"] += 1
                    ACT(fsg[si][:, 0:n], res[0][0][:, 0:n], AF.Silu, [res[0][1]], [R_fsg[si]])
                    TTOP("dve", actT[:, p, t0 - t_lo:t0 - t_lo + n], fsg[si][:, 0:n], res[1][0][:, 0:n], ALU.mult, [R_fsg[si], res[1][1]], [R_act[p][ti]])
            if pi + 1 < len(FFN_PASSES):
                width = sum(TT[t][1] for t in tiles)
                rd_all = [R_up[c][len(tiles) - 1] for c in range(8)]
                CP("dve", up[:, :, 0:2], up[:, :, width:width + 2], rd_all, [R_halo])
            S.ph = "f_down"
            src = [(actT[:, k], {t: R_act[k][ti] for ti, t in enumerate(tiles)}) for k in range(22)]
            PD.retire([x for row in R_up for x in row])
            if pi + 1 < len(FFN_PASSES):
                R_up, ru = new_up_res(FFN_PASSES[pi + 1])
                nfront = len(FFN_PASSES[pi - 1]) if (pi > 0 and BG) else 0
                BG[nfront:nfront] = apply_steps(FFN_PASSES[pi + 1], ru)
            st = proj_mm(l, ["wdn%d" % i for i in range(8)], 1, src, tiles, o_yt + (pi % 2) * 8 * 832 * 4, t_lo, (6, 7), "ffn_post_g", bgk=3)
            bgflush()
            PD.retire([x for row in R_act for x in row])
            if pi + 1 < len(FFN_PASSES):
                R_act = new_act_res(FFN_PASSES[pi + 1])
            S.ph = "f_tail"
            if pi + 1 < len(FFN_PASSES):
                BG.extend(proj_tail_steps(l, "ffn_post_g", st))
            else:
                last_tail.extend(proj_tail_steps(l, "ffn_post_g", st))
        PD.retire(R_facc + R_fsg + [R_halo] + [x for t_, x in enumerate(R_rstd["r"].r[0]) if t_ != 4])
        return last_tail, R_rstd["r"].r[0][4]

    hook = None
    rstd4 = None
    for s in range(NSEQ):
        for l in range(NLAYER):
            mixer(s, l, hook, rstd4)
            hook, rstd4 = None, None
            if STOP in ("prenorm", "conv", "qkv", "attn", "mixer"):
                break
            tail_steps, rstd4 = ffn(s, l)
            last = (l == NLAYER - 1)
            if last:
                tap("h", hT, R_h.all()) if False else None
                store_seq(s, range(4))
                if s + 1 < NSEQ:
                    load_tiles(s + 1, range(4))

            def hook(tail_steps=tail_steps, last=last, s=s):
                for f in tail_steps:
                    f()
                if last:
                    store_seq(s, [4])
                    if s + 1 < NSEQ:
                        load_tiles(s + 1, [4])
        if STOP is not None:
            tap("h", hT, R_h.all())
            store_seq(s, range(5))
            hook = None
    if hook is not None:
        hook()
        if "h" in taps:
            tap("h", hT, R_h.all())

    S.finalize()
    esems = {}
    dsems = {}
    for e in Sched.COMPUTE:
        esems[e] = es.enter_context(nc.semaphore("s_" + e))
    for stname, st in S.streams.items():
        for i in range(st["nsem"]):
            dsems[(stname, i)] = es.enter_context(nc.semaphore("d_%s%d" % (stname, i)))
    block = es.enter_context(nc.Block())

    @block.tensor
    def _(eng):
        S.emit("pe", eng, esems, dsems)

    @block.scalar
    def _(eng):
        S.emit("act", eng, esems, dsems)

    @block.vector
    def _(eng):
        S.emit("dve", eng, esems, dsems)

    @block.gpsimd
    def _(eng):
        S.emit("pool", eng, esems, dsems)

    @block.sync
    def _(eng):
        S.emit("sp", eng, esems, dsems)
        for stname, st in S.streams.items():
            n = len(st["recs"])
            for i in range(st["nsem"]):
                k = len([1 for j in range(n) if j % st["nsem"] == i])
                if k:
                    eng.wait_ge(dsems[(stname, i)], 16 * k)
    es.close()
    stats = {e: len(v) for e, v in S.ins.items()}
    global _LAST_SCHED
    _LAST_SCHED = S
    return nc, stats


def make_in_maps(inputs, cfg):
    NSEQ = cfg["nseq"]
    x = np.asarray(inputs["x"], np.float32)
    f32 = lambda a: np.asarray(a, np.float32)
    wfull = host_weights(f32(inputs["w_in"]), f32(inputs["a_pw_w"]), f32(inputs["w_out"]),
                         f32(inputs["ffn_w_up"]), f32(inputs["ffn_w_down"]))
    vecs = host_vecs({k: f32(v) for k, v in inputs.items()})
    bfg = np.ascontiguousarray(np.broadcast_to(f32(inputs["b_forget"]).reshape(1, 16), (128, 16)))
    metaT = np.ascontiguousarray(f32(inputs["meta_tokens"]).reshape(NMETA, 8, 128).transpose(2, 1, 0))
    consts = host_consts()
    maps = []
    ncores = cfg.get("ncores", NCORES)
    for c in range(ncores):
        xs = x[c * NSEQ:(c + 1) * NSEQ]
        xTn = np.ascontiguousarray(xs.reshape(NSEQ, SEQ, 8, 128).transpose(0, 3, 2, 1))
        maps.append(dict(xT=xTn, metaT=metaT, wf=wfull, vecs=vecs, bfg=bfg, consts=consts))
    return maps


def kernel(**inputs):
    cfg = dict(CFG)
    nc, _ = build(cfg)
    maps = make_in_maps(inputs, cfg)
    res = run_bass_kernel_spmd(nc, maps, core_ids=list(range(NCORES)))
    outs = []
    for c in range(NCORES):
        oT = res.results[c]["outT"]
        outs.append(np.ascontiguousarray(oT.transpose(0, 3, 2, 1)).reshape(cfg["nseq"], SEQ, DM))
    return np.concatenate(outs, axis=0).astype(np.float32)
```
